# Optimizing a Trainium2 kernel written in Bass

```python
import math
import jax, jax.numpy as jnp
from jax import lax
import numpy as np


D_MODEL = 1024
BATCH = 2
SEQ = 8192
DEPTH = 2
DEC_BATCH = 2
DEC_SEQ = 16384
PAST_LEN = 128

GRID_W = 64
N_EVEN = (DEPTH + 1) // 2
N_ODD = DEPTH // 2

MLA_HEADS = 8
MLA_Q_RANK = 256
MLA_KV_RANK = 128
MLA_NOPE = 64
MLA_ROPE = 32
MLA_V = 64
ROPE_THETA = 10000.0
Q_BLOCK = 128

NAT_HEADS = 8
NAT_HEAD_DIM = 64
NAT_W = NAT_HEADS * NAT_HEAD_DIM
NAT_KH_MAX = 8
NAT_KW = 16

EV_IN = MLA_Q_RANK + MLA_KV_RANK + MLA_ROPE + 3 * NAT_W
EV_MIX = MLA_HEADS * MLA_V + NAT_W

CONV_CH = 512
CONV_WIDTH = 31

S5_CH = 512
S5_GROUP = 16
S5_GROUPS = S5_CH // S5_GROUP
S5_STATE = 64

OD_IN = 2 * CONV_CH + S5_CH
OD_MIX = CONV_CH + S5_CH

D_FF = 2816
FFN_RES_SCALE = 0.5
NORM_EPS = 1e-6
NEG_INF = -1e30

kernel_name = 'hybrid_mla_natten_conformer_s5_encoder'


def rms_norm(x, g):
    xf = x.astype(jnp.float32)
    y = xf * lax.rsqrt(jnp.mean(xf * xf, axis=-1, keepdims=True) + NORM_EPS)
    return (y * g.astype(jnp.float32)).astype(x.dtype)


def layer_norm(x, g, b):
    xf = x.astype(jnp.float32)
    mu = jnp.mean(xf, axis=-1, keepdims=True)
    xc = xf - mu
    y = xc * lax.rsqrt(jnp.mean(xc * xc, axis=-1, keepdims=True) + NORM_EPS)
    return (y * g.astype(jnp.float32) + b.astype(jnp.float32)).astype(x.dtype)


def swiglu(x, wg, wu, wd):
    return (jax.nn.silu(x @ wg) * (x @ wu)) @ wd


def rotary(x):
    L = x.shape[1]
    half = x.shape[-1] // 2
    inv = ROPE_THETA ** (-jnp.arange(half, dtype=jnp.float32) / half)
    ang = jnp.arange(L, dtype=jnp.float32)[:, None] * inv[None, :]
    cos = jnp.cos(ang)[None, :, None, :]
    sin = jnp.sin(ang)[None, :, None, :]
    xf = x.astype(jnp.float32)
    x1, x2 = xf[..., :half], xf[..., half:]
    return jnp.concatenate([x1 * cos - x2 * sin, x2 * cos + x1 * sin], axis=-1).astype(x.dtype)


def mla(q_lat, kv_lat, k_rope, q_norm, kv_norm, w_uq, w_ukv):
    B, L, _ = q_lat.shape
    q = (rms_norm(q_lat, q_norm) @ w_uq).reshape(B, L, MLA_HEADS, MLA_NOPE + MLA_ROPE)
    kv = (rms_norm(kv_lat, kv_norm) @ w_ukv).reshape(B, L, MLA_HEADS, MLA_NOPE + MLA_V)
    q = jnp.concatenate([q[..., :MLA_NOPE], rotary(q[..., MLA_NOPE:])], axis=-1)
    k_r = jnp.broadcast_to(rotary(k_rope[:, :, None, :]), (B, L, MLA_HEADS, MLA_ROPE))
    k = jnp.concatenate([kv[..., :MLA_NOPE], k_r], axis=-1)
    v = kv[..., MLA_NOPE:]
    scale = (MLA_NOPE + MLA_ROPE) ** -0.5
    qb = q.reshape(B, L // Q_BLOCK, Q_BLOCK, MLA_HEADS, MLA_NOPE + MLA_ROPE).transpose(1, 0, 2, 3, 4)

    def block(qi):
        s = jnp.einsum('bqhd,bkhd->bhqk', qi, k, preferred_element_type=jnp.float32) * scale
        p = jax.nn.softmax(s, axis=-1).astype(v.dtype)
        return jnp.einsum('bhqk,bkhd->bqhd', p, v)

    o = lax.map(block, qb)
    return o.transpose(1, 0, 2, 3, 4).reshape(B, L, MLA_HEADS * MLA_V)


def neighborhood_attention(q, k, v, rpb):
    B, L, _ = q.shape
    rows = L // GRID_W
    kh = min(NAT_KH_MAX, rows)
    kw = NAT_KW
    shp = (B, rows, GRID_W, NAT_HEADS, NAT_HEAD_DIM)
    q, k, v = q.reshape(shp), k.reshape(shp), v.reshape(shp)
    r = jnp.arange(rows)
    row_idx = jnp.clip(r - kh // 2, 0, rows - kh)[:, None] + jnp.arange(kh)[None, :]
    k_rows = k[:, row_idx]
    v_rows = v[:, row_idx]
    c = jnp.arange(GRID_W)
    col_start = jnp.clip(c - kw // 2, 0, GRID_W - kw)
    col_ok = (c[None, :] >= col_start[:, None]) & (c[None, :] < col_start[:, None] + kw)
    row_off = row_idx - r[:, None] + (NAT_KH_MAX - 1)
    col_off = jnp.clip(c[None, :] - c[:, None], -(kw - 1), kw - 1) + (kw - 1)
    bias = rpb[:, row_off[:, None, :, None], col_off[None, :, None, :]]
    bias = bias.astype(jnp.float32).transpose(1, 0, 2, 3, 4)
    bias = jnp.where(col_ok[:, None, :], bias, NEG_INF)
    s = jnp.einsum('brqhd,brkwhd->brhqkw', q, k_rows, preferred_element_type=jnp.float32)
    s = s * (NAT_HEAD_DIM ** -0.5) + bias[None]
    sh = s.shape
    p = jax.nn.softmax(s.reshape(sh[:4] + (kh * GRID_W,)), axis=-1).reshape(sh).astype(v.dtype)
    o = jnp.einsum('brhqkw,brkwhd->brqhd', p, v_rows)
    return o.reshape(B, L, NAT_W)


def conformer_conv(a, g, dw_w, dw_b, ln_g, ln_b):
    u = a * jax.nn.sigmoid(g)
    y = lax.conv_general_dilated(u, dw_w[:, None, :], window_strides=(1,),
                                 padding=[(CONV_WIDTH // 2, CONV_WIDTH // 2)],
                                 dimension_numbers=('NWC', 'WIO', 'NWC'),
                                 feature_group_count=CONV_CH) + dw_b
    return jax.nn.silu(layer_norm(y, ln_g, ln_b))


def _ssm_combine(e1, e2):
    a1, b1 = e1
    a2, b2 = e2
    return a2 * a1, a2 * b1 + b2


def s5(u, lam_re, lam_im, log_step, b_re, b_im, c_re, c_im, d, w_glu):
    B, L, _ = u.shape
    uf = u.astype(jnp.float32)
    ug = uf.reshape(B, L, S5_GROUPS, S5_GROUP).astype(jnp.complex64)
    y = d.astype(jnp.float32) * uf
    for direction, rev in ((0, False), (1, True)):
        lam = lax.complex(lam_re[direction].astype(jnp.float32), lam_im[direction].astype(jnp.float32))
        dt = jnp.exp(log_step[direction].astype(jnp.float32))[:, None]
        lam_bar = jnp.exp(lam * dt)
        bmat = lax.complex(b_re[direction].astype(jnp.float32), b_im[direction].astype(jnp.float32))
        b_bar = ((lam_bar - 1.0) / lam)[..., None] * bmat
        bu = jnp.einsum('blgc,gpc->blgp', ug, b_bar)
        a = jnp.broadcast_to(lam_bar, bu.shape)
        _, xs = lax.associative_scan(_ssm_combine, (a, bu), axis=1, reverse=rev)
        cmat = lax.complex(c_re[direction].astype(jnp.float32), c_im[direction].astype(jnp.float32))
        y = y + jnp.einsum('blgp,gcp->blgc', xs, cmat).real.reshape(B, L, S5_CH)
    z = jax.nn.gelu(y.astype(u.dtype))
    return z * jax.nn.sigmoid(z @ w_glu)


def setup_inputs(seed: int = 0) -> dict:
    key = jax.random.key(seed)
    ks = iter(jax.random.split(key, 40))

    def nrm(shape, scale):
        return jax.random.normal(next(ks), shape, jnp.float32) * scale

    def gain(shape):
        return 1.0 + 0.05 * jax.random.normal(next(ks), shape, jnp.float32)

    P, G = S5_STATE, S5_GROUPS
    lam_im0 = jnp.pi * jnp.arange(P, dtype=jnp.float32)
    return {
        'x_prompt': jax.random.normal(next(ks), (BATCH, SEQ, D_MODEL), jnp.float32),
        'x_sample': jax.random.normal(next(ks), (DEC_BATCH, DEC_SEQ, D_MODEL), jnp.float32),
        'norm_g': gain((DEPTH, 6, D_MODEL)),
        'ffn_w_gate': nrm((DEPTH, 2, D_MODEL, D_FF), D_MODEL ** -0.5),
        'ffn_w_up': nrm((DEPTH, 2, D_MODEL, D_FF), D_MODEL ** -0.5),
        'ffn_w_down': nrm((DEPTH, 2, D_FF, D_MODEL), D_FF ** -0.5),
        'ev_w_in': nrm((N_EVEN, D_MODEL, EV_IN), D_MODEL ** -0.5),
        'mla_q_norm': gain((N_EVEN, MLA_Q_RANK)),
        'mla_kv_norm': gain((N_EVEN, MLA_KV_RANK)),
        'mla_w_uq': nrm((N_EVEN, MLA_Q_RANK, MLA_HEADS * (MLA_NOPE + MLA_ROPE)), MLA_Q_RANK ** -0.5),
        'mla_w_ukv': nrm((N_EVEN, MLA_KV_RANK, MLA_HEADS * (MLA_NOPE + MLA_V)), MLA_KV_RANK ** -0.5),
        'nat_rpb': nrm((N_EVEN, NAT_HEADS, 2 * NAT_KH_MAX - 1, 2 * NAT_KW - 1), 0.02),
        'ev_w_out': nrm((N_EVEN, EV_MIX, D_MODEL), EV_MIX ** -0.5),
        'od_w_in': nrm((N_ODD, D_MODEL, OD_IN), D_MODEL ** -0.5),
        'conv_dw_w': nrm((N_ODD, CONV_WIDTH, CONV_CH), CONV_WIDTH ** -0.5),
        'conv_dw_b': nrm((N_ODD, CONV_CH), 0.02),
        'conv_ln_g': gain((N_ODD, CONV_CH)),
        'conv_ln_b': nrm((N_ODD, CONV_CH), 0.02),
        's5_lambda_re': -0.5 + nrm((N_ODD, 2, G, P), 0.01),
        's5_lambda_im': lam_im0 + nrm((N_ODD, 2, G, P), 0.01),
        's5_log_step': jax.random.uniform(next(ks), (N_ODD, 2, G), jnp.float32,
                                          minval=math.log(1e-3), maxval=math.log(1e-1)),
        's5_b_re': nrm((N_ODD, 2, G, P, S5_GROUP), (2 * S5_GROUP) ** -0.5),
        's5_b_im': nrm((N_ODD, 2, G, P, S5_GROUP), (2 * S5_GROUP) ** -0.5),
        's5_c_re': nrm((N_ODD, 2, G, S5_GROUP, P), (2 * P) ** -0.5),
        's5_c_im': nrm((N_ODD, 2, G, S5_GROUP, P), (2 * P) ** -0.5),
        's5_d': nrm((N_ODD, S5_CH), 1.0),
        's5_w_glu': nrm((N_ODD, S5_CH, S5_CH), S5_CH ** -0.5),
        'od_w_out': nrm((N_ODD, OD_MIX, D_MODEL), OD_MIX ** -0.5),
    }


def reference(x_prompt, x_sample, norm_g, ffn_w_gate, ffn_w_up, ffn_w_down,
              ev_w_in, mla_q_norm, mla_kv_norm, mla_w_uq, mla_w_ukv, nat_rpb, ev_w_out,
              od_w_in, conv_dw_w, conv_dw_b, conv_ln_g, conv_ln_b,
              s5_lambda_re, s5_lambda_im, s5_log_step, s5_b_re, s5_b_im, s5_c_re, s5_c_im,
              s5_d, s5_w_glu, od_w_out):

    def even_mixer(h, i):
        z = h @ ev_w_in[i]
        cuts = [MLA_Q_RANK, MLA_Q_RANK + MLA_KV_RANK, MLA_Q_RANK + MLA_KV_RANK + MLA_ROPE]
        cuts = cuts + [cuts[-1] + NAT_W, cuts[-1] + 2 * NAT_W]
        q_lat, kv_lat, k_rope, nq, nk, nv = jnp.split(z, cuts, axis=-1)
        a = mla(q_lat, kv_lat, k_rope, mla_q_norm[i], mla_kv_norm[i], mla_w_uq[i], mla_w_ukv[i])
        b = neighborhood_attention(nq, nk, nv, nat_rpb[i])
        return jnp.concatenate([a, b], axis=-1) @ ev_w_out[i]

    def odd_mixer(h, i):
        z = h @ od_w_in[i]
        ca, cg, su = jnp.split(z, [CONV_CH, 2 * CONV_CH], axis=-1)
        c = conformer_conv(ca, cg, conv_dw_w[i], conv_dw_b[i], conv_ln_g[i], conv_ln_b[i])
        s = s5(su, s5_lambda_re[i], s5_lambda_im[i], s5_log_step[i], s5_b_re[i], s5_b_im[i],
               s5_c_re[i], s5_c_im[i], s5_d[i], s5_w_glu[i])
        return jnp.concatenate([c, s], axis=-1) @ od_w_out[i]

    def trunk(x):
        h = x
        for layer in range(DEPTH):
            g = norm_g[layer]
            f = swiglu(rms_norm(h, g[0]), ffn_w_gate[layer, 0], ffn_w_up[layer, 0], ffn_w_down[layer, 0])
            h = h + FFN_RES_SCALE * rms_norm(f, g[1])
            m = rms_norm(h, g[2])
            if layer % 2 == 0:
                m = even_mixer(m, layer // 2)
            else:
                m = odd_mixer(m, layer // 2)
            h = h + rms_norm(m, g[3])
            f = swiglu(rms_norm(h, g[4]), ffn_w_gate[layer, 1], ffn_w_up[layer, 1], ffn_w_down[layer, 1])
            h = h + FFN_RES_SCALE * rms_norm(f, g[5])
        return h

    y_prompt = trunk(x_prompt)
    y_sample = trunk(x_sample)
    return (y_prompt, y_sample)
```

```python
import contextlib
import os
import numpy as np
import ml_dtypes
import concourse.bass as bass
import concourse.mybir as mybir
from concourse.bass_utils import run_bass_kernel_spmd

F32 = mybir.dt.float32
BF16 = mybir.dt.bfloat16
AF = mybir.ActivationFunctionType
ALU = mybir.AluOpType

T = 16384
D = 1024
DFF = 2816
NFC = DFF // 128
EPS = 1e-6
NCORES = 3


class Buf:
    __slots__ = ("lw", "rd", "name")

    def __init__(self, name=""):
        self.lw = None
        self.rd = {}
        self.name = name


class Sched:
    CENG = ("pe", "act", "dve", "pool")
    ENGS = ("pe", "act", "dve", "pool", "sp")
    NSLOT = {"sp": 16, "pool": 8, "act": 4}

    def __init__(self, nc, stack):
        self.nc = nc
        self.items = {e: [] for e in self.ENGS}
        self.nops = {e: 0 for e in self.CENG}
        self.known = {e: {} for e in self.ENGS}
        self.dma_n = {q: 0 for q in self.NSLOT}
        self.signalled = set()
        self.val = {}
        self.cnt = {e: 0 for e in self.CENG}
        self.sem = {}
        for e in self.CENG:
            self.sem[e] = stack.enter_context(nc.semaphore("s_" + e))
        for q, n in self.NSLOT.items():
            for s in range(n):
                self.sem[("d", q, s)] = stack.enter_context(nc.semaphore("d_%s%d" % (q, s)))

    def _deps(self, eng, reads, writes, extra=()):
        deps = {}

        def add(sig):
            if sig is None:
                return
            k, i = sig
            if deps.get(k, -1) < i:
                deps[k] = i
        for b in reads:
            add(b.lw)
        for b in writes:
            add(b.lw)
            for k, i in b.rd.items():
                if k == eng and eng in self.CENG:
                    continue
                add((k, i))
        for s in extra:
            add(s)
        if eng == "pe":
            deps.pop("pe", None)
        waits = []
        kn = self.known[eng]
        for k, i in deps.items():
            if kn.get(k, -1) >= i:
                continue
            kn[k] = i
            waits.append((k, i))
            self.signalled.add((k, i))
        return waits

    def _commit(self, sig, rkey, ridx, reads, writes):
        for b in writes:
            b.lw = sig
            b.rd = {}
        for b in reads:
            if b.rd.get(rkey, -1) < ridx:
                b.rd[rkey] = ridx

    def op(self, eng, fn, reads=(), writes=()):
        waits = self._deps(eng, reads, writes)
        idx = self.nops[eng]
        self.nops[eng] += 1
        sig = (eng, idx)
        self.items[eng].append((waits, fn, sig))
        self._commit(sig, eng, idx, reads, writes)
        return sig

    def dma(self, q, out, in_, reads=(), writes=(), **kw):
        n = self.dma_n[q]
        self.dma_n[q] += 1
        K = self.NSLOT[q]
        slot, rnd = n % K, n // K
        key = ("d", q, slot)
        extra = [(key, rnd - 1)] if rnd >= 1 else []
        waits = self._deps(q, reads, writes, extra)
        sig = (key, rnd)
        self.items[q].append((waits, lambda e: e.dma_start(out=out, in_=in_, **kw), sig))
        self._commit(sig, key, rnd, reads, writes)
        return sig

    def barrier(self):
        targets = []
        for e in self.CENG:
            if self.nops[e] > 0:
                targets.append((e, self.nops[e] - 1))
        for q, K in self.NSLOT.items():
            n = self.dma_n[q]
            for s in range(K):
                if n > s:
                    targets.append((("d", q, s), (n - 1 - s) // K))
        for e in self.ENGS:
            kn = self.known[e]
            waits = []
            for k, i in targets:
                if k == e:
                    continue
                if kn.get(k, -1) >= i:
                    continue
                kn[k] = i
                waits.append((k, i))
                self.signalled.add((k, i))
            self.items[e].append((waits, None, None))

    def emit(self):
        self.barrier()
        for e in self.CENG:
            for (_, fn, sig) in self.items[e]:
                if sig is not None and sig[0] == e and sig in self.signalled and sig not in self.val:
                    self.cnt[e] += 1
                    self.val[sig] = self.cnt[e]
        with self.nc.Block() as block:
            @block.tensor
            def _(h):
                self._emit_eng("pe", h)

            @block.scalar
            def _(h):
                self._emit_eng("act", h)

            @block.vector
            def _(h):
                self._emit_eng("dve", h)

            @block.gpsimd
            def _(h):
                self._emit_eng("pool", h)

            @block.sync
            def _(h):
                self._emit_eng("sp", h)
        self.items = {e: [] for e in self.ENGS}

    def _emit_eng(self, e, h):
        for (waits, fn, sig) in self.items[e]:
            for (k, i) in waits:
                if isinstance(k, tuple):
                    h.wait_ge(self.sem[k], 16 * (i + 1))
                else:
                    h.wait_ge(self.sem[k], self.val[(k, i)])
            if fn is None:
                continue
            ins = fn(h)
            if isinstance(sig[0], tuple):
                ins.then_inc(self.sem[sig[0]], 16)
            elif sig in self.val:
                ins.then_inc(self.sem[sig[0]], 1)


def bcast_rows(ap1d, nparts):
    return bass.AP(ap1d.tensor, ap1d.offset, [[0, nparts]] + [list(x) for x in ap1d.ap])


class Ctx:
    pass


def load_weight_bf16(S, dst_sb, dst_buf, src, nchunks):
    for kc in range(nchunks):
        S.dma("pool", dst_sb[:, kc, :], src[kc * 128:(kc + 1) * 128, :], writes=[dst_buf])


def phase_ffn(nc, S, C, h_in, h_out, wg, wu, wd, g_in, g_out):
    with contextlib.ExitStack() as st:
        sb, ps = mk(st, nc)
        Wg = sb("Wg", [128, 8, DFF], BF16)
        Wu = sb("Wu", [128, 8, DFF], BF16)
        Wd = sb("Wd", [128, NFC, D], BF16)
        gin = sb("gin", [128, D], F32)
        gout = sb("gout", [128, D], F32)
        Xa = sb("Xa", [128, 2, D], F32)
        Xb = sb("Xb", [128, 2, D], F32)
        xn = sb("xn", [128, 2, D], BF16)
        xnT = sb("xnT", [128, 2, 8, 512], BF16)
        aT = sb("aT", [128, NFC, 512], BF16)
        sg = sb("sg", [128, 2, 512], BF16)
        tmp = sb("tmp", [128, 2, 512], F32)
        junk = sb("junk", [128, D], BF16)
        st_ss = sb("ss", [128, 16], F32)
        G = [ps("G%d" % i, [128, 512], F32) for i in range(2)]
        U = [ps("U%d" % i, [128, 512], F32) for i in range(2)]
        Dp = [ps("D%d" % i, [128, 512], F32) for i in range(3)]
        TP = ps("TP", [128, 8, 128], BF16)

        bWg, bWu, bWd, bgin, bgout = Buf(), Buf(), Buf(), Buf(), Buf()
        bXa = [Buf() for _ in range(2)]
        bXb = [Buf() for _ in range(2)]
        bxn = [Buf() for _ in range(2)]
        bxnT = [[Buf() for _ in range(4)] for _ in range(2)]
        baT = [Buf() for _ in range(NFC)]
        bsg = [Buf() for _ in range(2)]
        btmp = [Buf() for _ in range(2)]
        bjunk = Buf()
        bss = [Buf() for _ in range(4)]
        bss2 = [Buf() for _ in range(4)]
        bG = [Buf() for _ in range(2)]
        bU = [Buf() for _ in range(2)]
        bD = [Buf() for _ in range(3)]
        bTP = Buf()

        S.dma("sp", gin[:], bcast_rows(g_in, 128), writes=[bgin])
        S.dma("sp", gout[:], bcast_rows(g_out, 128), writes=[bgout])
        S.op("pool", lambda e: e.tensor_scalar(out=gout[:], in0=gout[:], scalar1=0.5, scalar2=None, op0=ALU.mult),
             reads=[bgout], writes=[bgout])
        load_weight_bf16(S, Wg, bWg, wg, 8)
        load_weight_bf16(S, Wu, bWu, wu, 8)
        load_weight_bf16(S, Wd, bWd, wd, NFC)
        ntile = int(os.environ.get('FFN_NT', T // 512))
        nfe = [0]

        xslot = {}

        def front_norm(t, s):
            xb = nfe[0] % 2
            nfe[0] += 1
            xslot[(t, s)] = xb
            r0 = t * 512 + s * 128
            S.dma("sp", Xa[:, xb, :], h_in[r0:r0 + 128, :], writes=[bXa[xb]])
            S.op("act", lambda e: e.activation(out=junk[:], in_=Xa[:, xb, :], func=AF.Square, accum_out=st_ss[:, s:s + 1]),
                 reads=[bXa[xb]], writes=[bjunk, bss[s]])
            rstd_inplace(S, st_ss[:, s:s + 1], bss[s], D)
            S.op("dve", lambda e: e.scalar_tensor_tensor(out=xn[:, xb, :], in0=Xa[:, xb, :], scalar=st_ss[:, s:s + 1],
                                                         in1=gin[:], op0=ALU.mult, op1=ALU.mult),
                 reads=[bXa[xb], bss[s], bgin], writes=[bxn[xb]])

        def front_T(t, s):
            tb = t % 2
            xb = xslot.pop((t, s))
            for kc in range(8):
                S.op("pe", lambda e, kc=kc: e.transpose(out=TP[:, kc, :], in_=xn[:, xb, kc * 128:(kc + 1) * 128],
                                                        identity=C.ident[:]),
                     reads=[bxn[xb], C.bident], writes=[bTP])
            S.op("act", lambda e: e.copy(out=xnT[:, tb, :, s * 128:(s + 1) * 128], in_=TP[:]),
                 reads=[bTP], writes=[bxnT[tb][s]])

        front_norm(0, 0)
        for s in range(4):
            if s + 1 < 4:
                front_norm(0, s + 1)
            front_T(0, s)
        nep = 0
        for t in range(ntile):
            tb = t % 2
            for fc in range(NFC):
                b = fc % 2
                for kc in range(8):
                    S.op("pe", lambda e, fc=fc, kc=kc, b=b, tb=tb: e.matmul(G[b][:], lhsT=Wg[:, kc, fc * 128:(fc + 1) * 128],
                                                                     rhs=xnT[:, tb, kc, :], start=(kc == 0), stop=(kc == 7)),
                         reads=[bWg] + bxnT[tb], writes=[bG[b]])
                for kc in range(8):
                    S.op("pe", lambda e, fc=fc, kc=kc, b=b, tb=tb: e.matmul(U[b][:], lhsT=Wu[:, kc, fc * 128:(fc + 1) * 128],
                                                                     rhs=xnT[:, tb, kc, :], start=(kc == 0), stop=(kc == 7)),
                         reads=[bWu] + bxnT[tb], writes=[bU[b]])
                S.op("act", lambda e, b=b: e.activation(out=sg[:, b, :], in_=G[b][:], func=AF.Silu),
                     reads=[bG[b]], writes=[bsg[b]])
                S.op("dve", lambda e, b=b, fc=fc: e.tensor_tensor(out=aT[:, fc, :], in0=sg[:, b, :], in1=U[b][:], op=ALU.mult),
                     reads=[bsg[b], bU[b]], writes=[baT[fc]])
            if t + 1 < ntile:
                front_norm(t + 1, 0)
            for s in range(4):
                eb = nep % 2
                nep += 1
                r0 = t * 512 + s * 128
                S.dma("sp", Xb[:, eb, :], h_in[r0:r0 + 128, :], writes=[bXb[eb]])
                dbs = []
                for half in range(2):
                    di = (2 * s + half) % 3
                    dbs.append(di)
                    for fc in range(NFC):
                        S.op("pe", lambda e, di=di, fc=fc, s=s, half=half: e.matmul(
                            Dp[di][:], lhsT=aT[:, fc, s * 128:(s + 1) * 128],
                            rhs=Wd[:, fc, half * 512:(half + 1) * 512], start=(fc == 0), stop=(fc == NFC - 1)),
                            reads=[bWd, baT[fc]], writes=[bD[di]])
                    S.op("act", lambda e, di=di, s=s, half=half: e.activation(
                        out=junk[:, 0:512], in_=Dp[di][:], func=AF.Square,
                        accum_out=st_ss[:, 4 + 2 * s + half:5 + 2 * s + half]),
                        reads=[bD[di]], writes=[bjunk, bss2[s]])
                if t + 1 < ntile:
                    if s + 1 < 4:
                        front_norm(t + 1, s + 1)
                    front_T(t + 1, s)
                c0 = 4 + 2 * s
                S.op("dve", lambda e, c0=c0: e.tensor_tensor(out=st_ss[:, c0:c0 + 1], in0=st_ss[:, c0:c0 + 1],
                                                             in1=st_ss[:, c0 + 1:c0 + 2], op=ALU.add),
                     reads=[bss2[s]], writes=[bss2[s]])
                rstd_inplace(S, st_ss[:, c0:c0 + 1], bss2[s], D)
                for half in range(2):
                    di = dbs[half]
                    S.op("dve", lambda e, di=di, c0=c0, half=half: e.scalar_tensor_tensor(
                        out=tmp[:, half, :], in0=Dp[di][:], scalar=st_ss[:, c0:c0 + 1],
                        in1=gout[:, half * 512:(half + 1) * 512], op0=ALU.mult, op1=ALU.mult),
                        reads=[bD[di], bss2[s], bgout], writes=[btmp[half]])
                    S.op("pool", lambda e, eb=eb, half=half: e.tensor_tensor(
                        out=Xb[:, eb, half * 512:(half + 1) * 512], in0=Xb[:, eb, half * 512:(half + 1) * 512],
                        in1=tmp[:, half, :], op=ALU.add),
                        reads=[bXb[eb], btmp[half]], writes=[bXb[eb]])
                S.dma("pool", h_out[r0:r0 + 128, :], Xb[:, eb, :], reads=[bXb[eb]])
        S.emit()


_PH = [0]


def mk(st, nc):
    _PH[0] += 1
    pre = "p%d_" % _PH[0]

    def sb(name, shape, dt):
        return st.enter_context(nc.sbuf_tensor(pre + name, shape, dt))

    def ps(name, shape, dt):
        return st.enter_context(nc.psum_tensor(pre + name, shape, dt))
    return sb, ps


def bc_ap(a, shape_ap):
    return bass.AP(a.tensor, a.offset, [list(a.ap[0])] + [list(x) for x in shape_ap])


def rstd_inplace(S, ss, b, n):
    S.op("dve", lambda e: e.tensor_scalar(out=ss, in0=ss, scalar1=1.0 / n, scalar2=EPS, op0=ALU.mult, op1=ALU.add),
         reads=[b], writes=[b])
    S.op("act", lambda e: e.activation(out=ss, in_=ss, func=AF.Sqrt), reads=[b], writes=[b])
    S.op("dve", lambda e: e.reciprocal(out=ss, in_=ss), reads=[b], writes=[b])


def rope_ops(S, x1, x2, cos, sin, o1, o2, tt, bx, btt, bo, bcs, bcast_h=None):
    t1, t2, t3, t4 = tt
    S.op("dve", lambda e: e.tensor_tensor(out=t1, in0=x1, in1=cos, op=ALU.mult), reads=[bx, bcs], writes=[btt[0]])
    S.op("dve", lambda e: e.tensor_tensor(out=t2, in0=x2, in1=sin, op=ALU.mult), reads=[bx, bcs], writes=[btt[1]])
    S.op("pool", lambda e: e.tensor_tensor(out=t3, in0=x2, in1=cos, op=ALU.mult), reads=[bx, bcs], writes=[btt[2]])
    S.op("pool", lambda e: e.tensor_tensor(out=t4, in0=x1, in1=sin, op=ALU.mult), reads=[bx, bcs], writes=[btt[3]])
    a1, a2, a3, a4 = (t1, t2, t3, t4) if bcast_h is None else bcast_h
    S.op("dve", lambda e: e.tensor_tensor(out=o1, in0=a1, in1=a2, op=ALU.subtract), reads=[btt[0], btt[1]], writes=[bo])
    S.op("pool", lambda e: e.tensor_tensor(out=o2, in0=a3, in1=a4, op=ALU.add), reads=[btt[2], btt[3]], writes=[bo])


def phase_ev_in(nc, S, C, h_in, g2, w_in, qn_g, kvn_g, w_uq, w_ukv, cos_d, sin_d, kaug, qaug, qT, kT, vA, nqT, nkT, nvA):
    with contextlib.ExitStack() as st:
        sb, ps = mk(st, nc)
        Win = sb("Win", [128, 8, 1952], BF16)
        Wuq = sb("Wuq", [128, 2, 768], BF16)
        Wukv = sb("Wukv", [128, 1024], BF16)
        gin = sb("gin", [128, D], F32)
        qng = sb("qng", [128, 256], F32)
        kvng = sb("kvng", [128, 128], F32)
        cos = sb("cos", [128, 128, 16], F32)
        sin = sb("sin", [128, 128, 16], F32)
        X = sb("X", [128, 2, 4, D], F32)
        xn = sb("xn", [128, 2, D], BF16)
        mT = sb("mT", [128, 2, 8, 512], BF16)
        junk = sb("junk", [128, D], BF16)
        ss = sb("ss", [128, 2, 8], F32)
        ssl = sb("ssl", [128, 8], F32)
        lat_bf = sb("lat_bf", [128, 384], BF16)
        kr_f = sb("kr_f", [128, 32], F32)
        latT = sb("latT", [128, 2, 3, 128], BF16)
        q_sb = sb("q_sb", [128, 8, 96], F32)
        q_bf = sb("q_bf", [128, 8, 96], BF16)
        k_bf = sb("k_bf", [128, 8, 96], BF16)
        tq = sb("tq", [128, 4, 8, 16], F32)
        tk = sb("tk", [128, 4, 16], F32)
        qT_sb = sb("qT_sb", [96, 8, 512], BF16)
        kT_sb = sb("kT_sb", [96, 8, 512], BF16)
        v_sb = sb("v_sb", [128, 4, 8, 65], BF16)
        nv_sb = sb("nv_sb", [128, 4, 8, 65], BF16)
        nst = sb("nst", [128, 2, 512], BF16)
        TPm = ps("TPm", [128, 8, 128], BF16)
        LAT = ps("LAT", [128, 416], F32)
        TP2 = ps("TP2", [128, 3, 128], BF16)
        TPq = ps("TPq", [128, 8, 128], BF16)
        P = [ps("P%d" % i, [128, 512], F32) for i in range(4)]

        bW, bg, bcs = Buf(), Buf(), Buf()
        bX = [[Buf() for _ in range(4)] for _ in range(2)]
        bxn = [Buf() for _ in range(2)]
        bmT = [[Buf() for _ in range(4)] for _ in range(2)]
        bjunk, bssl = Buf(), Buf()
        bss = [Buf() for _ in range(2)]
        blat, bkr, bq, bqb, bkb = Buf(), Buf(), Buf(), Buf(), Buf()
        blatT = [Buf() for _ in range(2)]
        btq = [Buf() for _ in range(4)]
        btk = [Buf() for _ in range(4)]
        bqT, bkT, bv, bnv = Buf(), Buf(), Buf(), Buf()
        bnst = [Buf() for _ in range(2)]
        bTPm, bLAT, bTP2, bTPq = Buf(), Buf(), Buf(), Buf()
        bP = [Buf() for _ in range(4)]

        S.dma("sp", gin[:], bcast_rows(g2, 128), writes=[bg])
        S.dma("sp", qng[:], bcast_rows(qn_g, 128), writes=[bg])
        S.dma("sp", kvng[:], bcast_rows(kvn_g, 128), writes=[bg])
        S.dma("sp", cos[:], cos_d.rearrange("(j p) e -> p j e", p=128), writes=[bcs])
        S.dma("sp", sin[:], sin_d.rearrange("(j p) e -> p j e", p=128), writes=[bcs])
        load_weight_bf16(S, Win, bW, w_in, 8)
        load_weight_bf16(S, Wuq, bW, w_uq, 2)
        S.dma("pool", Wukv[:], w_ukv, writes=[bW])
        for h in range(8):
            S.dma("sp", kT[h, 96:98, :], kaug, writes=[Buf()])
            S.dma("sp", qT[h, 96:98, :], qaug, writes=[Buf()])
        S.op("pool", lambda e: e.memset(v_sb[:, :, :, 64:65], 1.0), writes=[bv])
        S.op("pool", lambda e: e.memset(nv_sb[:, :, :, 64:65], 1.0), writes=[bnv])

        for t in range(T // 512):
            tb = t % 2
            norm_tile(nc, S, C, t, h_in, X, bX, xn, bxn, mT, bmT, TPm, bTPm, junk, bjunk, ss, bss, gin, bg)
            for s in range(4):
                sl = slice(s * 128, (s + 1) * 128)
                lb = s % 2
                for kc in range(8):
                    S.op("pe", lambda e, kc=kc, sl=sl, tb=tb: e.matmul(LAT[:], lhsT=mT[:, tb, kc, sl], rhs=Win[:, kc, 0:416],
                                                               start=(kc == 0), stop=(kc == 7)),
                         reads=[bmT[tb][s], bW], writes=[bLAT])
                S.op("act", lambda e: e.activation(out=junk[:, 0:256], in_=LAT[:, 0:256], func=AF.Square,
                                                   accum_out=ssl[:, 4:5]), reads=[bLAT], writes=[bjunk, bssl])
                S.op("act", lambda e: e.activation(out=junk[:, 0:128], in_=LAT[:, 256:384], func=AF.Square,
                                                   accum_out=ssl[:, 5:6]), reads=[bLAT], writes=[bjunk, bssl])
                S.op("dve", lambda e: e.tensor_scalar(out=ssl[:, 4:5], in0=ssl[:, 4:5], scalar1=1.0 / 256, scalar2=EPS,
                                                      op0=ALU.mult, op1=ALU.add), reads=[bssl], writes=[bssl])
                S.op("dve", lambda e: e.tensor_scalar(out=ssl[:, 5:6], in0=ssl[:, 5:6], scalar1=1.0 / 128, scalar2=EPS,
                                                      op0=ALU.mult, op1=ALU.add), reads=[bssl], writes=[bssl])
                S.op("act", lambda e: e.activation(out=ssl[:, 4:6], in_=ssl[:, 4:6], func=AF.Sqrt), reads=[bssl], writes=[bssl])
                S.op("dve", lambda e: e.reciprocal(out=ssl[:, 4:6], in_=ssl[:, 4:6]), reads=[bssl], writes=[bssl])
                S.op("dve", lambda e: e.scalar_tensor_tensor(out=lat_bf[:, 0:256], in0=LAT[:, 0:256], scalar=ssl[:, 4:5],
                                                             in1=qng[:], op0=ALU.mult, op1=ALU.mult),
                     reads=[bLAT, bssl, bg], writes=[blat])
                S.op("dve", lambda e: e.scalar_tensor_tensor(out=lat_bf[:, 256:384], in0=LAT[:, 256:384], scalar=ssl[:, 5:6],
                                                             in1=kvng[:], op0=ALU.mult, op1=ALU.mult),
                     reads=[bLAT, bssl, bg], writes=[blat])
                S.op("act", lambda e: e.copy(out=kr_f[:], in_=LAT[:, 384:416]), reads=[bLAT], writes=[bkr])
                j = t * 4 + s
                cs, sn = cos[:, j, :], sin[:, j, :]
                tks = [tk[:, i, :] for i in range(4)]
                bch = [bc_ap(tk[:, i, :], [[0, 8], [1, 16]]) for i in range(4)]
                rope_ops(S, kr_f[:, 0:16], kr_f[:, 16:32], cs, sn, k_bf[:, :, 64:80], k_bf[:, :, 80:96], tks,
                         bkr, btk, bkb, bcs, bcast_h=bch)
                for c in range(3):
                    S.op("pe", lambda e, c=c: e.transpose(out=TP2[:, c, :], in_=lat_bf[:, c * 128:(c + 1) * 128],
                                                          identity=C.ident[:]),
                         reads=[blat, C.bident], writes=[bTP2])
                S.op("act", lambda e, lb=lb: e.copy(out=latT[:, lb, :, :], in_=TP2[:]), reads=[bTP2], writes=[blatT[lb]])
                for c in range(2):
                    S.op("pe", lambda e, c=c, lb=lb: e.matmul(P[0][:], lhsT=latT[:, lb, c, :], rhs=Wuq[:, c, 0:512],
                                                              start=(c == 0), stop=(c == 1)),
                         reads=[blatT[lb], bW], writes=[bP[0]])
                for c in range(2):
                    S.op("pe", lambda e, c=c, lb=lb: e.matmul(P[1][:, 0:256], lhsT=latT[:, lb, c, :], rhs=Wuq[:, c, 512:768],
                                                              start=(c == 0), stop=(c == 1)),
                         reads=[blatT[lb], bW], writes=[bP[1]])
                qf = q_sb[:].rearrange("p h e -> p (h e)")
                S.op("act", lambda e, qf=qf: e.copy(out=qf[:, 0:512], in_=P[0][:]), reads=[bP[0]], writes=[bq])
                S.op("act", lambda e, qf=qf: e.copy(out=qf[:, 512:768], in_=P[1][:, 0:256]), reads=[bP[1]], writes=[bq])
                S.op("dve", lambda e: e.tensor_copy(out=q_bf[:, :, 0:64], in_=q_sb[:, :, 0:64]), reads=[bq], writes=[bqb])
                csb = bc_ap(cs, [[0, 8], [1, 16]])
                snb = bc_ap(sn, [[0, 8], [1, 16]])
                tqs = [tq[:, i, :, :] for i in range(4)]
                rope_ops(S, q_sb[:, :, 64:80], q_sb[:, :, 80:96], csb, snb, q_bf[:, :, 64:80], q_bf[:, :, 80:96], tqs,
                         bq, btq, bqb, bcs)
                for hf in range(2):
                    S.op("pe", lambda e, hf=hf, lb=lb: e.matmul(P[2 + hf][:], lhsT=latT[:, lb, 2, :],
                                                                rhs=Wukv[:, hf * 512:(hf + 1) * 512], start=True, stop=True),
                         reads=[blatT[lb], bW], writes=[bP[2 + hf]])
                    pv = bc_ap(P[2 + hf][:], [[128, 4], [1, 64]])
                    pv2 = bass.AP(pv.tensor, pv.offset + 64, [list(x) for x in pv.ap])
                    S.op("act", lambda e, hf=hf, pv=pv: e.copy(out=k_bf[:, hf * 4:(hf + 1) * 4, 0:64], in_=pv),
                         reads=[bP[2 + hf]], writes=[bkb])
                    S.op("dve", lambda e, hf=hf, pv2=pv2, s=s: e.tensor_copy(out=v_sb[:, s, hf * 4:(hf + 1) * 4, 0:64], in_=pv2),
                         reads=[bP[2 + hf]], writes=[bv])
                for h in range(8):
                    S.op("pe", lambda e, h=h: e.transpose(out=TPq[0:96, h, :], in_=q_bf[:, h, :], identity=C.ident[:]),
                         reads=[bqb, C.bident], writes=[bTPq])
                S.op("act", lambda e, sl=sl: e.copy(out=qT_sb[:, :, sl], in_=TPq[0:96, :, :]), reads=[bTPq], writes=[bqT])
                for h in range(8):
                    S.op("pe", lambda e, h=h: e.transpose(out=TPq[0:96, h, :], in_=k_bf[:, h, :], identity=C.ident[:]),
                         reads=[bkb, C.bident], writes=[bTPq])
                S.op("dve", lambda e, sl=sl: e.tensor_copy(out=kT_sb[:, :, sl], in_=TPq[0:96, :, :]), reads=[bTPq], writes=[bkT])
                for kc in range(8):
                    S.op("pe", lambda e, kc=kc, sl=sl, tb=tb: e.matmul(P[0][:], lhsT=mT[:, tb, kc, sl], rhs=Win[:, kc, 1440:1952],
                                                               start=(kc == 0), stop=(kc == 7)),
                         reads=[bmT[tb][s], bW], writes=[bP[0]])
                pv = bc_ap(P[0][:], [[64, 8], [1, 64]])
                S.op("act", lambda e, pv=pv, s=s: e.copy(out=nv_sb[:, s, :, 0:64], in_=pv), reads=[bP[0]], writes=[bnv])
            for c in range(8):
                pi = 1 + (c % 2) * 2
                for kc in range(8):
                    S.op("pe", lambda e, kc=kc, c=c, pi=pi, tb=tb: e.matmul(P[pi][:], lhsT=Win[:, kc, 416 + c * 128:544 + c * 128],
                                                                     rhs=mT[:, tb, kc, :], start=(kc == 0), stop=(kc == 7)),
                         reads=bmT[tb] + [bW], writes=[bP[pi]])
                nb = c % 2
                if c % 2 == 0:
                    S.op("act", lambda e, nb=nb, pi=pi: e.copy(out=nst[:, nb, :], in_=P[pi][:]), reads=[bP[pi]], writes=[bnst[nb]])
                else:
                    S.op("dve", lambda e, nb=nb, pi=pi: e.tensor_copy(out=nst[:, nb, :], in_=P[pi][:]), reads=[bP[pi]], writes=[bnst[nb]])
                dst = (nqT if c < 4 else nkT).rearrange("h r t -> (h r) t")
                cc = c % 4
                S.dma("pool", dst[cc * 128:(cc + 1) * 128, t * 512:(t + 1) * 512], nst[:, nb, :], reads=[bnst[nb]])
            cols = slice(t * 512, (t + 1) * 512)
            S.dma("pool", qT[:, 0:96, cols].rearrange("h r t -> r h t"), qT_sb[:], reads=[bqT])
            S.dma("pool", kT[:, 0:96, cols].rearrange("h r t -> r h t"), kT_sb[:], reads=[bkT])
            S.dma("pool", vA[t * 512:(t + 1) * 512].rearrange("(s p) h e -> p s h e", p=128), v_sb[:], reads=[bv])
            S.dma("pool", nvA[t * 512:(t + 1) * 512].rearrange("(s p) h e -> p s h e", p=128), nv_sb[:], reads=[bnv])
        S.emit()


MLA_SCALE = 96.0 ** -0.5


def phase_mla(nc, S, C, qT, kT, vA, mlaT, rcd):
    with contextlib.ExitStack() as st:
        sb, ps = mk(st, nc)
        KT = sb("KT", [98, 2, T], BF16)
        V = sb("V", [128, 2, 128, 65], BF16)
        qt_sb = sb("qt_sb", [98, 3, 512], BF16)
        PT = sb("PT", [128, 3, 1024], BF16)
        rc = sb("rc", [128, 2, 512], F32)
        bcs = sb("bcs", [64, 2, 512], F32)
        oT = sb("oT", [64, 2, 512], BF16)
        Sp = [ps("S%d" % i, [128, 1024], F32) for i in range(3)]
        O = [ps("O%d" % i, [128, 512], F32) for i in range(2)]
        bKT = [Buf() for _ in range(2)]
        bV = [Buf() for _ in range(2)]
        bq = [Buf() for _ in range(3)]
        bPT = [Buf() for _ in range(3)]
        bS = [Buf() for _ in range(3)]
        bO = [Buf() for _ in range(2)]
        brc = [Buf() for _ in range(2)]
        brcd = [Buf() for _ in range(2)]
        bbcs = [Buf() for _ in range(2)]
        boT = [Buf() for _ in range(2)]
        tiles = [(h, qt) for h in range(8) for qt in range(T // 512)]
        groups = [list(range(k, k + 2)) for k in range(0, 128, 2)]

        def load_head(h):
            hb = h % 2
            for c in range(4):
                S.dma("sp", KT[:, hb, c * 4096:(c + 1) * 4096], kT[h, :, c * 4096:(c + 1) * 4096], writes=[bKT[hb]])
            vsrc = bass.AP(vA.tensor, vA.offset + h * 65, [[8 * 65, 128], [128 * 8 * 65, 128], [1, 65]])
            S.dma("sp", V[:, hb, :, :], vsrc, writes=[bV[hb]])

        def load_q(ti):
            h, qt = tiles[ti]
            S.dma("sp", qt_sb[:, ti % 3, :], qT[h, :, qt * 512:(qt + 1) * 512], writes=[bq[ti % 3]])

        def qk(k, ti, kbs):
            h, qt = tiles[ti]
            hb, qb = h % 2, ti % 3
            g = k % 3
            for j, kb in enumerate(kbs):
                S.op("pe", lambda e, kb=kb, g=g, hb=hb, qb=qb, j=j: e.matmul(
                    Sp[g][:, j * 512:(j + 1) * 512], lhsT=KT[:, hb, kb * 128:(kb + 1) * 128], rhs=qt_sb[:, qb, :],
                    start=True, stop=True), reads=[bKT[hb], bq[qb]], writes=[bS[g]])

        def ex(k, ti, kbs):
            g, p = k % 3, k % 3
            n = len(kbs) * 512
            S.op("act", lambda e, g=g, p=p, n=n: e.activation(out=PT[:, p, 0:n], in_=Sp[g][:, 0:n], func=AF.Exp, scale=MLA_SCALE),
                 reads=[bS[g]], writes=[bPT[p]])

        def pv(k, ti, kbs):
            h, qt = tiles[ti]
            hb, ob = h % 2, ti % 2
            p = k % 3
            for j, kb in enumerate(kbs):
                S.op("pe", lambda e, kb=kb, p=p, hb=hb, ob=ob, j=j: e.matmul(
                    O[ob][0:65, :], lhsT=V[:, hb, kb, :], rhs=PT[:, p, j * 512:(j + 1) * 512],
                    start=(kb == 0), stop=(kb == 127)), reads=[bV[hb], bPT[p]], writes=[bO[ob]])

        def epilogue(ti):
            h, qt = tiles[ti]
            ob = ti % 2
            S.op("dve", lambda e, ob=ob: e.reciprocal(out=rc[64:65, ob, :], in_=O[ob][64:65, :]), reads=[bO[ob]], writes=[brc[ob]])
            S.dma("sp", rcd[ob:ob + 1, :], rc[64:65, ob, :], reads=[brc[ob]], writes=[brcd[ob]])
            S.dma("sp", bcs[:, ob, :], bcast_rows(rcd[ob], 64), reads=[brcd[ob]], writes=[bbcs[ob]])
            S.op("dve", lambda e, ob=ob: e.tensor_tensor(out=oT[:, ob, :], in0=O[ob][0:64, :], in1=bcs[:, ob, :], op=ALU.mult),
                 reads=[bO[ob], bbcs[ob]], writes=[boT[ob]])
            S.dma("sp", mlaT[h * 64:(h + 1) * 64, qt * 512:(qt + 1) * 512], oT[:, ob, :], reads=[boT[ob]])

        steps = [(ti, kbs) for ti in range(len(tiles)) for kbs in groups]
        load_head(0)
        load_q(0)
        load_q(1)
        AHEAD = 2
        for k0 in range(AHEAD):
            qk(k0, *steps[k0])
        pending_epi = None
        for k, (ti, kbs) in enumerate(steps):
            if kbs[0] == 0:
                h, qt = tiles[ti]
                if ti + 2 < len(tiles):
                    load_q(ti + 2)
                if qt == 0 and h + 1 < 8:
                    load_head(h + 1)
            ex(k, ti, kbs)
            if k + AHEAD < len(steps):
                qk(k + AHEAD, *steps[k + AHEAD])
            pv(k, ti, kbs)
            if pending_epi is not None and kbs[0] == 6:
                epilogue(pending_epi)
                pending_epi = None
            if kbs[-1] == 127:
                pending_epi = ti
        if pending_epi is not None:
            epilogue(pending_epi)
        S.emit()


def nat_ws(R, rows):
    return min(max(R - 4, 0), rows - 8)


def phase_nat(nc, S, C, nqT, nkT, nvA, rpb, jpad_d, colmask_d, flag_d, rep, natO):
    with contextlib.ExitStack() as st:
        sb, ps = mk(st, nc)
        KT = sb("nKT", [64, T], BF16)
        QT = sb("nQT", [64, T], BF16)
        V = sb("nV", [128, 256, 65], BF16)
        B = sb("nB", [128, 8, 14, 64], F32)
        Bint = sb("nBint", [128, 8, 512], F32)
        cm = sb("cm", [128, 64], F32)
        rT = sb("rT", [31, 120], F32)
        jp = sb("jp", [31, 127], F32)
        pr = sb("pr", [120, 127], F32)
        fl = sb("fl", [128, 1], F32)
        Sb = sb("Sb", [128, 2, 512], F32)
        Pb = sb("Pb", [128, 2, 512], BF16)
        Ost = sb("Ost", [64, 256, 64], BF16)
        OB = sb("OB", [64, 8, 64], F32)
        OA = sb("OA", [64, 8, 64], F32)
        rcp = sb("rcp", [64, 2, 2], F32)
        brcp = [Buf() for _ in range(2)]
        Sp = [ps("nS%d" % i, [128, 512], F32) for i in range(2)]
        Op = [ps("nO%d" % i, [64, 2, 65], F32) for i in range(2)]
        PR = ps("PR", [120, 127], F32)
        bKT, bQT, bV, bB, bcm, brT, bjp, bpr, bfl, bPR, brep = (Buf() for _ in range(11))
        bSb = [Buf() for _ in range(2)]
        bPb = [Buf() for _ in range(2)]
        bS = [Buf() for _ in range(2)]
        bO = [Buf() for _ in range(2)]
        bOst, bOB, bOA = Buf(), Buf(), Buf()
        S.dma("sp", rT[:], rpb.rearrange("h d m -> m (h d)"), writes=[brT], allow_slow_non_contiguous=True)
        S.dma("sp", jp[:], jpad_d, writes=[bjp])
        S.dma("sp", cm[0:64, :], colmask_d, writes=[bcm])
        S.dma("sp", cm[64:128, :], colmask_d, writes=[bcm])
        S.dma("sp", fl[:], flag_d, writes=[bfl])
        S.op("pe", lambda e: e.matmul(PR[:], lhsT=rT[:], rhs=jp[:], start=True, stop=True), reads=[brT, bjp], writes=[bPR])
        S.op("act", lambda e: e.copy(out=pr[:], in_=PR[:]), reads=[bPR], writes=[bpr])
        W = 127
        rep3 = rep.rearrange("(a p) w -> a p w", p=64)
        S.dma("sp", rep3, bc_ap(pr[:], [[0, 64], [1, W]]), reads=[bpr], writes=[brep])
        for h in range(8):
            for kr in range(2):
                src = bass.AP(rep.tensor, rep.offset + (h * 15 + kr) * 64 * W + 63, [[W - 1, 64], [64 * W, 14], [1, 64]])
                S.dma("sp", B[kr * 64:(kr + 1) * 64, h, :, :], src, reads=[brep], writes=[bB])
        S.op("dve", lambda e: e.tensor_tensor(out=B[:].rearrange("p h d c -> p (h d) c"), in0=B[:].rearrange("p h d c -> p (h d) c"),
                                              in1=bc_ap(cm[:], [[0, 112], [1, 64]]), op=ALU.add),
             reads=[bB, bcm], writes=[bB])
        for h in range(8):
            for r in range(2):
                S.op("dve", lambda e, h=h, r=r: e.tensor_copy(out=bc_ap(Bint[:, h, r * 256:(r + 1) * 256], [[64, 4], [1, 64]]),
                                                              in_=bc_ap(B[:, h, 3, :], [[2 * 64, 4], [1, 64]])),
                     reads=[bB], writes=[bB])
        nrow = T // 64

        def row(h, R, ws, dst, i):
            for b in range(4):
                r0 = ws + 2 * b
                S.op("pe", lambda e, b=b, r0=r0, R=R, i=i: e.matmul(Sp[i][:, b * 64:(b + 1) * 64],
                                                                     lhsT=KT[:, r0 * 64:r0 * 64 + 128],
                                                                     rhs=QT[:, R * 64:(R + 1) * 64], start=True, stop=True),
                     reads=[bKT, bQT], writes=[bS[i]])
            d0 = ws - R + 7
            bsl = bc_ap(B[:, h, d0, :], [[2 * 64, 4], [1, 64]])
            so = bc_ap(Sb[:, i, 0:256], [[64, 4], [1, 64]])
            si = bc_ap(Sp[i][:, 0:256], [[64, 4], [1, 64]])
            S.op("dve", lambda e, bsl=bsl, so=so, si=si: e.scalar_tensor_tensor(out=so, in0=si, scalar=0.125, in1=bsl,
                                                                                op0=ALU.mult, op1=ALU.add),
                 reads=[bS[i], bB], writes=[bSb[i]])
            S.op("act", lambda e, i=i: e.activation(out=Pb[:, i, 0:256], in_=Sb[:, i, 0:256], func=AF.Exp),
                 reads=[bSb[i]], writes=[bPb[i]])
            for b in range(4):
                r0 = ws + 2 * b
                S.op("pe", lambda e, b=b, r0=r0, i=i: e.matmul(Op[i][:, 0, :], lhsT=Pb[:, i, b * 64:(b + 1) * 64],
                                                               rhs=V[:, r0, :], start=(b == 0), stop=(b == 3)),
                     reads=[bPb[i], bV], writes=[bO[i]])
            S.op("dve", lambda e, i=i: e.reciprocal(out=rcp[:, i, 0:1], in_=Op[i][:, 0, 64:65]), reads=[bO[i]], writes=[brcp[i]])
            S.op("dve", lambda e, i=i, dst=dst[0]: e.tensor_scalar(out=dst, in0=Op[i][:, 0, 0:64], scalar1=rcp[:, i, 0:1],
                                                                   scalar2=None, op0=ALU.mult),
                 reads=[bO[i], brcp[i]], writes=[dst[1]])

        def rowpair(h, R, i):
            for r in range(2):
                for b in range(4):
                    r0 = R + r - 4 + 2 * b
                    S.op("pe", lambda e, b=b, r0=r0, r=r, i=i: e.matmul(Sp[i][:, r * 256 + b * 64:r * 256 + (b + 1) * 64],
                                                                         lhsT=KT[:, r0 * 64:r0 * 64 + 128],
                                                                         rhs=QT[:, (R + r) * 64:(R + r + 1) * 64], start=True, stop=True),
                         reads=[bKT, bQT], writes=[bS[i]])
            S.op("dve", lambda e: e.scalar_tensor_tensor(out=Sb[:, i, :], in0=Sp[i][:], scalar=0.125, in1=Bint[:, h, :],
                                                         op0=ALU.mult, op1=ALU.add),
                 reads=[bS[i], bB], writes=[bSb[i]])
            S.op("act", lambda e: e.activation(out=Pb[:, i, :], in_=Sb[:, i, :], func=AF.Exp), reads=[bSb[i]], writes=[bPb[i]])
            for r in range(2):
                for b in range(4):
                    r0 = R + r - 4 + 2 * b
                    S.op("pe", lambda e, b=b, r0=r0, r=r: e.matmul(Op[i][:, r, :], lhsT=Pb[:, i, r * 256 + b * 64:r * 256 + (b + 1) * 64],
                                                                   rhs=V[:, r0, :], start=(b == 0), stop=(b == 3)),
                         reads=[bPb[i], bV], writes=[bO[i]])
            S.op("dve", lambda e: e.reciprocal(out=rcp[:, i, :], in_=Op[i][:, :, 64]), reads=[bO[i]], writes=[brcp[i]])
            S.op("dve", lambda e: e.tensor_tensor(out=Ost[:, R:R + 2, :], in0=Op[i][:, :, 0:64],
                                                  in1=bc_ap(rcp[:, i, :], [[1, 2], [0, 64]]), op=ALU.mult),
                 reads=[bO[i], brcp[i]], writes=[bOst])

        n = 0
        for h in range(8):
            for c in range(4):
                S.dma("sp", KT[:, c * 4096:(c + 1) * 4096], nkT[h, :, c * 4096:(c + 1) * 4096], writes=[bKT])
                S.dma("sp", QT[:, c * 4096:(c + 1) * 4096], nqT[h, :, c * 4096:(c + 1) * 4096], writes=[bQT])
            vlo = bass.AP(nvA.tensor, nvA.offset + h * 65, [[8 * 65, 64], [64 * 8 * 65, 256], [1, 65]])
            vhi = bass.AP(nvA.tensor, nvA.offset + 64 * 8 * 65 + h * 65, [[8 * 65, 64], [64 * 8 * 65, 255], [1, 65]])
            S.dma("sp", V[0:64, :, :], vlo, writes=[bV])
            S.dma("sp", V[64:128, 0:255, :], vhi, writes=[bV])
            R = 0
            while R < nrow:
                if 4 <= R and R + 1 <= nrow - 5:
                    rowpair(h, R, n % 2)
                    R += 2
                else:
                    row(h, R, nat_ws(R, nrow), (Ost[:, R, :], bOst), n % 2)
                    R += 1
                n += 1
            for k, R in enumerate(range(124, 132)):
                ws_b = 120 if R < 128 else 128
                row(h, R, ws_b, (OB[:, k, :], bOB), n % 2)
                n += 1
            S.op("dve", lambda e: e.tensor_copy(out=OA[:], in_=Ost[:, 124:132, :]), reads=[bOst], writes=[bOA])
            S.op("dve", lambda e: e.tensor_tensor(out=OB[:], in0=OB[:], in1=OA[:], op=ALU.subtract),
                 reads=[bOB, bOA], writes=[bOB])
            S.op("dve", lambda e: e.scalar_tensor_tensor(out=Ost[:, 124:132, :], in0=OB[:], scalar=fl[0:64, 0:1], in1=OA[:],
                                                         op0=ALU.mult, op1=ALU.add),
                 reads=[bOB, bOA, bfl], writes=[bOst])
            dst = bass.AP(natO.tensor, natO.offset + 512 + h * 64, [[1024, 64], [64 * 1024, 256], [1, 64]])
            S.dma("sp", dst, Ost[:], reads=[bOst])
        S.emit()


def phase_proj(nc, S, C, h_in, h_out, w_out, g_out, tokM, ntok, feats, wmap=None, tok_cols=None):
    nfeat = 8 - ntok
    with contextlib.ExitStack() as st:
        sb, ps = mk(st, nc)
        Wo = sb("Wo", [128, 8, D], BF16)
        gout = sb("gout", [128, D], F32)
        X = sb("X", [128, 4, D], F32)
        fT = sb("fT", [128, 2, 8, 512], BF16)
        tM = sb("tM", [128, 2, 4, max(ntok, 1) * 128], BF16)
        tmp = sb("tmp", [128, 2, 512], F32)
        junk = sb("junk", [128, 512], BF16)
        ss = sb("ss", [128, 8], F32)
        TP = ps("TP", [128, 8, 128], BF16)
        Dp = [ps("D%d" % i, [128, 512], F32) for i in range(4)]
        bW, bg = Buf(), Buf()
        bX = [Buf() for _ in range(4)]
        bfT = [Buf() for _ in range(2)]
        bfT2 = [[Buf() for _ in range(4)] for _ in range(2)]
        btM = [Buf() for _ in range(2)]
        btmp = [Buf() for _ in range(2)]
        bjunk = Buf()
        bss = [Buf() for _ in range(4)]
        bTP = Buf()
        bD = [Buf() for _ in range(4)]
        S.dma("sp", gout[:], bcast_rows(g_out, 128), writes=[bg])
        load_weight_bf16(S, Wo, bW, w_out, 8)
        for t in range(T // 512):
            b = t % 2
            cols = slice(t * 512, (t + 1) * 512)
            c_at = ntok
            for (fap, nch) in (feats or []):
                S.dma("sp", fT[:, b, c_at:c_at + nch, :], fap[:, cols].rearrange("(c p) t -> p c t", p=128), writes=[bfT[b]])
                c_at += nch
            if ntok:
                tsrc = tokM[t * 512:(t + 1) * 512, :] if tok_cols is None else tokM[t * 512:(t + 1) * 512, tok_cols[0]:tok_cols[1]]
                S.dma("sp", tM[:, b, :, :], tsrc.rearrange("(s p) f -> p s f", p=128), writes=[btM[b]])
            for s in range(4):
                r0 = t * 512 + s * 128
                S.dma("sp", X[:, s, :], h_in[r0:r0 + 128, :], writes=[bX[s]])
                for c in range(ntok):
                    S.op("pe", lambda e, s=s, c=c, b=b: e.transpose(out=TP[:, c, :], in_=tM[:, b, s, c * 128:(c + 1) * 128],
                                                                    identity=C.ident[:]),
                         reads=[btM[b], C.bident], writes=[bTP])
                if ntok:
                    S.op("act", lambda e, s=s, b=b: e.copy(out=fT[:, b, 0:ntok, s * 128:(s + 1) * 128], in_=TP[:, 0:ntok, :]),
                         reads=[bTP], writes=[bfT2[b][s]])
            for s in range(4):
                dis = []
                for half in range(2):
                    di = (2 * s + half) % 4
                    dis.append(di)
                    for c in range(8):
                        rd = [bW, bfT2[b][s]] if c < ntok else [bW, bfT[b]]
                        wc = c if wmap is None else wmap[c]
                        S.op("pe", lambda e, di=di, c=c, s=s, half=half, b=b, wc=wc: e.matmul(
                            Dp[di][:], lhsT=fT[:, b, c, s * 128:(s + 1) * 128], rhs=Wo[:, wc, half * 512:(half + 1) * 512],
                            start=(c == 0), stop=(c == 7)), reads=rd, writes=[bD[di]])
                    S.op("act", lambda e, di=di, s=s, half=half: e.activation(
                        out=junk[:], in_=Dp[di][:], func=AF.Square, accum_out=ss[:, 2 * s + half:2 * s + half + 1]),
                        reads=[bD[di]], writes=[bjunk, bss[s]])
                c0 = 2 * s
                S.op("dve", lambda e, c0=c0: e.tensor_tensor(out=ss[:, c0:c0 + 1], in0=ss[:, c0:c0 + 1],
                                                             in1=ss[:, c0 + 1:c0 + 2], op=ALU.add),
                     reads=[bss[s]], writes=[bss[s]])
                rstd_inplace(S, ss[:, c0:c0 + 1], bss[s], D)
                for half in range(2):
                    di = dis[half]
                    S.op("dve", lambda e, di=di, c0=c0, half=half: e.scalar_tensor_tensor(
                        out=tmp[:, half, :], in0=Dp[di][:], scalar=ss[:, c0:c0 + 1],
                        in1=gout[:, half * 512:(half + 1) * 512], op0=ALU.mult, op1=ALU.mult),
                        reads=[bD[di], bss[s], bg], writes=[btmp[half]])
                    S.op("pool", lambda e, s=s, half=half: e.tensor_tensor(
                        out=X[:, s, half * 512:(half + 1) * 512], in0=X[:, s, half * 512:(half + 1) * 512],
                        in1=tmp[:, half, :], op=ALU.add),
                        reads=[bX[s], btmp[half]], writes=[bX[s]])
                r0 = t * 512 + s * 128
                S.dma("pool", h_out[r0:r0 + 128, :], X[:, s, :], reads=[bX[s]])
        S.emit()


def norm_tile(nc, S, C, t, h_in, X, bX, xn, bxn, mT, bmT, TPm, bTPm, junk, bjunk, ss, bss, gin, bg):
    tb = t % 2
    for s in range(4):
        r0 = t * 512 + s * 128
        S.dma("sp", X[:, tb, s, :], h_in[r0:r0 + 128, :], writes=[bX[tb][s]])
        S.op("act", lambda e, s=s: e.activation(out=junk[:], in_=X[:, tb, s, :], func=AF.Square, accum_out=ss[:, tb, s:s + 1]),
             reads=[bX[tb][s]], writes=[bjunk, bss[tb]])
    rstd_inplace(S, ss[:, tb, 0:4], bss[tb], D)
    for s in range(4):
        S.op("dve", lambda e, s=s: e.scalar_tensor_tensor(out=xn[:, s % 2, :], in0=X[:, tb, s, :], scalar=ss[:, tb, s:s + 1],
                                                          in1=gin[:], op0=ALU.mult, op1=ALU.mult),
             reads=[bX[tb][s], bss[tb], bg], writes=[bxn[s % 2]])
        for kc in range(8):
            S.op("pe", lambda e, s=s, kc=kc: e.transpose(out=TPm[:, kc, :], in_=xn[:, s % 2, kc * 128:(kc + 1) * 128],
                                                         identity=C.ident[:]),
                 reads=[bxn[s % 2], C.bident], writes=[bTPm])
        S.op("act", lambda e, s=s: e.copy(out=mT[:, tb, :, s * 128:(s + 1) * 128], in_=TPm[:]), reads=[bTPm], writes=[bmT[tb][s]])


def phase_od_in(nc, S, C, h_in, g2, w_in, sel_d, uT, Ug):
    with contextlib.ExitStack() as st:
        sb, ps = mk(st, nc)
        Wod = sb("Wod", [128, 8, 1536], BF16)
        gin = sb("gin", [128, D], F32)
        X = sb("X", [128, 2, 4, D], F32)
        xn = sb("xn", [128, 2, D], BF16)
        mT = sb("mT", [128, 2, 8, 512], BF16)
        junk = sb("junk", [128, D], BF16)
        ss = sb("ss", [128, 2, 8], F32)
        sel = sb("sel", [128, 64, 128], BF16)
        suT = sb("suT", [128, 2, 512], BF16)
        Ust = sb("Ust", [128, 32, 256], BF16)
        ust = sb("ust", [128, 2, 512], BF16)
        sgm = sb("sgm", [128, 2, 512], F32)
        TPm = ps("TPm", [128, 8, 128], BF16)
        CA = [ps("CA%d" % i, [128, 512], F32) for i in range(2)]
        CG = [ps("CG%d" % i, [128, 512], F32) for i in range(2)]
        SU = ps("SU", [128, 512], F32)
        RG = [ps("RG%d" % i, [128, 8, 64], F32) for i in range(2)]
        bW, bg, bsel = Buf(), Buf(), Buf()
        bX = [[Buf() for _ in range(4)] for _ in range(2)]
        bxn = [Buf() for _ in range(2)]
        bmT = [[Buf() for _ in range(4)] for _ in range(2)]
        bjunk, bTPm, bSU, bUst = Buf(), Buf(), Buf(), Buf()
        bss = [Buf() for _ in range(2)]
        bCA = [Buf() for _ in range(2)]
        bCG = [Buf() for _ in range(2)]
        bRG = [Buf() for _ in range(2)]
        bsuT = [Buf() for _ in range(2)]
        bust = [Buf() for _ in range(2)]
        bsgm = [Buf() for _ in range(2)]
        S.dma("sp", gin[:], bcast_rows(g2, 128), writes=[bg])
        S.dma("sp", sel[:], sel_d, writes=[bsel])
        load_weight_bf16(S, Wod, bW, w_in, 8)
        nrg = 0
        for t in range(T // 512):
            norm_tile(nc, S, C, t, h_in, X, bX, xn, bxn, mT, bmT, TPm, bTPm, junk, bjunk, ss, bss, gin, bg)
            cols = slice(t * 512, (t + 1) * 512)
            tb = t % 2
            for cc in range(4):
                b = cc % 2
                for kc in range(8):
                    S.op("pe", lambda e, kc=kc, cc=cc, b=b, tb=tb: e.matmul(CA[b][:], lhsT=Wod[:, kc, cc * 128:(cc + 1) * 128],
                                                                     rhs=mT[:, tb, kc, :], start=(kc == 0), stop=(kc == 7)),
                         reads=bmT[tb] + [bW], writes=[bCA[b]])
                for kc in range(8):
                    S.op("pe", lambda e, kc=kc, cc=cc, b=b, tb=tb: e.matmul(CG[b][:], lhsT=Wod[:, kc, 512 + cc * 128:640 + cc * 128],
                                                                     rhs=mT[:, tb, kc, :], start=(kc == 0), stop=(kc == 7)),
                         reads=bmT[tb] + [bW], writes=[bCG[b]])
                S.op("act", lambda e, b=b: e.activation(out=sgm[:, b, :], in_=CG[b][:], func=AF.Sigmoid),
                     reads=[bCG[b]], writes=[bsgm[b]])
                S.op("dve", lambda e, b=b: e.tensor_tensor(out=ust[:, b, :], in0=CA[b][:], in1=sgm[:, b, :], op=ALU.mult),
                     reads=[bCA[b], bsgm[b]], writes=[bust[b]])
                S.dma("pool", uT[cc * 128:(cc + 1) * 128, cols], ust[:, b, :], reads=[bust[b]])
            for cc in range(4):
                b = cc % 2
                for kc in range(8):
                    S.op("pe", lambda e, kc=kc, cc=cc, tb=tb: e.matmul(SU[:], lhsT=Wod[:, kc, 1024 + cc * 128:1152 + cc * 128],
                                                                rhs=mT[:, tb, kc, :], start=(kc == 0), stop=(kc == 7)),
                         reads=bmT[tb] + [bW], writes=[bSU])
                S.op("act", lambda e, b=b: e.copy(out=bc_ap(suT[:, b, :], [[64, 8], [1, 64]]), in_=bc_ap(SU[:], [[1, 8], [8, 64]])),
                     reads=[bSU], writes=[bsuT[b]])
                rb = nrg % 2
                nrg += 1
                base = suT[:, b, :]
                for g1 in range(8):
                    for j in range(8):
                        rhs = bass.AP(base.tensor, base.offset + j * 64, [list(base.ap[0]), [1, 64]])
                        S.op("pe", lambda e, g1=g1, j=j, rhs=rhs, rb=rb: e.matmul(RG[rb][:, g1, :], lhsT=sel[:, g1 * 8 + j, :],
                                                                                  rhs=rhs, start=(j == 0), stop=(j == 7)),
                             reads=[bsuT[b], bsel], writes=[bRG[rb]])
                k0 = (t % 4) * 64
                S.op("dve", lambda e, cc=cc, rb=rb, k0=k0: e.tensor_copy(out=Ust[:, cc * 8:(cc + 1) * 8, k0:k0 + 64], in_=RG[rb][:]),
                     reads=[bRG[rb]], writes=[bUst])
            if t % 4 == 3:
                kk = (t // 4) * 256
                S.dma("pool", Ug[:, :, kk:kk + 256].rearrange("g p k -> p g k"), Ust[:], reads=[bUst])
        S.emit()


def conv_gen(nc, S, C, sb, ps, uT, dw_w, dw_b, ln_g, ln_b, keep_d, ident32_d, convT, acc_bufs=2):
    if True:
        Uw = sb("Uw", [128, 3, 542], BF16)
        Wdg = sb("Wdg", [128, 4, 31, 128], BF16)
        acc = sb("acc", [128, acc_bufs, 4, 512], F32)
        sq = sb("sq", [128, 4, 512], F32)
        wt = sb("wt", [128, 4, 31], F32)
        bi = sb("bi", [128, 4], F32)
        lg = sb("lg", [128, 4], F32)
        lb = sb("lb", [128, 4], F32)
        keep = sb("keep", [128, 1], F32)
        id32 = sb("id32", [128, 128], F32)
        onesc = sb("onesc", [128, 128], F32)
        m1s = sb("m1s", [128, 512], F32)
        rs = sb("rs", [128, 512], F32)
        tt_ = sb("tt", [128, 512], F32)
        yn = sb("yn", [128, 2, 512], F32)
        cst = sb("cst", [128, 2, 512], BF16)
        CV = [ps("CV%d" % i, [128, 512], F32) for i in range(2)]
        M1 = ps("M1", [128, 512], F32)
        M2 = ps("M2", [128, 512], F32)
        bCV = [Buf() for _ in range(2)]
        bUw = [Buf() for _ in range(3)]
        bacc = [[Buf() for _ in range(4)] for _ in range(acc_bufs)]
        bsq = [Buf() for _ in range(4)]
        bc = Buf()
        bM1, bM2, bm1s, brs, btt = Buf(), Buf(), Buf(), Buf(), Buf()
        byn = [Buf() for _ in range(2)]
        bcst = [Buf() for _ in range(2)]
        for cc in range(4):
            S.dma("sp", wt[:, cc, :], dw_w[:, cc * 128:(cc + 1) * 128].rearrange("k p -> p k"), writes=[bc],
                  allow_slow_non_contiguous=True)
        S.dma("sp", bi[:], dw_b.rearrange("(c p) -> p c", p=128), writes=[bc], allow_slow_non_contiguous=True)
        S.dma("sp", lg[:], ln_g.rearrange("(c p) -> p c", p=128), writes=[bc], allow_slow_non_contiguous=True)
        S.dma("sp", lb[:], ln_b.rearrange("(c p) -> p c", p=128), writes=[bc], allow_slow_non_contiguous=True)
        S.dma("sp", keep[:], keep_d, writes=[bc])
        S.dma("sp", id32[:], ident32_d, writes=[bc])
        S.op("pool", lambda e: e.memset(onesc[:], 1.0 / 512), writes=[bc])
        for cc in range(4):
            for k in range(31):
                S.op("dve", lambda e, cc=cc, k=k: e.tensor_scalar(out=Wdg[:, cc, k, :], in0=id32[:], scalar1=wt[:, cc, k:k + 1],
                                                                  scalar2=None, op0=ALU.mult), reads=[bc], writes=[bc])
        n = 0
        ny = 0
        NT = T // 512
        for t in range(NT):
            ab = t % acc_bufs
            for cc in range(4):
                b = n % 3
                cb = n % 2
                n += 1
                lo = t * 512 - 15
                hi = t * 512 + 527
                o0 = 0
                if t == 0:
                    S.op("pool", lambda e, b=b: e.memset(Uw[:, b, 0:15], 0.0), writes=[bUw[b]])
                    o0, lo = 15, 0
                o1 = 542
                if t == NT - 1:
                    S.op("pool", lambda e, b=b: e.memset(Uw[:, b, 527:542], 0.0), writes=[bUw[b]])
                    o1, hi = 527, T
                S.dma("sp", Uw[:, b, o0:o1], uT[cc * 128:(cc + 1) * 128, lo:hi], writes=[bUw[b]])
                if t == NT // 2 - 1:
                    S.op("pool", lambda e, b=b: e.tensor_scalar(out=Uw[:, b, 527:542], in0=Uw[:, b, 527:542], scalar1=keep[:, 0:1],
                                                                scalar2=None, op0=ALU.mult), reads=[bUw[b], bc], writes=[bUw[b]])
                if t == NT // 2:
                    S.op("pool", lambda e, b=b: e.tensor_scalar(out=Uw[:, b, 0:15], in0=Uw[:, b, 0:15], scalar1=keep[:, 0:1],
                                                                scalar2=None, op0=ALU.mult), reads=[bUw[b], bc], writes=[bUw[b]])
                for k in range(31):
                    S.op("pe", lambda e, b=b, cc=cc, k=k, cb=cb: e.matmul(CV[cb][:], lhsT=Wdg[:, cc, k, :], rhs=Uw[:, b, k:k + 512],
                                                                          start=(k == 0), stop=(k == 30)),
                         reads=[bUw[b], bc], writes=[bCV[cb]])
                S.op("act", lambda e, cc=cc, cb=cb, ab=ab: e.activation(out=acc[:, ab, cc, :], in_=CV[cb][:], func=AF.Identity,
                                                                        bias=bi[:, cc:cc + 1]),
                     reads=[bCV[cb], bc], writes=[bacc[ab][cc]])
                S.op("act", lambda e, cc=cc, ab=ab: e.activation(out=sq[:, cc, :], in_=acc[:, ab, cc, :], func=AF.Square),
                     reads=[bacc[ab][cc]], writes=[bsq[cc]])
            for cc in range(4):
                S.op("pe", lambda e, cc=cc, ab=ab: e.matmul(M1[:], lhsT=onesc[:], rhs=acc[:, ab, cc, :], start=(cc == 0), stop=(cc == 3)),
                     reads=[bacc[ab][cc], bc], writes=[bM1])
            for cc in range(4):
                S.op("pe", lambda e, cc=cc: e.matmul(M2[:], lhsT=onesc[:], rhs=sq[:, cc, :], start=(cc == 0), stop=(cc == 3)),
                     reads=[bsq[cc], bc], writes=[bM2])
            S.op("act", lambda e: e.copy(out=m1s[:], in_=M1[:]), reads=[bM1], writes=[bm1s])
            S.op("dve", lambda e: e.tensor_tensor(out=tt_[:], in0=m1s[:], in1=m1s[:], op=ALU.mult), reads=[bm1s], writes=[btt])
            S.op("dve", lambda e: e.tensor_tensor(out=rs[:], in0=M2[:], in1=tt_[:], op=ALU.subtract), reads=[bM2, btt], writes=[brs])
            S.op("dve", lambda e: e.tensor_scalar(out=rs[:], in0=rs[:], scalar1=EPS, scalar2=None, op0=ALU.add), reads=[brs], writes=[brs])
            S.op("act", lambda e: e.activation(out=rs[:], in_=rs[:], func=AF.Sqrt), reads=[brs], writes=[brs])
            S.op("dve", lambda e: e.reciprocal(out=rs[:], in_=rs[:]), reads=[brs], writes=[brs])
            for cc in range(4):
                yb = ny % 2
                ny += 1
                S.op("dve", lambda e, cc=cc, yb=yb, ab=ab: e.tensor_tensor(out=yn[:, yb, :], in0=acc[:, ab, cc, :], in1=m1s[:], op=ALU.subtract),
                     reads=[bacc[ab][cc], bm1s], writes=[byn[yb]])
                S.op("dve", lambda e, yb=yb: e.tensor_tensor(out=yn[:, yb, :], in0=yn[:, yb, :], in1=rs[:], op=ALU.mult),
                     reads=[byn[yb], brs], writes=[byn[yb]])
                S.op("act", lambda e, cc=cc, yb=yb: e.activation(out=cst[:, yb, :], in_=yn[:, yb, :], func=AF.Silu,
                                                                 bias=lb[:, cc:cc + 1], scale=lg[:, cc:cc + 1]),
                     reads=[byn[yb], bc], writes=[bcst[yb]])
                S.dma("pool", convT[cc * 128:(cc + 1) * 128, t * 512:(t + 1) * 512], cst[:, yb, :], reads=[bcst[yb]])
            yield t


def phase_conv(nc, S, C, *args):
    with contextlib.ExitStack() as st:
        sb, ps = mk(st, nc)
        for _ in conv_gen(nc, S, C, sb, ps, *args):
            pass
        S.emit()


class El:
    def __init__(self, S, buf, eng="dve"):
        self.S, self.b, self.eng = S, buf, eng

    def tt(self, out, a, b, op, extra=()):
        self.S.op(self.eng, lambda e: e.tensor_tensor(out=out, in0=a, in1=b, op=op), reads=[self.b] + list(extra), writes=[self.b])

    def ts(self, out, a, s1, op0, s2=None, op1=None, extra=()):
        if op1 is None:
            self.S.op(self.eng, lambda e: e.tensor_scalar(out=out, in0=a, scalar1=s1, scalar2=None, op0=op0),
                      reads=[self.b] + list(extra), writes=[self.b])
        else:
            self.S.op(self.eng, lambda e: e.tensor_scalar(out=out, in0=a, scalar1=s1, scalar2=s2, op0=op0, op1=op1),
                      reads=[self.b] + list(extra), writes=[self.b])

    def stt(self, out, a, sc, b, op0, op1, extra=()):
        self.S.op(self.eng, lambda e: e.scalar_tensor_tensor(out=out, in0=a, scalar=sc, in1=b, op0=op0, op1=op1),
                  reads=[self.b] + list(extra), writes=[self.b])

    def cp(self, out, a, extra=()):
        self.S.op(self.eng, lambda e: e.tensor_copy(out=out, in_=a), reads=[self.b] + list(extra), writes=[self.b])

    def ms(self, out, val):
        self.S.op(self.eng, lambda e: e.memset(out, val), writes=[self.b])

    def cmul(self, o_r, o_i, ar, ai, br, bi, t1, t2):
        self.tt(t1, ar, br, ALU.mult)
        self.tt(t2, ai, bi, ALU.mult)
        self.tt(o_r, t1, t2, ALU.subtract)
        self.tt(t1, ar, bi, ALU.mult)
        self.tt(t2, ai, br, ALU.mult)
        self.tt(o_i, t1, t2, ALU.add)

    def exp_taylor(self, out, x, deg):
        self.ts(out, x, 1.0 / deg, ALU.mult, 1.0, ALU.add)
        for n in range(deg - 1, 0, -1):
            self.tt(out, out, x, ALU.mult)
            self.ts(out, out, 1.0 / n, ALU.mult, 1.0, ALU.add)


def phase_s5_prep(nc, S, C, I, Pp):
    lam_re, lam_im, log_step = I["s5_lambda_re"][0], I["s5_lambda_im"][0], I["s5_log_step"][0]
    b_re, b_im, c_re, c_im, d_skip = I["s5_b_re"][0], I["s5_b_im"][0], I["s5_c_re"][0], I["s5_c_im"][0], I["s5_d"][0]
    with contextlib.ExitStack() as st:
        sb, ps = mk(st, nc)
        A = sb("A", [128, 48, 32], F32)
        LPr = sb("LPr", [128, 16, 32], F32)
        LPi = sb("LPi", [128, 16, 32], F32)
        Bre = sb("Bre", [128, 32, 16], F32)
        Bim = sb("Bim", [128, 32, 16], F32)
        Bbr = sb("Bbr", [128, 32, 16], F32)
        Bbi = sb("Bbi", [128, 32, 16], F32)
        Cre = sb("Cre", [128, 32, 16], F32)
        Cim = sb("Cim", [128, 32, 16], F32)
        MBr = sb("MBr", [128, 32, 8, 16], F32)
        MBi = sb("MBi", [128, 32, 8, 16], F32)
        MKr = sb("MKr", [128, 32, 8, 16], F32)
        MKi = sb("MKi", [128, 32, 8, 16], F32)
        t1 = sb("t1", [128, 16, 16], F32)
        t2 = sb("t2", [128, 16, 16], F32)
        mf = sb("mf", [128, 128], F32)
        mbk = sb("mbk", [128, 128], F32)
        id32 = sb("id32", [128, 128], F32)
        dall = sb("dall", [128, 32], F32)
        Kt = sb("Kt", [128, 128], F32)
        Kt2 = sb("Kt2", [128, 128], F32)
        KP = [ps("KP%d" % i, [128, 128], F32) for i in range(2)]
        TPp = [ps("TPp%d" % i, [128, 4, 128], F32) for i in range(2)]
        bp = Buf()
        bKP = [Buf() for _ in range(2)]
        bTPp = [Buf() for _ in range(2)]
        bKt = Buf()
        E = El(S, bp)
        nslot = [0]
        names = {}

        def v(name):
            if name not in names:
                names[name] = nslot[0]
                nslot[0] += 1
                assert nslot[0] <= 48
            return A[:, names[name], :]
        for a in range(2):
            ps_ = slice(a * 64, (a + 1) * 64)
            for dd in range(2):
                for nm, src in (("lr", lam_re), ("li", lam_im)):
                    dst = v(nm)[ps_, dd * 16:(dd + 1) * 16]
                    sap = bass.AP(src.tensor, src.offset + dd * 32 * 64 + a * 64, [[1, 64], [2 * 64, 16]])
                    S.dma("sp", dst, sap, writes=[bp], allow_slow_non_contiguous=True)
                dst = v("ls")[ps_, dd * 16:(dd + 1) * 16]
                sap = bass.AP(log_step.tensor, log_step.offset + dd * 32 + a, [[0, 64], [2, 16]])
                S.dma("sp", dst, sap, writes=[bp], allow_slow_non_contiguous=True)
            for dstT, src in ((Bre, b_re), (Bim, b_im)):
                for dd in range(2):
                    dst4 = dstT[ps_, dd * 16:(dd + 1) * 16, :]
                    sap = bass.AP(src.tensor, src.offset + dd * 32 * 64 * 16 + a * 64 * 16, [[16, 64], [2 * 64 * 16, 16], [1, 16]])
                    S.dma("sp", dst4, sap, writes=[bp], allow_slow_non_contiguous=True)
            for dstT, src in ((Cre, c_re), (Cim, c_im)):
                for dd in range(2):
                    for gp in range(16):
                        dst4 = dstT[ps_, dd * 16 + gp, :]
                        sap = bass.AP(src.tensor, src.offset + dd * 32 * 16 * 64 + gp * 2 * 16 * 64 + a * 16 * 64, [[1, 64], [64, 16]])
                        S.dma("sp", dst4, sap, writes=[bp], allow_slow_non_contiguous=True)
        S.dma("sp", mf[:], I["maskf"], writes=[bp])
        S.dma("sp", mbk[:], I["maskb"], writes=[bp])
        S.dma("sp", id32[:], I["ident32"], writes=[bp])
        S.dma("sp", Pp.keep[:], I["segkeep"], writes=[Pp.bk])
        dsrc = d_skip.rearrange("(g c) -> c g", c=16)
        for j in range(8):
            S.dma("sp", dall[j * 16:(j + 1) * 16, :], dsrc, writes=[bp], allow_slow_non_contiguous=True)
        if int(os.environ.get('S5_STOP', '9')) == 1:
            S.emit()
            return
        lr, li, ls = v("lr"), v("li"), v("ls")
        x, dt = v("x"), v("dt")
        E.ts(x, ls, 0.125, ALU.mult)
        E.exp_taylor(dt, x, 11)
        for _ in range(3):
            E.tt(dt, dt, dt, ALU.mult)
        a_, th = v("a"), v("th")
        E.tt(a_, lr, dt, ALU.mult)
        E.tt(th, li, dt, ALU.mult)
        ea, eam, na = v("ea"), v("eam"), v("na")
        E.exp_taylor(ea, a_, 8)
        E.ts(na, a_, -1.0, ALU.mult)
        E.exp_taylor(eam, na, 8)
        x2, p, sn, cs = v("x2"), v("p"), v("sn"), v("cs")
        E.ts(x, th, 1.0 / 32, ALU.mult)
        E.tt(x2, x, x, ALU.mult)
        E.ts(p, x2, -1.0 / 110, ALU.mult, 1.0, ALU.add)
        for m in (72.0, 42.0, 20.0, 6.0):
            E.tt(p, p, x2, ALU.mult)
            E.ts(p, p, -1.0 / m, ALU.mult, 1.0, ALU.add)
        E.tt(sn, p, x, ALU.mult)
        E.ts(p, x2, -1.0 / 132, ALU.mult, 1.0, ALU.add)
        for m in (90.0, 56.0, 30.0, 12.0, 2.0):
            E.tt(p, p, x2, ALU.mult)
            E.ts(p, p, -1.0 / m, ALU.mult, 1.0, ALU.add)
        E.cp(cs, p)
        cc_, ss_, q = v("cc"), v("ss2"), v("q")
        for _ in range(5):
            E.tt(cc_, cs, cs, ALU.mult)
            E.tt(ss_, sn, sn, ALU.mult)
            E.tt(q, cs, sn, ALU.mult)
            E.tt(cs, cc_, ss_, ALU.subtract)
            E.ts(sn, q, 2.0, ALU.mult)
        E.tt(cc_, cs, cs, ALU.mult)
        E.tt(ss_, sn, sn, ALU.mult)
        E.tt(q, cc_, ss_, ALU.add)
        E.ts(q, q, -1.0, ALU.add)
        E.ts(p, q, 0.375, ALU.mult, -0.5, ALU.add)
        E.tt(p, p, q, ALU.mult)
        E.ts(p, p, 1.0, ALU.add)
        E.tt(cs, cs, p, ALU.mult)
        E.tt(sn, sn, p, ALU.mult)
        lbr, lbi, lir, lii = v("lbr"), v("lbi"), v("lir"), v("lii")
        E.tt(lbr, ea, cs, ALU.mult)
        E.tt(lbi, ea, sn, ALU.mult)
        E.tt(lir, eam, cs, ALU.mult)
        E.tt(lii, eam, sn, ALU.mult)
        E.ts(lii, lii, -1.0, ALU.mult)
        den, m1, nr, ni, cr, ci, u1 = v("den"), v("m1"), v("nr"), v("ni"), v("cr"), v("ci"), v("u1")
        E.tt(den, lr, lr, ALU.mult)
        E.tt(u1, li, li, ALU.mult)
        E.tt(den, den, u1, ALU.add)
        S.op("dve", lambda e: e.reciprocal(out=den, in_=den), reads=[bp], writes=[bp])
        E.ts(m1, lbr, -1.0, ALU.add)
        E.tt(nr, m1, lr, ALU.mult)
        E.tt(u1, lbi, li, ALU.mult)
        E.tt(nr, nr, u1, ALU.add)
        E.tt(ni, lbi, lr, ALU.mult)
        E.tt(u1, m1, li, ALU.mult)
        E.tt(ni, ni, u1, ALU.subtract)
        E.tt(cr, nr, den, ALU.mult)
        E.tt(ci, ni, den, ALU.mult)
        E.ms(LPr[:, 7, :], 1.0)
        E.ms(LPi[:, 7, :], 0.0)
        w1, w2 = v("w1"), v("w2")
        for n in range(1, 9):
            E.cmul(LPr[:, 7 + n, :], LPi[:, 7 + n, :], LPr[:, 6 + n, :], LPi[:, 6 + n, :], lbr, lbi, w1, w2)
        for n in range(1, 8):
            E.cmul(LPr[:, 7 - n, :], LPi[:, 7 - n, :], LPr[:, 8 - n, :], LPi[:, 8 - n, :], lir, lii, w1, w2)
        E.cp(Pp.LamR[:, 0, :], LPr[:, 15, :], extra=[Pp.bL])
        E.cp(Pp.LamI[:, 0, :], LPi[:, 15, :], extra=[Pp.bL])
        for l in range(1, 10):
            E.cmul(Pp.LamR[:, l, :], Pp.LamI[:, l, :], Pp.LamR[:, l - 1, :], Pp.LamI[:, l - 1, :],
                   Pp.LamR[:, l - 1, :], Pp.LamI[:, l - 1, :], w1, w2)
        S.op("dve", lambda e: e.tensor_scalar(out=Pp.LamN[:], in0=Pp.LamI[:], scalar1=-1.0, scalar2=None, op0=ALU.mult),
             reads=[bp], writes=[bp, Pp.bL])
        if int(os.environ.get('S5_STOP', '9')) == 2:
            S.emit()
            return
        crb = bc_ap(cr, [[1, 32], [0, 16]])
        cib = bc_ap(ci, [[1, 32], [0, 16]])
        T1 = sb("T1", [128, 32, 16], F32)
        T2 = sb("T2", [128, 32, 16], F32)
        E.tt(T1[:], crb, Bre[:], ALU.mult)
        E.tt(T2[:], cib, Bim[:], ALU.mult)
        E.tt(Bbr[:], T1[:], T2[:], ALU.subtract)
        E.tt(T1[:], crb, Bim[:], ALU.mult)
        E.tt(T2[:], cib, Bre[:], ALU.mult)
        E.tt(Bbi[:], T1[:], T2[:], ALU.add)
        for d in range(2):
            dsl = slice(d * 16, (d + 1) * 16)
            for j in range(8):
                def lp(n):
                    return (bc_ap(LPr[:, n + 7, dsl], [[1, 16], [0, 16]]), bc_ap(LPi[:, n + 7, dsl], [[1, 16], [0, 16]]))
                pr_, pi_ = lp(7 - j if d == 0 else j)
                E.cmul(MBr[:, dsl, j, :], MBi[:, dsl, j, :], pr_, pi_, Bbr[:, dsl, :], Bbi[:, dsl, :], t1[:], t2[:])
                pr_, pi_ = lp(j + 1 if d == 0 else 8 - j)
                E.tt(t1[:], pr_, Cre[:, dsl, :], ALU.mult)
                E.tt(t2[:], pi_, Cim[:, dsl, :], ALU.mult)
                E.tt(Pp.MCr[:, dsl, j, :], t1[:], t2[:], ALU.subtract, extra=[Pp.bM])
                E.tt(t1[:], pr_, Cim[:, dsl, :], ALU.mult)
                E.tt(t2[:], pi_, Cre[:, dsl, :], ALU.mult)
                E.stt(Pp.MCn[:, dsl, j, :], t1[:], -1.0, t2[:], ALU.mult, ALU.subtract, extra=[Pp.bM])
                pr_, pi_ = lp(j - 7 if d == 0 else -j)
                E.tt(t1[:], pr_, Cre[:, dsl, :], ALU.mult)
                E.tt(t2[:], pi_, Cim[:, dsl, :], ALU.mult)
                E.tt(MKr[:, dsl, j, :], t1[:], t2[:], ALU.subtract)
                E.tt(t1[:], pr_, Cim[:, dsl, :], ALU.mult)
                E.tt(t2[:], pi_, Cre[:, dsl, :], ALU.mult)
                E.stt(MKi[:, dsl, j, :], t1[:], -1.0, t2[:], ALU.mult, ALU.subtract)
        if int(os.environ.get('S5_STOP', '9')) == 3:
            S.emit()
            return
        for g in range(32):
            gp, a = g // 2, g % 2
            ps_ = slice(a * 64, (a + 1) * 64)
            for d in range(2):
                q_ = d * 16 + gp
                S.op("pe", lambda e, d=d, q_=q_, ps_=ps_: e.matmul(
                    KP[d][:], lhsT=MBr[ps_, q_, :, :].rearrange("p j c -> p (j c)"),
                    rhs=MKr[ps_, q_, :, :].rearrange("p j c -> p (j c)"), start=True, stop=False),
                    reads=[bp], writes=[bKP[d]])
                S.op("pe", lambda e, d=d, q_=q_, ps_=ps_: e.matmul(
                    KP[d][:], lhsT=MBi[ps_, q_, :, :].rearrange("p j c -> p (j c)"),
                    rhs=MKi[ps_, q_, :, :].rearrange("p j c -> p (j c)"), start=False, stop=True),
                    reads=[bp], writes=[bKP[d]])
            S.op("dve", lambda e: e.tensor_tensor(out=Kt[:], in0=KP[0][:], in1=mf[:], op=ALU.mult), reads=[bKP[0], bp], writes=[bKt])
            S.op("dve", lambda e: e.tensor_tensor(out=Kt2[:], in0=KP[1][:], in1=mbk[:], op=ALU.mult), reads=[bKP[1], bp], writes=[bKt])
            S.op("dve", lambda e: e.tensor_tensor(out=Kt[:], in0=Kt[:], in1=Kt2[:], op=ALU.add), reads=[bKt], writes=[bKt])
            S.op("dve", lambda e, g=g: e.scalar_tensor_tensor(out=Pp.Kin[:, g, :], in0=id32[:], scalar=dall[:, g:g + 1], in1=Kt[:],
                                                              op0=ALU.mult, op1=ALU.add), reads=[bKt, bp], writes=[Pp.bM])
        if int(os.environ.get('S5_STOP', '9')) == 4:
            S.emit()
            return
        nb = 0
        for q_ in range(32):
            tb = nb % 2
            for ri, M in enumerate((MBr, MBi)):
                k = (q_ % 2) * 2 + ri
                S.op("pe", lambda e, M=M, q_=q_, k=k, tb=tb: e.transpose(
                    out=TPp[tb][:, k, :], in_=M[:, q_, :, :].rearrange("p j c -> p (j c)"), identity=id32[:]),
                    reads=[bp], writes=[bTPp[tb]])
            if q_ % 2 == 1:
                s0 = (q_ - 1) * 2
                S.op("act", lambda e, s0=s0, tb=tb: e.copy(out=Pp.MBT[:, s0:s0 + 4, :], in_=TPp[tb][:]),
                     reads=[bTPp[tb]], writes=[Pp.bM])
                nb += 1
        S.emit()


def phase_s5_main(nc, S, C, Pp, Ug, Yg, bg_factory=None):
    with contextlib.ExitStack() as st:
        sb, ps = mk(st, nc)
        bg = bg_factory(sb, ps) if bg_factory is not None else None
        nbuf = 1 if bg is not None else 2
        U = sb("U", [128, nbuf, 2, 2048], BF16)
        AB = sb("AB", [128, 2, 2, 2048], F32)
        Xb = sb("Xb", [128, 2, 2, 2049], BF16)
        Yst = sb("Yst", [128, 2, 512], BF16)
        tv = sb("tv", [128, 4], F32)
        PSs = [ps("PSs%d" % i, [128, 512], F32) for i in range(2 * nbuf)]
        PY = [ps("PY%d" % i, [128, 512], F32) for i in range(2)]
        bU = [Buf() for _ in range(2)]
        bA = [[Buf() for _ in range(2)] for _ in range(2)]
        bXb = [[Buf() for _ in range(2)] for _ in range(2)]
        bYst = [Buf() for _ in range(2)]
        btv = Buf()
        bPS = [Buf() for _ in range(2 * nbuf)]
        bPY = [Buf() for _ in range(2)]
        S.op("pool", lambda e: e.memset(Xb[:, 0, :, 0:1], 0.0), writes=[bXb[0][0], bXb[0][1]])
        S.op("pool", lambda e: e.memset(Xb[:, 1, :, 2048:2049], 0.0), writes=[bXb[1][0], bXb[1][1]])
        nps = 0
        npy = 0
        H = 1024

        def hs_scan(d, col, lo):
            base = [AB[:, 0, ri, lo:lo + H] for ri in range(2)]

            def view(ri, off, step, cnt):
                a = base[ri]
                return bass.AP(a.tensor, a.offset + off, [list(a.ap[0]), [step, cnt]])

            def update(l, doff, soff, step, cnt):
                ar = Pp.LamR[:, l, col:col + 1]
                ai = Pp.LamI[:, l, col:col + 1]
                an = Pp.LamN[:, l, col:col + 1]
                dr_, di_ = view(0, doff, step, cnt), view(1, doff, step, cnt)
                sr_, si_ = view(0, soff, step, cnt), view(1, soff, step, cnt)
                a0 = base[0]
                d2 = bass.AP(a0.tensor, a0.offset + doff, [list(a0.ap[0]), [2048, 2], [step, cnt]])
                s2 = bass.AP(a0.tensor, a0.offset + soff, [list(a0.ap[0]), [2048, 2], [step, cnt]])
                rd = [bA[0][0], bA[0][1], Pp.bL]
                S.op("dve", lambda e: e.scalar_tensor_tensor(out=d2, in0=s2, scalar=ar, in1=d2, op0=ALU.mult, op1=ALU.add),
                     reads=rd, writes=[bA[0][0], bA[0][1]])
                S.op("dve", lambda e: e.scalar_tensor_tensor(out=dr_, in0=si_, scalar=an, in1=dr_, op0=ALU.mult, op1=ALU.add),
                     reads=rd, writes=[bA[0][0]])
                S.op("dve", lambda e: e.scalar_tensor_tensor(out=di_, in0=sr_, scalar=ai, in1=di_, op0=ALU.mult, op1=ALU.add),
                     reads=[bA[0][0], bA[0][1], Pp.bL], writes=[bA[0][1]])
            for l in range(10):
                st2 = 1 << (l + 1)
                s_ = 1 << l
                cnt = H // st2
                if d == 0:
                    update(l, st2 - 1, s_ - 1, st2, cnt)
                else:
                    update(l, 0, s_, st2, cnt)
            for l in range(8, -1, -1):
                st2 = 1 << (l + 1)
                s_ = 1 << l
                cnt = H // st2 - 1
                if d == 0:
                    update(l, st2 - 1 + s_, st2 - 1, st2, cnt)
                else:
                    update(l, s_, st2, st2, cnt)

        def inject(d, col):
            fr, to = (H - 1, H) if d == 0 else (H, H - 1)
            ar = Pp.LamR[:, 0, col:col + 1]
            ai = Pp.LamI[:, 0, col:col + 1]
            an = Pp.LamN[:, 0, col:col + 1]
            er, ei = AB[:, 0, 0, fr:fr + 1], AB[:, 0, 1, fr:fr + 1]
            rd = [bA[0][0], bA[0][1], Pp.bL, btv, Pp.bk]
            wr = [btv]
            S.op("dve", lambda e: e.tensor_scalar(out=tv[:, 0:1], in0=er, scalar1=ar, scalar2=None, op0=ALU.mult), reads=rd, writes=wr)
            S.op("dve", lambda e: e.scalar_tensor_tensor(out=tv[:, 0:1], in0=ei, scalar=an, in1=tv[:, 0:1], op0=ALU.mult, op1=ALU.add),
                 reads=rd, writes=wr)
            S.op("dve", lambda e: e.tensor_scalar(out=tv[:, 1:2], in0=ei, scalar1=ar, scalar2=None, op0=ALU.mult), reads=rd, writes=wr)
            S.op("dve", lambda e: e.scalar_tensor_tensor(out=tv[:, 1:2], in0=er, scalar=ai, in1=tv[:, 1:2], op0=ALU.mult, op1=ALU.add),
                 reads=rd, writes=wr)
            for ri in range(2):
                S.op("dve", lambda e, ri=ri: e.scalar_tensor_tensor(out=AB[:, 0, ri, to:to + 1], in0=tv[:, ri:ri + 1], scalar=Pp.keep[:, 0:1],
                                                                     in1=AB[:, 0, ri, to:to + 1], op0=ALU.mult, op1=ALU.add),
                     reads=rd, writes=[bA[0][ri]])

        for gp in range(16):
            ub = gp % nbuf
            for a in range(2):
                S.dma("sp", U[:, ub, a, :], Ug[2 * gp + a], writes=[bU[ub]])
            for d in range(2):
                col = d * 16 + gp
                for kb in range(4):
                    ks = slice(kb * 512, (kb + 1) * 512)
                    for ri in range(2):
                        pi = nps % (2 * nbuf)
                        nps += 1
                        for a in range(2):
                            slot = col * 2 + ri
                            S.op("pe", lambda e, pi=pi, a=a, slot=slot, ks=ks, ub=ub: e.matmul(
                                PSs[pi][a * 64:(a + 1) * 64, :], lhsT=Pp.MBT[:, slot, a * 64:(a + 1) * 64], rhs=U[:, ub, a, ks],
                                start=True, stop=True),
                                reads=[Pp.bM, bU[ub]], writes=[bPS[pi]])
                        if ri == 0:
                            S.op("act", lambda e, pi=pi, ri=ri, ks=ks: e.copy(out=AB[:, 0, ri, ks], in_=PSs[pi][:]),
                                 reads=[bPS[pi]], writes=[bA[0][ri]])
                        else:
                            S.op("dve", lambda e, pi=pi, ri=ri, ks=ks: e.tensor_copy(out=AB[:, 0, ri, ks], in_=PSs[pi][:]),
                                 reads=[bPS[pi]], writes=[bA[0][ri]])
                if d == 0:
                    hs_scan(0, col, 0)
                    inject(0, col)
                    hs_scan(0, col, H)
                else:
                    hs_scan(1, col, H)
                    inject(1, col)
                    hs_scan(1, col, 0)
                if bg is not None:
                    next(bg, None)
                off = 1 if d == 0 else 0
                for ri in range(2):
                    S.op("act", lambda e, d=d, ri=ri, off=off: e.copy(out=Xb[:, d, ri, off:off + 2048], in_=AB[:, 0, ri, :]),
                         reads=[bA[0][ri]], writes=[bXb[d][ri]])
                    S.op("pool", lambda e, d=d, ri=ri: e.tensor_scalar(out=Xb[:, d, ri, H:H + 1], in0=Xb[:, d, ri, H:H + 1],
                                                                       scalar1=Pp.keep[:, 0:1], scalar2=None, op0=ALU.mult),
                         reads=[bXb[d][ri], Pp.bk], writes=[bXb[d][ri]])
            for a in range(2):
                g = 2 * gp + a
                ps_ = slice(a * 64, (a + 1) * 64)
                for kb in range(4):
                    ks = slice(kb * 512, (kb + 1) * 512)
                    yi = npy % 2
                    npy += 1
                    S.op("pe", lambda e, yi=yi, g=g, a=a, ks=ks, ub=ub: e.matmul(PY[yi][:], lhsT=Pp.Kin[:, g, :], rhs=U[:, ub, a, ks],
                                                                                 start=True, stop=False),
                         reads=[Pp.bM, bU[ub]], writes=[bPY[yi]])
                    for d in range(2):
                        q_ = d * 16 + gp
                        xs = slice(kb * 512 + d, kb * 512 + d + 512)
                        S.op("pe", lambda e, yi=yi, q_=q_, ps_=ps_, d=d, xs=xs: e.matmul(
                            PY[yi][:], lhsT=Pp.MCr[ps_, q_, :, :].rearrange("p j c -> p (j c)"), rhs=Xb[ps_, d, 0, xs],
                            start=False, stop=False), reads=[Pp.bM, bXb[d][0]], writes=[bPY[yi]])
                        S.op("pe", lambda e, yi=yi, q_=q_, ps_=ps_, d=d, xs=xs: e.matmul(
                            PY[yi][:], lhsT=Pp.MCn[ps_, q_, :, :].rearrange("p j c -> p (j c)"), rhs=Xb[ps_, d, 1, xs],
                            start=False, stop=(d == 1)), reads=[Pp.bM, bXb[d][1]], writes=[bPY[yi]])
                    S.op("act", lambda e, yi=yi: e.copy(out=Yst[:, yi, :], in_=PY[yi][:]), reads=[bPY[yi]], writes=[bYst[yi]])
                    S.dma("sp", Yg[g, :, ks], Yst[:, yi, :], reads=[bYst[yi]])
        if bg is not None:
            for _ in bg:
                pass
        S.emit()


GELU_C = 1.5957691216057308


def phase_s5_post(nc, S, C, Yg, selT_d, w_glu, sT):
    with contextlib.ExitStack() as st:
        sb, ps = mk(st, nc)
        Wgl = sb("Wgl", [128, 4, 512], BF16)
        selT = sb("selT", [128, 64, 128], BF16)
        Ysb = sb("Ysb", [128, 2, 32, 256], BF16)
        y = sb("y", [128, 2, 512], F32)
        w = sb("w", [128, 2, 512], F32)
        zf = sb("zf", [128, 4, 512], F32)
        zb = sb("zb", [128, 4, 512], BF16)
        sg = sb("sg", [128, 2, 512], F32)
        so = sb("so", [128, 2, 512], BF16)
        DG = [ps("DG%d" % i, [128, 8, 64], F32) for i in range(2)]
        GP = [ps("GP%d" % i, [128, 512], F32) for i in range(2)]
        bW, bsel = Buf(), Buf()
        bY = [Buf() for _ in range(2)]
        by = [Buf() for _ in range(2)]
        bw = [Buf() for _ in range(2)]
        bz = [Buf() for _ in range(4)]
        bsg = [Buf() for _ in range(2)]
        bso = [Buf() for _ in range(2)]
        bDG = [Buf() for _ in range(2)]
        bGP = [Buf() for _ in range(2)]
        load_weight_bf16(S, Wgl, bW, w_glu, 4)
        S.dma("sp", selT[:], selT_d, writes=[bsel])
        n = 0
        for t4 in range(T // 2048):
            yb = t4 % 2
            S.dma("sp", Ysb[:, yb, :, :], Yg[:, :, t4 * 256:(t4 + 1) * 256].rearrange("g p k -> p g k"), writes=[bY[yb]])
            for tt in range(4):
                t = t4 * 4 + tt
                for cc in range(4):
                    b = n % 2
                    n += 1
                    for i in range(8):
                        for g1 in range(8):
                            S.op("pe", lambda e, b=b, i=i, g1=g1, cc=cc, yb=yb, tt=tt: e.matmul(
                                DG[b][:, i, :], lhsT=selT[:, g1 * 8 + i, :], rhs=Ysb[:, yb, cc * 8 + g1, tt * 64:(tt + 1) * 64],
                                start=(g1 == 0), stop=(g1 == 7)), reads=[bsel, bY[yb]], writes=[bDG[b]])
                    src = bc_ap(DG[b][:], [[1, 64], [64, 8]])
                    dst = bc_ap(y[:, b, :], [[8, 64], [1, 8]])
                    S.op("act", lambda e, src=src, dst=dst: e.copy(out=dst, in_=src), reads=[bDG[b]], writes=[by[b]])
                    S.op("dve", lambda e, b=b: e.tensor_tensor(out=w[:, b, :], in0=y[:, b, :], in1=y[:, b, :], op=ALU.mult),
                         reads=[by[b]], writes=[bw[b]])
                    S.op("dve", lambda e, b=b: e.tensor_scalar(out=w[:, b, :], in0=w[:, b, :], scalar1=0.044715, scalar2=1.0,
                                                               op0=ALU.mult, op1=ALU.add), reads=[bw[b]], writes=[bw[b]])
                    S.op("dve", lambda e, b=b: e.tensor_tensor(out=w[:, b, :], in0=w[:, b, :], in1=y[:, b, :], op=ALU.mult),
                         reads=[bw[b], by[b]], writes=[bw[b]])
                    S.op("act", lambda e, b=b: e.activation(out=w[:, b, :], in_=w[:, b, :], func=AF.Sigmoid, scale=GELU_C),
                         reads=[bw[b]], writes=[bw[b]])
                    S.op("pool", lambda e, b=b, cc=cc: e.tensor_tensor(out=zf[:, cc, :], in0=y[:, b, :], in1=w[:, b, :], op=ALU.mult),
                         reads=[by[b], bw[b]], writes=[bz[cc]])
                    S.op("pool", lambda e, cc=cc: e.tensor_copy(out=zb[:, cc, :], in_=zf[:, cc, :]), reads=[bz[cc]], writes=[bz[cc]])
                for co in range(4):
                    b = co % 2
                    for cc in range(4):
                        S.op("pe", lambda e, b=b, cc=cc, co=co: e.matmul(GP[b][:], lhsT=Wgl[:, cc, co * 128:(co + 1) * 128],
                                                                         rhs=zb[:, cc, :], start=(cc == 0), stop=(cc == 3)),
                             reads=[bW] + bz, writes=[bGP[b]])
                    S.op("act", lambda e, b=b: e.activation(out=sg[:, b, :], in_=GP[b][:], func=AF.Sigmoid),
                         reads=[bGP[b]], writes=[bsg[b]])
                    S.op("dve", lambda e, b=b, co=co: e.tensor_tensor(out=so[:, b, :], in0=zf[:, co, :], in1=sg[:, b, :], op=ALU.mult),
                         reads=[bz[co], bsg[b]], writes=[bso[b]])
                    S.dma("sp", sT[co * 128:(co + 1) * 128, t * 512:(t + 1) * 512], so[:, b, :], reads=[bso[b]])
        S.emit()


INPUT_SPECS = [
    ("x", [T, D], F32), ("norm_g", [2, 6, D], F32),
    ("ffn_w_gate", [2, 2, D, DFF], F32), ("ffn_w_up", [2, 2, D, DFF], F32), ("ffn_w_down", [2, 2, DFF, D], F32),
    ("ev_w_in", [1, D, 1952], F32), ("mla_q_norm", [1, 256], F32), ("mla_kv_norm", [1, 128], F32),
    ("mla_w_uq", [1, 256, 768], F32), ("mla_w_ukv", [1, 128, 1024], F32), ("nat_rpb", [1, 8, 15, 31], F32),
    ("ev_w_out", [1, D, D], F32),
    ("od_w_in", [1, D, 1536], F32), ("conv_dw_w", [1, 31, 512], F32), ("conv_dw_b", [1, 512], F32),
    ("conv_ln_g", [1, 512], F32), ("conv_ln_b", [1, 512], F32),
    ("s5_lambda_re", [1, 2, 32, 64], F32), ("s5_lambda_im", [1, 2, 32, 64], F32), ("s5_log_step", [1, 2, 32], F32),
    ("s5_b_re", [1, 2, 32, 64, 16], F32), ("s5_b_im", [1, 2, 32, 64, 16], F32),
    ("s5_c_re", [1, 2, 32, 16, 64], F32), ("s5_c_im", [1, 2, 32, 16, 64], F32),
    ("s5_d", [1, 512], F32), ("s5_w_glu", [1, 512, 512], F32), ("od_w_out", [1, D, D], F32),
    ("ident", [128, 128], BF16), ("ident32", [128, 128], F32), ("rope_cos", [T, 16], F32), ("rope_sin", [T, 16], F32),
    ("kaug", [2, T], BF16), ("qaug", [2, T], BF16), ("segflag", [128, 1], F32), ("segkeep", [128, 1], F32),
    ("colmask", [64, 64], F32), ("jpad", [31, 127], F32),
    ("sel", [128, 64, 128], BF16), ("selT", [128, 64, 128], BF16), ("maskf", [128, 128], F32), ("maskb", [128, 128], F32),
]


def build(phases=("all",), dbg=()):
    nc = bass.Bass("TRN2", target_bir_lowering=False)
    I = {}
    for name, shape, dt in INPUT_SPECS:
        I[name] = nc.dram_tensor(name, shape, dt, kind="ExternalInput").ap()
    y = nc.dram_tensor("y", [T, D], F32, kind="ExternalOutput").ap()

    def scr(name, shape, dt):
        return nc.dram_tensor(name, shape, dt, kind=("ExternalOutput" if name in dbg else "Internal")).ap()
    hA = scr("hA", [T, D], F32)
    hB = scr("hB", [T, D], F32)
    qT = scr("qT", [8, 98, T], BF16)
    kT = scr("kT", [8, 98, T], BF16)
    vA = scr("vA", [T, 8, 65], BF16)
    nqT = scr("nqT", [8, 64, T], BF16)
    nkT = scr("nkT", [8, 64, T], BF16)
    nvA = scr("nvA", [T, 8, 65], BF16)
    mixO = scr("mixO", [T, 1024], BF16)
    mlaT = scr("mlaT", [512, T], BF16)
    rcd = scr("rcd", [2, 512], F32)
    rep = scr("rep", [120 * 64, 127], F32)
    uT = scr("uT", [512, T], BF16)
    Ug = scr("Ug", [32, 128, 2048], BF16)
    Yg = scr("Yg", [32, 128, 2048], BF16)
    convO = scr("convO", [512, T], BF16)
    sT = scr("sT", [512, T], BF16)
    allp = "all" in phases

    def on(p):
        return allp or p in phases

    with contextlib.ExitStack() as stack:
        S = Sched(nc, stack)
        C = Ctx()
        C.ident = stack.enter_context(nc.sbuf_tensor("ident_sb", [128, 128], BF16))
        C.bident = Buf()
        S.dma("sp", C.ident[:], I["ident"][:], writes=[C.bident])
        g = I["norm_g"]
        x = I["x"]
        cur = x
        if on("ffn1_0"):
            phase_ffn(nc, S, C, cur, hA, I["ffn_w_gate"][0, 0], I["ffn_w_up"][0, 0], I["ffn_w_down"][0, 0], g[0, 0], g[0, 1])
            cur = hA
        if on("ev_in"):
            phase_ev_in(nc, S, C, cur, g[0, 2], I["ev_w_in"][0], I["mla_q_norm"][0], I["mla_kv_norm"][0],
                        I["mla_w_uq"][0], I["mla_w_ukv"][0], I["rope_cos"], I["rope_sin"], I["kaug"], I["qaug"], qT, kT, vA, nqT, nkT, nvA)
        if on("mla"):
            phase_mla(nc, S, C, qT, kT, vA, mlaT, rcd)
        if on("nat"):
            phase_nat(nc, S, C, nqT, nkT, nvA, I["nat_rpb"][0], I["jpad"], I["colmask"], I["segflag"], rep, mixO)
        if on("ev_out"):
            dst = hB if allp else y
            phase_proj(nc, S, C, cur, dst, I["ev_w_out"][0], g[0, 3], mixO, 4, [(mlaT, 4)], wmap=[4, 5, 6, 7, 0, 1, 2, 3], tok_cols=(512, 1024))
            cur = dst
        if on("ffn2_0"):
            phase_ffn(nc, S, C, cur, hA, I["ffn_w_gate"][0, 1], I["ffn_w_up"][0, 1], I["ffn_w_down"][0, 1], g[0, 4], g[0, 5])
            cur = hA
        if on("ffn1_1"):
            phase_ffn(nc, S, C, cur, hB, I["ffn_w_gate"][1, 0], I["ffn_w_up"][1, 0], I["ffn_w_down"][1, 0], g[1, 0], g[1, 1])
            cur = hB
        if on("od_in"):
            phase_od_in(nc, S, C, cur, g[1, 2], I["od_w_in"][0], I["sel"], uT, Ug)
        conv_args = (uT, I["conv_dw_w"][0], I["conv_dw_b"][0], I["conv_ln_g"][0], I["conv_ln_b"][0],
                     I["segkeep"], I["ident32"], convO)
        fuse_conv = False
        if on("conv") and not fuse_conv:
            phase_conv(nc, S, C, *conv_args)
        if on("s5") or on("s5prep"):
            with contextlib.ExitStack() as pst:
                Pp = Ctx()
                Pp.Kin = pst.enter_context(nc.sbuf_tensor("Kin", [128, 32, 128], BF16))
                Pp.MBT = pst.enter_context(nc.sbuf_tensor("MBT", [128, 64, 128], BF16))
                Pp.MCr = pst.enter_context(nc.sbuf_tensor("MCr", [128, 32, 8, 16], BF16))
                Pp.MCn = pst.enter_context(nc.sbuf_tensor("MCn", [128, 32, 8, 16], BF16))
                Pp.LamR = pst.enter_context(nc.sbuf_tensor("LamR", [128, 10, 32], F32))
                Pp.LamI = pst.enter_context(nc.sbuf_tensor("LamI", [128, 10, 32], F32))
                Pp.LamN = pst.enter_context(nc.sbuf_tensor("LamN", [128, 10, 32], F32))
                Pp.keep = pst.enter_context(nc.sbuf_tensor("keep_s5", [128, 1], F32))
                Pp.bM, Pp.bL, Pp.bk = Buf(), Buf(), Buf()
                phase_s5_prep(nc, S, C, I, Pp)
                if on("s5"):
                    bgf = (lambda sb_, ps_: conv_gen(nc, S, C, sb_, ps_, *conv_args, acc_bufs=1)) if fuse_conv else None
                    phase_s5_main(nc, S, C, Pp, Ug, Yg, bgf)
        if on("s5_post"):
            phase_s5_post(nc, S, C, Yg, I["selT"], I["s5_w_glu"][0], sT)
        if on("od_out"):
            dst = hA if allp else y
            phase_proj(nc, S, C, cur, dst, I["od_w_out"][0], g[1, 3], None, 0, [(convO, 4), (sT, 4)])
            cur = dst
        if on("ffn2_1"):
            phase_ffn(nc, S, C, cur, y, I["ffn_w_gate"][1, 1], I["ffn_w_up"][1, 1], I["ffn_w_down"][1, 1], g[1, 4], g[1, 5])
    return nc


def host_consts(kind):
    c = {}
    c["ident"] = np.eye(128, dtype=ml_dtypes.bfloat16)
    pos = np.arange(T, dtype=np.float32)
    if kind == 1:
        pos = np.concatenate([np.arange(T // 2, dtype=np.float32)] * 2)
    inv = (np.float32(10000.0) ** (-np.arange(16, dtype=np.float32) / np.float32(16))).astype(np.float32)
    ang = (pos[:, None] * inv[None, :]).astype(np.float32)
    c["rope_cos"] = np.cos(ang).astype(np.float32)
    c["rope_sin"] = np.sin(ang).astype(np.float32)
    seg = np.zeros(T, np.float32)
    if kind == 1:
        seg[T // 2:] = 1.0
    big = -30000.0 * float(kind)
    c["kaug"] = np.stack([big * seg, np.full(T, big, np.float32)]).astype(ml_dtypes.bfloat16)
    c["qaug"] = np.stack([1.0 - 2.0 * seg, seg]).astype(ml_dtypes.bfloat16)
    c["segflag"] = np.full((128, 1), float(kind), np.float32)
    cq = np.arange(64)
    cs = np.clip(cq - 8, 0, 48)
    ck = np.arange(64)
    ok = (ck[:, None] >= cs[None, :]) & (ck[:, None] < cs[None, :] + 16)
    c["colmask"] = np.where(ok, 0.0, -30000.0).astype(np.float32)
    jp = np.zeros((31, 127), np.float32)
    for m in range(31):
        jp[m, 78 - m] = 1.0
    c["jpad"] = jp
    c["ident32"] = np.eye(128, dtype=np.float32)
    c["segkeep"] = np.full((128, 1), 1.0 - float(kind), np.float32)
    sel = np.zeros((128, 64, 128), np.float32)
    selT = np.zeros((128, 64, 128), np.float32)
    for g1 in range(8):
        for j in range(8):
            for ch in range(16):
                sel[g1 * 16 + ch, g1 * 8 + j, j * 16 + ch] = 1.0
                selT[j * 16 + ch, g1 * 8 + j, g1 * 16 + ch] = 1.0
    c["sel"] = sel.astype(ml_dtypes.bfloat16)
    c["selT"] = selT.astype(ml_dtypes.bfloat16)
    jj = np.arange(128) // 16
    c["maskf"] = (jj[None, :] >= jj[:, None]).astype(np.float32)
    c["maskb"] = (jj[:, None] >= jj[None, :]).astype(np.float32)
    return c


def kernel(**inputs):
    inp = {k: np.asarray(v) for k, v in inputs.items()}
    xs = [inp["x_sample"][0], inp["x_sample"][1], np.concatenate([inp["x_prompt"][0], inp["x_prompt"][1]], axis=0)]
    kinds = [0, 0, 1]
    nc = build()
    in_maps = []
    for x, k in zip(xs, kinds):
        m = {"x": np.ascontiguousarray(x, dtype=np.float32)}
        for name, shape, dt in INPUT_SPECS:
            if name in inp:
                m[name] = np.ascontiguousarray(inp[name], dtype=np.float32)
        m.update(host_consts(k))
        in_maps.append(m)
    res = run_bass_kernel_spmd(nc, in_maps, core_ids=list(range(NCORES)))
    ys = [np.asarray(r["y"], dtype=np.float32) for r in res.results]
    y_sample = np.stack([ys[0], ys[1]], axis=0)
    y_prompt = ys[2].reshape(2, T // 2, D)
    return (y_prompt, y_sample)
```

```python
import contextlib
import os
import numpy as np
import ml_dtypes
import concourse.bass as bass
import concourse.mybir as mybir
from concourse.bass_utils import run_bass_kernel_spmd

F32 = mybir.dt.float32
BF16 = mybir.dt.bfloat16
AF = mybir.ActivationFunctionType
ALU = mybir.AluOpType

T = 16384
D = 1024
DFF = 2816
NFC = DFF // 128
EPS = 1e-6
NCORES = 3


class Buf:
    __slots__ = ("lw", "rd", "name")

    def __init__(self, name=""):
        self.lw = None
        self.rd = {}
        self.name = name


class Sched:
    CENG = ("pe", "act", "dve", "pool")
    ENGS = ("pe", "act", "dve", "pool", "sp")
    NSLOT = {"sp": 16, "pool": 8, "act": 4}

    def __init__(self, nc, stack):
        self.nc = nc
        self.items = {e: [] for e in self.ENGS}
        self.nops = {e: 0 for e in self.CENG}
        self.known = {e: {} for e in self.ENGS}
        self.dma_n = {q: 0 for q in self.NSLOT}
        self.signalled = set()
        self.val = {}
        self.cnt = {e: 0 for e in self.CENG}
        self.sem = {}
        for e in self.CENG:
            self.sem[e] = stack.enter_context(nc.semaphore("s_" + e))
        for q, n in self.NSLOT.items():
            for s in range(n):
                self.sem[("d", q, s)] = stack.enter_context(nc.semaphore("d_%s%d" % (q, s)))

    def _deps(self, eng, reads, writes, extra=()):
        deps = {}

        def add(sig):
            if sig is None:
                return
            k, i = sig
            if deps.get(k, -1) < i:
                deps[k] = i
        for b in reads:
            add(b.lw)
        for b in writes:
            add(b.lw)
            for k, i in b.rd.items():
                if k == eng and eng in self.CENG:
                    continue
                add((k, i))
        for s in extra:
            add(s)
        if eng == "pe":
            deps.pop("pe", None)
        waits = []
        kn = self.known[eng]
        for k, i in deps.items():
            if kn.get(k, -1) >= i:
                continue
            kn[k] = i
            waits.append((k, i))
            self.signalled.add((k, i))
        return waits

    def _commit(self, sig, rkey, ridx, reads, writes):
        for b in writes:
            b.lw = sig
            b.rd = {}
        for b in reads:
            if b.rd.get(rkey, -1) < ridx:
                b.rd[rkey] = ridx

    def op(self, eng, fn, reads=(), writes=()):
        waits = self._deps(eng, reads, writes)
        idx = self.nops[eng]
        self.nops[eng] += 1
        sig = (eng, idx)
        self.items[eng].append((waits, fn, sig))
        self._commit(sig, eng, idx, reads, writes)
        return sig

    def dma(self, q, out, in_, reads=(), writes=(), **kw):
        n = self.dma_n[q]
        self.dma_n[q] += 1
        K = self.NSLOT[q]
        slot, rnd = n % K, n // K
        key = ("d", q, slot)
        extra = [(key, rnd - 1)] if rnd >= 1 else []
        waits = self._deps(q, reads, writes, extra)
        sig = (key, rnd)
        self.items[q].append((waits, lambda e: e.dma_start(out=out, in_=in_, **kw), sig))
        self._commit(sig, key, rnd, reads, writes)
        return sig

    def barrier(self):
        targets = []
        for e in self.CENG:
            if self.nops[e] > 0:
                targets.append((e, self.nops[e] - 1))
        for q, K in self.NSLOT.items():
            n = self.dma_n[q]
            for s in range(K):
                if n > s:
                    targets.append((("d", q, s), (n - 1 - s) // K))
        for e in self.ENGS:
            kn = self.known[e]
            waits = []
            for k, i in targets:
                if k == e:
                    continue
                if kn.get(k, -1) >= i:
                    continue
                kn[k] = i
                waits.append((k, i))
                self.signalled.add((k, i))
            self.items[e].append((waits, None, None))

    def emit(self):
        self.barrier()
        for e in self.CENG:
            for (_, fn, sig) in self.items[e]:
                if sig is not None and sig[0] == e and sig in self.signalled and sig not in self.val:
                    self.cnt[e] += 1
                    self.val[sig] = self.cnt[e]
        with self.nc.Block() as block:
            @block.tensor
            def _(h):
                self._emit_eng("pe", h)

            @block.scalar
            def _(h):
                self._emit_eng("act", h)

            @block.vector
            def _(h):
                self._emit_eng("dve", h)

            @block.gpsimd
            def _(h):
                self._emit_eng("pool", h)

            @block.sync
            def _(h):
                self._emit_eng("sp", h)
        self.items = {e: [] for e in self.ENGS}

    def _emit_eng(self, e, h):
        for (waits, fn, sig) in self.items[e]:
            for (k, i) in waits:
                if isinstance(k, tuple):
                    h.wait_ge(self.sem[k], 16 * (i + 1))
                else:
                    h.wait_ge(self.sem[k], self.val[(k, i)])
            if fn is None:
                continue
            ins = fn(h)
            if isinstance(sig[0], tuple):
                ins.then_inc(self.sem[sig[0]], 16)
            elif sig in self.val:
                ins.then_inc(self.sem[sig[0]], 1)


def bcast_rows(ap1d, nparts):
    return bass.AP(ap1d.tensor, ap1d.offset, [[0, nparts]] + [list(x) for x in ap1d.ap])


class Ctx:
    pass


def load_weight_bf16(S, dst_sb, dst_buf, src, nchunks):
    for kc in range(nchunks):
        S.dma("pool", dst_sb[:, kc, :], src[kc * 128:(kc + 1) * 128, :], writes=[dst_buf])


def phase_ffn(nc, S, C, h_in, h_out, wg, wu, wd, g_in, g_out):
    with contextlib.ExitStack() as st:
        sb, ps = mk(st, nc)
        Wg = sb("Wg", [128, 8, DFF], BF16)
        Wu = sb("Wu", [128, 8, DFF], BF16)
        Wd = sb("Wd", [128, NFC, D], BF16)
        gin = sb("gin", [128, D], F32)
        gout = sb("gout", [128, D], F32)
        Xa = sb("Xa", [128, 2, D], F32)
        Xb = sb("Xb", [128, 2, D], F32)
        xn = sb("xn", [128, 2, D], BF16)
        xnT = sb("xnT", [128, 2, 8, 512], BF16)
        aT = sb("aT", [128, NFC, 512], BF16)
        sg = sb("sg", [128, 2, 512], BF16)
        tmp = sb("tmp", [128, 2, 512], F32)
        junk = sb("junk", [128, D], BF16)
        st_ss = sb("ss", [128, 16], F32)
        G = [ps("G%d" % i, [128, 512], F32) for i in range(2)]
        U = [ps("U%d" % i, [128, 512], F32) for i in range(2)]
        Dp = [ps("D%d" % i, [128, 512], F32) for i in range(3)]
        TP = ps("TP", [128, 8, 128], BF16)

        bWg, bWu, bWd, bgin, bgout = Buf(), Buf(), Buf(), Buf(), Buf()
        bXa = [Buf() for _ in range(2)]
        bXb = [Buf() for _ in range(2)]
        bxn = [Buf() for _ in range(2)]
        bxnT = [[Buf() for _ in range(4)] for _ in range(2)]
        baT = [Buf() for _ in range(NFC)]
        bsg = [Buf() for _ in range(2)]
        btmp = [Buf() for _ in range(2)]
        bjunk = Buf()
        bss = [Buf() for _ in range(4)]
        bss2 = [Buf() for _ in range(4)]
        bG = [Buf() for _ in range(2)]
        bU = [Buf() for _ in range(2)]
        bD = [Buf() for _ in range(3)]
        bTP = Buf()

        S.dma("sp", gin[:], bcast_rows(g_in, 128), writes=[bgin])
        S.dma("sp", gout[:], bcast_rows(g_out, 128), writes=[bgout])
        S.op("pool", lambda e: e.tensor_scalar(out=gout[:], in0=gout[:], scalar1=0.5, scalar2=None, op0=ALU.mult),
             reads=[bgout], writes=[bgout])
        load_weight_bf16(S, Wg, bWg, wg, 8)
        load_weight_bf16(S, Wu, bWu, wu, 8)
        load_weight_bf16(S, Wd, bWd, wd, NFC)
        ntile = int(os.environ.get('FFN_NT', T // 512))
        nfe = [0]

        xslot = {}

        def front_norm(t, s):
            xb = nfe[0] % 2
            nfe[0] += 1
            xslot[(t, s)] = xb
            r0 = t * 512 + s * 128
            S.dma("sp", Xa[:, xb, :], h_in[r0:r0 + 128, :], writes=[bXa[xb]])
            S.op("act", lambda e: e.activation(out=junk[:], in_=Xa[:, xb, :], func=AF.Square, accum_out=st_ss[:, s:s + 1]),
                 reads=[bXa[xb]], writes=[bjunk, bss[s]])
            rstd_inplace(S, st_ss[:, s:s + 1], bss[s], D)
            S.op("dve", lambda e: e.scalar_tensor_tensor(out=xn[:, xb, :], in0=Xa[:, xb, :], scalar=st_ss[:, s:s + 1],
                                                         in1=gin[:], op0=ALU.mult, op1=ALU.mult),
                 reads=[bXa[xb], bss[s], bgin], writes=[bxn[xb]])

        def front_T(t, s):
            tb = t % 2
            xb = xslot.pop((t, s))
            for kc in range(8):
                S.op("pe", lambda e, kc=kc: e.transpose(out=TP[:, kc, :], in_=xn[:, xb, kc * 128:(kc + 1) * 128],
                                                        identity=C.ident[:]),
                     reads=[bxn[xb], C.bident], writes=[bTP])
            S.op("act", lambda e: e.copy(out=xnT[:, tb, :, s * 128:(s + 1) * 128], in_=TP[:]),
                 reads=[bTP], writes=[bxnT[tb][s]])

        front_norm(0, 0)
        for s in range(4):
            if s + 1 < 4:
                front_norm(0, s + 1)
            front_T(0, s)
        nep = 0
        for t in range(ntile):
            tb = t % 2
            for fc in range(NFC):
                b = fc % 2
                for kc in range(8):
                    S.op("pe", lambda e, fc=fc, kc=kc, b=b, tb=tb: e.matmul(G[b][:], lhsT=Wg[:, kc, fc * 128:(fc + 1) * 128],
                                                                     rhs=xnT[:, tb, kc, :], start=(kc == 0), stop=(kc == 7)),
                         reads=[bWg] + bxnT[tb], writes=[bG[b]])
                for kc in range(8):
                    S.op("pe", lambda e, fc=fc, kc=kc, b=b, tb=tb: e.matmul(U[b][:], lhsT=Wu[:, kc, fc * 128:(fc + 1) * 128],
                                                                     rhs=xnT[:, tb, kc, :], start=(kc == 0), stop=(kc == 7)),
                         reads=[bWu] + bxnT[tb], writes=[bU[b]])
                S.op("act", lambda e, b=b: e.activation(out=sg[:, b, :], in_=G[b][:], func=AF.Silu),
                     reads=[bG[b]], writes=[bsg[b]])
                S.op("dve", lambda e, b=b, fc=fc: e.tensor_tensor(out=aT[:, fc, :], in0=sg[:, b, :], in1=U[b][:], op=ALU.mult),
                     reads=[bsg[b], bU[b]], writes=[baT[fc]])
            if t + 1 < ntile:
                front_norm(t + 1, 0)
            for s in range(4):
                eb = nep % 2
                nep += 1
                r0 = t * 512 + s * 128
                S.dma("sp", Xb[:, eb, :], h_in[r0:r0 + 128, :], writes=[bXb[eb]])
                dbs = []
                for half in range(2):
                    di = (2 * s + half) % 3
                    dbs.append(di)
                    for fc in range(NFC):
                        S.op("pe", lambda e, di=di, fc=fc, s=s, half=half: e.matmul(
                            Dp[di][:], lhsT=aT[:, fc, s * 128:(s + 1) * 128],
                            rhs=Wd[:, fc, half * 512:(half + 1) * 512], start=(fc == 0), stop=(fc == NFC - 1)),
                            reads=[bWd, baT[fc]], writes=[bD[di]])
                    S.op("act", lambda e, di=di, s=s, half=half: e.activation(
                        out=junk[:, 0:512], in_=Dp[di][:], func=AF.Square,
                        accum_out=st_ss[:, 4 + 2 * s + half:5 + 2 * s + half]),
                        reads=[bD[di]], writes=[bjunk, bss2[s]])
                if t + 1 < ntile:
                    if s + 1 < 4:
                        front_norm(t + 1, s + 1)
                    front_T(t + 1, s)
                c0 = 4 + 2 * s
                S.op("dve", lambda e, c0=c0: e.tensor_tensor(out=st_ss[:, c0:c0 + 1], in0=st_ss[:, c0:c0 + 1],
                                                             in1=st_ss[:, c0 + 1:c0 + 2], op=ALU.add),
                     reads=[bss2[s]], writes=[bss2[s]])
                rstd_inplace(S, st_ss[:, c0:c0 + 1], bss2[s], D)
                for half in range(2):
                    di = dbs[half]
                    S.op("dve", lambda e, di=di, c0=c0, half=half: e.scalar_tensor_tensor(
                        out=tmp[:, half, :], in0=Dp[di][:], scalar=st_ss[:, c0:c0 + 1],
                        in1=gout[:, half * 512:(half + 1) * 512], op0=ALU.mult, op1=ALU.mult),
                        reads=[bD[di], bss2[s], bgout], writes=[btmp[half]])
                    S.op("pool", lambda e, eb=eb, half=half: e.tensor_tensor(
                        out=Xb[:, eb, half * 512:(half + 1) * 512], in0=Xb[:, eb, half * 512:(half + 1) * 512],
                        in1=tmp[:, half, :], op=ALU.add),
                        reads=[bXb[eb], btmp[half]], writes=[bXb[eb]])
                S.dma("pool", h_out[r0:r0 + 128, :], Xb[:, eb, :], reads=[bXb[eb]])
        S.emit()


_PH = [0]


def mk(st, nc):
    _PH[0] += 1
    pre = "p%d_" % _PH[0]

    def sb(name, shape, dt):
        return st.enter_context(nc.sbuf_tensor(pre + name, shape, dt))

    def ps(name, shape, dt):
        return st.enter_context(nc.psum_tensor(pre + name, shape, dt))
    return sb, ps


def bc_ap(a, shape_ap):
    return bass.AP(a.tensor, a.offset, [list(a.ap[0])] + [list(x) for x in shape_ap])


def rstd_inplace(S, ss, b, n):
    S.op("dve", lambda e: e.tensor_scalar(out=ss, in0=ss, scalar1=1.0 / n, scalar2=EPS, op0=ALU.mult, op1=ALU.add),
         reads=[b], writes=[b])
    S.op("act", lambda e: e.activation(out=ss, in_=ss, func=AF.Sqrt), reads=[b], writes=[b])
    S.op("dve", lambda e: e.reciprocal(out=ss, in_=ss), reads=[b], writes=[b])


def rope_ops(S, x1, x2, cos, sin, o1, o2, tt, bx, btt, bo, bcs, bcast_h=None):
    t1, t2, t3, t4 = tt
    S.op("dve", lambda e: e.tensor_tensor(out=t1, in0=x1, in1=cos, op=ALU.mult), reads=[bx, bcs], writes=[btt[0]])
    S.op("dve", lambda e: e.tensor_tensor(out=t2, in0=x2, in1=sin, op=ALU.mult), reads=[bx, bcs], writes=[btt[1]])
    S.op("pool", lambda e: e.tensor_tensor(out=t3, in0=x2, in1=cos, op=ALU.mult), reads=[bx, bcs], writes=[btt[2]])
    S.op("pool", lambda e: e.tensor_tensor(out=t4, in0=x1, in1=sin, op=ALU.mult), reads=[bx, bcs], writes=[btt[3]])
    a1, a2, a3, a4 = (t1, t2, t3, t4) if bcast_h is None else bcast_h
    S.op("dve", lambda e: e.tensor_tensor(out=o1, in0=a1, in1=a2, op=ALU.subtract), reads=[btt[0], btt[1]], writes=[bo])
    S.op("pool", lambda e: e.tensor_tensor(out=o2, in0=a3, in1=a4, op=ALU.add), reads=[btt[2], btt[3]], writes=[bo])


def phase_ev_in(nc, S, C, h_in, g2, w_in, qn_g, kvn_g, w_uq, w_ukv, cos_d, sin_d, kaug, qaug, qT, kT, vA, nqT, nkT, nvA):
    with contextlib.ExitStack() as st:
        sb, ps = mk(st, nc)
        Win = sb("Win", [128, 8, 1952], BF16)
        Wuq = sb("Wuq", [128, 2, 768], BF16)
        Wukv = sb("Wukv", [128, 1024], BF16)
        gin = sb("gin", [128, D], F32)
        qng = sb("qng", [128, 256], F32)
        kvng = sb("kvng", [128, 128], F32)
        cos = sb("cos", [128, 128, 16], F32)
        sin = sb("sin", [128, 128, 16], F32)
        X = sb("X", [128, 2, 4, D], F32)
        xn = sb("xn", [128, 2, D], BF16)
        mT = sb("mT", [128, 2, 8, 512], BF16)
        junk = sb("junk", [128, D], BF16)
        ss = sb("ss", [128, 2, 8], F32)
        ssl = sb("ssl", [128, 8], F32)
        lat_bf = sb("lat_bf", [128, 384], BF16)
        kr_f = sb("kr_f", [128, 32], F32)
        latT = sb("latT", [128, 2, 3, 128], BF16)
        q_sb = sb("q_sb", [128, 8, 96], F32)
        q_bf = sb("q_bf", [128, 8, 96], BF16)
        k_bf = sb("k_bf", [128, 8, 96], BF16)
        tq = sb("tq", [128, 4, 8, 16], F32)
        tk = sb("tk", [128, 4, 16], F32)
        qT_sb = sb("qT_sb", [96, 8, 512], BF16)
        kT_sb = sb("kT_sb", [96, 8, 512], BF16)
        v_sb = sb("v_sb", [128, 4, 8, 65], BF16)
        nv_sb = sb("nv_sb", [128, 4, 8, 65], BF16)
        nst = sb("nst", [128, 2, 512], BF16)
        TPm = ps("TPm", [128, 8, 128], BF16)
        LAT = ps("LAT", [128, 416], F32)
        TP2 = ps("TP2", [128, 3, 128], BF16)
        TPq = ps("TPq", [128, 8, 128], BF16)
        P = [ps("P%d" % i, [128, 512], F32) for i in range(4)]

        bW, bg, bcs = Buf(), Buf(), Buf()
        bX = [[Buf() for _ in range(4)] for _ in range(2)]
        bxn = [Buf() for _ in range(2)]
        bmT = [[Buf() for _ in range(4)] for _ in range(2)]
        bjunk, bssl = Buf(), Buf()
        bss = [Buf() for _ in range(2)]
        blat, bkr, bq, bqb, bkb = Buf(), Buf(), Buf(), Buf(), Buf()
        blatT = [Buf() for _ in range(2)]
        btq = [Buf() for _ in range(4)]
        btk = [Buf() for _ in range(4)]
        bqT, bkT, bv, bnv = Buf(), Buf(), Buf(), Buf()
        bnst = [Buf() for _ in range(2)]
        bTPm, bLAT, bTP2, bTPq = Buf(), Buf(), Buf(), Buf()
        bP = [Buf() for _ in range(4)]

        S.dma("sp", gin[:], bcast_rows(g2, 128), writes=[bg])
        S.dma("sp", qng[:], bcast_rows(qn_g, 128), writes=[bg])
        S.dma("sp", kvng[:], bcast_rows(kvn_g, 128), writes=[bg])
        S.dma("sp", cos[:], cos_d.rearrange("(j p) e -> p j e", p=128), writes=[bcs])
        S.dma("sp", sin[:], sin_d.rearrange("(j p) e -> p j e", p=128), writes=[bcs])
        load_weight_bf16(S, Win, bW, w_in, 8)
        load_weight_bf16(S, Wuq, bW, w_uq, 2)
        S.dma("pool", Wukv[:], w_ukv, writes=[bW])
        for h in range(8):
            S.dma("sp", kT[h, 96:98, :], kaug, writes=[Buf()])
            S.dma("sp", qT[h, 96:98, :], qaug, writes=[Buf()])
        S.op("pool", lambda e: e.memset(v_sb[:, :, :, 64:65], 1.0), writes=[bv])
        S.op("pool", lambda e: e.memset(nv_sb[:, :, :, 64:65], 1.0), writes=[bnv])

        for t in range(T // 512):
            tb = t % 2
            norm_tile(nc, S, C, t, h_in, X, bX, xn, bxn, mT, bmT, TPm, bTPm, junk, bjunk, ss, bss, gin, bg)
            for s in range(4):
                sl = slice(s * 128, (s + 1) * 128)
                lb = s % 2
                for kc in range(8):
                    S.op("pe", lambda e, kc=kc, sl=sl, tb=tb: e.matmul(LAT[:], lhsT=mT[:, tb, kc, sl], rhs=Win[:, kc, 0:416],
                                                               start=(kc == 0), stop=(kc == 7)),
                         reads=[bmT[tb][s], bW], writes=[bLAT])
                S.op("act", lambda e: e.activation(out=junk[:, 0:256], in_=LAT[:, 0:256], func=AF.Square,
                                                   accum_out=ssl[:, 4:5]), reads=[bLAT], writes=[bjunk, bssl])
                S.op("act", lambda e: e.activation(out=junk[:, 0:128], in_=LAT[:, 256:384], func=AF.Square,
                                                   accum_out=ssl[:, 5:6]), reads=[bLAT], writes=[bjunk, bssl])
                S.op("dve", lambda e: e.tensor_scalar(out=ssl[:, 4:5], in0=ssl[:, 4:5], scalar1=1.0 / 256, scalar2=EPS,
                                                      op0=ALU.mult, op1=ALU.add), reads=[bssl], writes=[bssl])
                S.op("dve", lambda e: e.tensor_scalar(out=ssl[:, 5:6], in0=ssl[:, 5:6], scalar1=1.0 / 128, scalar2=EPS,
                                                      op0=ALU.mult, op1=ALU.add), reads=[bssl], writes=[bssl])
                S.op("act", lambda e: e.activation(out=ssl[:, 4:6], in_=ssl[:, 4:6], func=AF.Sqrt), reads=[bssl], writes=[bssl])
                S.op("dve", lambda e: e.reciprocal(out=ssl[:, 4:6], in_=ssl[:, 4:6]), reads=[bssl], writes=[bssl])
                S.op("dve", lambda e: e.scalar_tensor_tensor(out=lat_bf[:, 0:256], in0=LAT[:, 0:256], scalar=ssl[:, 4:5],
                                                             in1=qng[:], op0=ALU.mult, op1=ALU.mult),
                     reads=[bLAT, bssl, bg], writes=[blat])
                S.op("dve", lambda e: e.scalar_tensor_tensor(out=lat_bf[:, 256:384], in0=LAT[:, 256:384], scalar=ssl[:, 5:6],
                                                             in1=kvng[:], op0=ALU.mult, op1=ALU.mult),
                     reads=[bLAT, bssl, bg], writes=[blat])
                S.op("act", lambda e: e.copy(out=kr_f[:], in_=LAT[:, 384:416]), reads=[bLAT], writes=[bkr])
                j = t * 4 + s
                cs, sn = cos[:, j, :], sin[:, j, :]
                tks = [tk[:, i, :] for i in range(4)]
                bch = [bc_ap(tk[:, i, :], [[0, 8], [1, 16]]) for i in range(4)]
                rope_ops(S, kr_f[:, 0:16], kr_f[:, 16:32], cs, sn, k_bf[:, :, 64:80], k_bf[:, :, 80:96], tks,
                         bkr, btk, bkb, bcs, bcast_h=bch)
                for c in range(3):
                    S.op("pe", lambda e, c=c: e.transpose(out=TP2[:, c, :], in_=lat_bf[:, c * 128:(c + 1) * 128],
                                                          identity=C.ident[:]),
                         reads=[blat, C.bident], writes=[bTP2])
                S.op("act", lambda e, lb=lb: e.copy(out=latT[:, lb, :, :], in_=TP2[:]), reads=[bTP2], writes=[blatT[lb]])
                for c in range(2):
                    S.op("pe", lambda e, c=c, lb=lb: e.matmul(P[0][:], lhsT=latT[:, lb, c, :], rhs=Wuq[:, c, 0:512],
                                                              start=(c == 0), stop=(c == 1)),
                         reads=[blatT[lb], bW], writes=[bP[0]])
                for c in range(2):
                    S.op("pe", lambda e, c=c, lb=lb: e.matmul(P[1][:, 0:256], lhsT=latT[:, lb, c, :], rhs=Wuq[:, c, 512:768],
                                                              start=(c == 0), stop=(c == 1)),
                         reads=[blatT[lb], bW], writes=[bP[1]])
                qf = q_sb[:].rearrange("p h e -> p (h e)")
                S.op("act", lambda e, qf=qf: e.copy(out=qf[:, 0:512], in_=P[0][:]), reads=[bP[0]], writes=[bq])
                S.op("act", lambda e, qf=qf: e.copy(out=qf[:, 512:768], in_=P[1][:, 0:256]), reads=[bP[1]], writes=[bq])
                S.op("dve", lambda e: e.tensor_copy(out=q_bf[:, :, 0:64], in_=q_sb[:, :, 0:64]), reads=[bq], writes=[bqb])
                csb = bc_ap(cs, [[0, 8], [1, 16]])
                snb = bc_ap(sn, [[0, 8], [1, 16]])
                tqs = [tq[:, i, :, :] for i in range(4)]
                rope_ops(S, q_sb[:, :, 64:80], q_sb[:, :, 80:96], csb, snb, q_bf[:, :, 64:80], q_bf[:, :, 80:96], tqs,
                         bq, btq, bqb, bcs)
                for hf in range(2):
                    S.op("pe", lambda e, hf=hf, lb=lb: e.matmul(P[2 + hf][:], lhsT=latT[:, lb, 2, :],
                                                                rhs=Wukv[:, hf * 512:(hf + 1) * 512], start=True, stop=True),
                         reads=[blatT[lb], bW], writes=[bP[2 + hf]])
                    pv = bc_ap(P[2 + hf][:], [[128, 4], [1, 64]])
                    pv2 = bass.AP(pv.tensor, pv.offset + 64, [list(x) for x in pv.ap])
                    S.op("act", lambda e, hf=hf, pv=pv: e.copy(out=k_bf[:, hf * 4:(hf + 1) * 4, 0:64], in_=pv),
                         reads=[bP[2 + hf]], writes=[bkb])
                    S.op("dve", lambda e, hf=hf, pv2=pv2, s=s: e.tensor_copy(out=v_sb[:, s, hf * 4:(hf + 1) * 4, 0:64], in_=pv2),
                         reads=[bP[2 + hf]], writes=[bv])
                for h in range(8):
                    S.op("pe", lambda e, h=h: e.transpose(out=TPq[0:96, h, :], in_=q_bf[:, h, :], identity=C.ident[:]),
                         reads=[bqb, C.bident], writes=[bTPq])
                S.op("act", lambda e, sl=sl: e.copy(out=qT_sb[:, :, sl], in_=TPq[0:96, :, :]), reads=[bTPq], writes=[bqT])
                for h in range(8):
                    S.op("pe", lambda e, h=h: e.transpose(out=TPq[0:96, h, :], in_=k_bf[:, h, :], identity=C.ident[:]),
                         reads=[bkb, C.bident], writes=[bTPq])
                S.op("dve", lambda e, sl=sl: e.tensor_copy(out=kT_sb[:, :, sl], in_=TPq[0:96, :, :]), reads=[bTPq], writes=[bkT])
                for kc in range(8):
                    S.op("pe", lambda e, kc=kc, sl=sl, tb=tb: e.matmul(P[0][:], lhsT=mT[:, tb, kc, sl], rhs=Win[:, kc, 1440:1952],
                                                               start=(kc == 0), stop=(kc == 7)),
                         reads=[bmT[tb][s], bW], writes=[bP[0]])
                pv = bc_ap(P[0][:], [[64, 8], [1, 64]])
                S.op("act", lambda e, pv=pv, s=s: e.copy(out=nv_sb[:, s, :, 0:64], in_=pv), reads=[bP[0]], writes=[bnv])
            for c in range(8):
                pi = 1 + (c % 2) * 2
                for kc in range(8):
                    S.op("pe", lambda e, kc=kc, c=c, pi=pi, tb=tb: e.matmul(P[pi][:], lhsT=Win[:, kc, 416 + c * 128:544 + c * 128],
                                                                     rhs=mT[:, tb, kc, :], start=(kc == 0), stop=(kc == 7)),
                         reads=bmT[tb] + [bW], writes=[bP[pi]])
                nb = c % 2
                if c % 2 == 0:
                    S.op("act", lambda e, nb=nb, pi=pi: e.copy(out=nst[:, nb, :], in_=P[pi][:]), reads=[bP[pi]], writes=[bnst[nb]])
                else:
                    S.op("dve", lambda e, nb=nb, pi=pi: e.tensor_copy(out=nst[:, nb, :], in_=P[pi][:]), reads=[bP[pi]], writes=[bnst[nb]])
                dst = (nqT if c < 4 else nkT).rearrange("h r t -> (h r) t")
                cc = c % 4
                S.dma("pool", dst[cc * 128:(cc + 1) * 128, t * 512:(t + 1) * 512], nst[:, nb, :], reads=[bnst[nb]])
            cols = slice(t * 512, (t + 1) * 512)
            S.dma("pool", qT[:, 0:96, cols].rearrange("h r t -> r h t"), qT_sb[:], reads=[bqT])
            S.dma("pool", kT[:, 0:96, cols].rearrange("h r t -> r h t"), kT_sb[:], reads=[bkT])
            S.dma("pool", vA[t * 512:(t + 1) * 512].rearrange("(s p) h e -> p s h e", p=128), v_sb[:], reads=[bv])
            S.dma("pool", nvA[t * 512:(t + 1) * 512].rearrange("(s p) h e -> p s h e", p=128), nv_sb[:], reads=[bnv])
        S.emit()


MLA_SCALE = 96.0 ** -0.5


def phase_mla(nc, S, C, qT, kT, vA, mlaT, rcd):
    with contextlib.ExitStack() as st:
        sb, ps = mk(st, nc)
        KT = sb("KT", [98, 2, T], BF16)
        V = sb("V", [128, 2, 128, 65], BF16)
        qt_sb = sb("qt_sb", [98, 3, 512], BF16)
        PT = sb("PT", [128, 3, 1024], BF16)
        rc = sb("rc", [128, 2, 512], F32)
        bcs = sb("bcs", [64, 2, 512], F32)
        oT = sb("oT", [64, 2, 512], BF16)
        Sp = [ps("S%d" % i, [128, 1024], F32) for i in range(3)]
        O = [ps("O%d" % i, [128, 512], F32) for i in range(2)]
        bKT = [Buf() for _ in range(2)]
        bV = [Buf() for _ in range(2)]
        bq = [Buf() for _ in range(3)]
        bPT = [Buf() for _ in range(3)]
        bS = [Buf() for _ in range(3)]
        bO = [Buf() for _ in range(2)]
        brc = [Buf() for _ in range(2)]
        brcd = [Buf() for _ in range(2)]
        bbcs = [Buf() for _ in range(2)]
        boT = [Buf() for _ in range(2)]
        tiles = [(h, qt) for h in range(8) for qt in range(T // 512)]
        groups = [list(range(k, k + 2)) for k in range(0, 128, 2)]

        def load_head(h):
            hb = h % 2
            for c in range(4):
                S.dma("sp", KT[:, hb, c * 4096:(c + 1) * 4096], kT[h, :, c * 4096:(c + 1) * 4096], writes=[bKT[hb]])
            vsrc = bass.AP(vA.tensor, vA.offset + h * 65, [[8 * 65, 128], [128 * 8 * 65, 128], [1, 65]])
            S.dma("sp", V[:, hb, :, :], vsrc, writes=[bV[hb]])

        def load_q(ti):
            h, qt = tiles[ti]
            S.dma("sp", qt_sb[:, ti % 3, :], qT[h, :, qt * 512:(qt + 1) * 512], writes=[bq[ti % 3]])

        def qk(k, ti, kbs):
            h, qt = tiles[ti]
            hb, qb = h % 2, ti % 3
            g = k % 3
            for j, kb in enumerate(kbs):
                S.op("pe", lambda e, kb=kb, g=g, hb=hb, qb=qb, j=j: e.matmul(
                    Sp[g][:, j * 512:(j + 1) * 512], lhsT=KT[:, hb, kb * 128:(kb + 1) * 128], rhs=qt_sb[:, qb, :],
                    start=True, stop=True), reads=[bKT[hb], bq[qb]], writes=[bS[g]])

        def ex(k, ti, kbs):
            g, p = k % 3, k % 3
            n = len(kbs) * 512
            S.op("act", lambda e, g=g, p=p, n=n: e.activation(out=PT[:, p, 0:n], in_=Sp[g][:, 0:n], func=AF.Exp, scale=MLA_SCALE),
                 reads=[bS[g]], writes=[bPT[p]])

        def pv(k, ti, kbs):
            h, qt = tiles[ti]
            hb, ob = h % 2, ti % 2
            p = k % 3
            for j, kb in enumerate(kbs):
                S.op("pe", lambda e, kb=kb, p=p, hb=hb, ob=ob, j=j: e.matmul(
                    O[ob][0:65, :], lhsT=V[:, hb, kb, :], rhs=PT[:, p, j * 512:(j + 1) * 512],
                    start=(kb == 0), stop=(kb == 127)), reads=[bV[hb], bPT[p]], writes=[bO[ob]])

        def epilogue(ti):
            h, qt = tiles[ti]
            ob = ti % 2
            S.op("dve", lambda e, ob=ob: e.reciprocal(out=rc[64:65, ob, :], in_=O[ob][64:65, :]), reads=[bO[ob]], writes=[brc[ob]])
            S.dma("sp", rcd[ob:ob + 1, :], rc[64:65, ob, :], reads=[brc[ob]], writes=[brcd[ob]])
            S.dma("sp", bcs[:, ob, :], bcast_rows(rcd[ob], 64), reads=[brcd[ob]], writes=[bbcs[ob]])
            S.op("dve", lambda e, ob=ob: e.tensor_tensor(out=oT[:, ob, :], in0=O[ob][0:64, :], in1=bcs[:, ob, :], op=ALU.mult),
                 reads=[bO[ob], bbcs[ob]], writes=[boT[ob]])
            S.dma("sp", mlaT[h * 64:(h + 1) * 64, qt * 512:(qt + 1) * 512], oT[:, ob, :], reads=[boT[ob]])

        steps = [(ti, kbs) for ti in range(len(tiles)) for kbs in groups]
        load_head(0)
        load_q(0)
        load_q(1)
        AHEAD = 2
        for k0 in range(AHEAD):
            qk(k0, *steps[k0])
        pending_epi = None
        for k, (ti, kbs) in enumerate(steps):
            if kbs[0] == 0:
                h, qt = tiles[ti]
                if ti + 2 < len(tiles):
                    load_q(ti + 2)
                if qt == 0 and h + 1 < 8:
                    load_head(h + 1)
            ex(k, ti, kbs)
            if k + AHEAD < len(steps):
                qk(k + AHEAD, *steps[k + AHEAD])
            pv(k, ti, kbs)
            if pending_epi is not None and kbs[0] == 6:
                epilogue(pending_epi)
                pending_epi = None
            if kbs[-1] == 127:
                pending_epi = ti
        if pending_epi is not None:
            epilogue(pending_epi)
        S.emit()


def nat_ws(R, rows):
    return min(max(R - 4, 0), rows - 8)


def phase_nat(nc, S, C, nqT, nkT, nvA, rpb, jpad_d, colmask_d, flag_d, rep, natO):
    with contextlib.ExitStack() as st:
        sb, ps = mk(st, nc)
        KT = sb("nKT", [64, T], BF16)
        QT = sb("nQT", [64, T], BF16)
        V = sb("nV", [128, 256, 65], BF16)
        B = sb("nB", [128, 8, 14, 64], F32)
        Bint = sb("nBint", [128, 8, 1024], F32)
        cm = sb("cm", [128, 64], F32)
        rT = sb("rT", [31, 120], F32)
        jp = sb("jp", [31, 127], F32)
        pr = sb("pr", [120, 127], F32)
        fl = sb("fl", [128, 1], F32)
        Sb = sb("Sb", [128, 2, 1024], F32)
        Pb = sb("Pb", [128, 2, 1024], BF16)
        Ost = sb("Ost", [64, 256, 64], BF16)
        OB = sb("OB", [64, 8, 64], F32)
        OA = sb("OA", [64, 8, 64], F32)
        rcp = sb("rcp", [64, 2, 4], F32)
        brcp = [Buf() for _ in range(2)]
        Sp = [ps("nS%d" % i, [128, 1024], F32) for i in range(2)]
        Op = [ps("nO%d" % i, [64, 4, 65], F32) for i in range(2)]
        PR = ps("PR", [120, 127], F32)
        bKT, bQT, bV, bB, bcm, brT, bjp, bpr, bfl, bPR, brep = (Buf() for _ in range(11))
        bSb = [Buf() for _ in range(2)]
        bPb = [Buf() for _ in range(2)]
        bS = [Buf() for _ in range(2)]
        bO = [Buf() for _ in range(2)]
        bOst, bOB, bOA = Buf(), Buf(), Buf()
        S.dma("sp", rT[:], rpb.rearrange("h d m -> m (h d)"), writes=[brT], allow_slow_non_contiguous=True)
        S.dma("sp", jp[:], jpad_d, writes=[bjp])
        S.dma("sp", cm[0:64, :], colmask_d, writes=[bcm])
        S.dma("sp", cm[64:128, :], colmask_d, writes=[bcm])
        S.dma("sp", fl[:], flag_d, writes=[bfl])
        S.op("pe", lambda e: e.matmul(PR[:], lhsT=rT[:], rhs=jp[:], start=True, stop=True), reads=[brT, bjp], writes=[bPR])
        S.op("act", lambda e: e.copy(out=pr[:], in_=PR[:]), reads=[bPR], writes=[bpr])
        W = 127
        rep3 = rep.rearrange("(a p) w -> a p w", p=64)
        S.dma("sp", rep3, bc_ap(pr[:], [[0, 64], [1, W]]), reads=[bpr], writes=[brep])
        for h in range(8):
            for kr in range(2):
                src = bass.AP(rep.tensor, rep.offset + (h * 15 + kr) * 64 * W + 63, [[W - 1, 64], [64 * W, 14], [1, 64]])
                S.dma("sp", B[kr * 64:(kr + 1) * 64, h, :, :], src, reads=[brep], writes=[bB])
        S.op("dve", lambda e: e.tensor_tensor(out=B[:].rearrange("p h d c -> p (h d) c"), in0=B[:].rearrange("p h d c -> p (h d) c"),
                                              in1=bc_ap(cm[:], [[0, 112], [1, 64]]), op=ALU.add),
             reads=[bB, bcm], writes=[bB])
        for h in range(8):
            for r in range(4):
                S.op("dve", lambda e, h=h, r=r: e.tensor_copy(out=bc_ap(Bint[:, h, r * 256:(r + 1) * 256], [[64, 4], [1, 64]]),
                                                              in_=bc_ap(B[:, h, 3, :], [[2 * 64, 4], [1, 64]])),
                     reads=[bB], writes=[bB])
        nrow = T // 64

        def row(h, R, ws, dst, i):
            for b in range(4):
                r0 = ws + 2 * b
                S.op("pe", lambda e, b=b, r0=r0, R=R, i=i: e.matmul(Sp[i][:, b * 64:(b + 1) * 64],
                                                                     lhsT=KT[:, r0 * 64:r0 * 64 + 128],
                                                                     rhs=QT[:, R * 64:(R + 1) * 64], start=True, stop=True),
                     reads=[bKT, bQT], writes=[bS[i]])
            d0 = ws - R + 7
            bsl = bc_ap(B[:, h, d0, :], [[2 * 64, 4], [1, 64]])
            so = bc_ap(Sb[:, i, 0:256], [[64, 4], [1, 64]])
            si = bc_ap(Sp[i][:, 0:256], [[64, 4], [1, 64]])
            S.op("dve", lambda e, bsl=bsl, so=so, si=si: e.scalar_tensor_tensor(out=so, in0=si, scalar=0.125, in1=bsl,
                                                                                op0=ALU.mult, op1=ALU.add),
                 reads=[bS[i], bB], writes=[bSb[i]])
            S.op("act", lambda e, i=i: e.activation(out=Pb[:, i, 0:256], in_=Sb[:, i, 0:256], func=AF.Exp),
                 reads=[bSb[i]], writes=[bPb[i]])
            for b in range(4):
                r0 = ws + 2 * b
                S.op("pe", lambda e, b=b, r0=r0, i=i: e.matmul(Op[i][:, 0, :], lhsT=Pb[:, i, b * 64:(b + 1) * 64],
                                                               rhs=V[:, r0, :], start=(b == 0), stop=(b == 3)),
                     reads=[bPb[i], bV], writes=[bO[i]])
            S.op("dve", lambda e, i=i: e.reciprocal(out=rcp[:, i, 0:1], in_=Op[i][:, 0, 64:65]), reads=[bO[i]], writes=[brcp[i]])
            S.op("dve", lambda e, i=i, dst=dst[0]: e.tensor_scalar(out=dst, in0=Op[i][:, 0, 0:64], scalar1=rcp[:, i, 0:1],
                                                                   scalar2=None, op0=ALU.mult),
                 reads=[bO[i], brcp[i]], writes=[dst[1]])

        def rowpair(h, R, i):
            for r in range(4):
                for b in range(4):
                    r0 = R + r - 4 + 2 * b
                    S.op("pe", lambda e, b=b, r0=r0, r=r, i=i: e.matmul(Sp[i][:, r * 256 + b * 64:r * 256 + (b + 1) * 64],
                                                                         lhsT=KT[:, r0 * 64:r0 * 64 + 128],
                                                                         rhs=QT[:, (R + r) * 64:(R + r + 1) * 64], start=True, stop=True),
                         reads=[bKT, bQT], writes=[bS[i]])
            S.op("dve", lambda e: e.scalar_tensor_tensor(out=Sb[:, i, :], in0=Sp[i][:], scalar=0.125, in1=Bint[:, h, :],
                                                         op0=ALU.mult, op1=ALU.add),
                 reads=[bS[i], bB], writes=[bSb[i]])
            S.op("act", lambda e: e.activation(out=Pb[:, i, :], in_=Sb[:, i, :], func=AF.Exp), reads=[bSb[i]], writes=[bPb[i]])
            for r in range(4):
                for b in range(4):
                    r0 = R + r - 4 + 2 * b
                    S.op("pe", lambda e, b=b, r0=r0, r=r: e.matmul(Op[i][:, r, :], lhsT=Pb[:, i, r * 256 + b * 64:r * 256 + (b + 1) * 64],
                                                                   rhs=V[:, r0, :], start=(b == 0), stop=(b == 3)),
                         reads=[bPb[i], bV], writes=[bO[i]])
            S.op("dve", lambda e: e.reciprocal(out=rcp[:, i, :], in_=Op[i][:, :, 64]), reads=[bO[i]], writes=[brcp[i]])
            S.op("dve", lambda e: e.tensor_tensor(out=Ost[:, R:R + 4, :], in0=Op[i][:, :, 0:64],
                                                  in1=bc_ap(rcp[:, i, :], [[1, 4], [0, 64]]), op=ALU.mult),
                 reads=[bO[i], brcp[i]], writes=[bOst])

        n = 0
        for h in range(8):
            for c in range(4):
                S.dma("sp", KT[:, c * 4096:(c + 1) * 4096], nkT[h, :, c * 4096:(c + 1) * 4096], writes=[bKT])
                S.dma("sp", QT[:, c * 4096:(c + 1) * 4096], nqT[h, :, c * 4096:(c + 1) * 4096], writes=[bQT])
            vlo = bass.AP(nvA.tensor, nvA.offset + h * 65, [[8 * 65, 64], [64 * 8 * 65, 256], [1, 65]])
            vhi = bass.AP(nvA.tensor, nvA.offset + 64 * 8 * 65 + h * 65, [[8 * 65, 64], [64 * 8 * 65, 255], [1, 65]])
            S.dma("sp", V[0:64, :, :], vlo, writes=[bV])
            S.dma("sp", V[64:128, 0:255, :], vhi, writes=[bV])
            R = 0
            while R < nrow:
                if 4 <= R and R + 3 <= nrow - 5:
                    rowpair(h, R, n % 2)
                    R += 4
                else:
                    row(h, R, nat_ws(R, nrow), (Ost[:, R, :], bOst), n % 2)
                    R += 1
                n += 1
            for k, R in enumerate(range(124, 132)):
                ws_b = 120 if R < 128 else 128
                row(h, R, ws_b, (OB[:, k, :], bOB), n % 2)
                n += 1
            S.op("dve", lambda e: e.tensor_copy(out=OA[:], in_=Ost[:, 124:132, :]), reads=[bOst], writes=[bOA])
            S.op("dve", lambda e: e.tensor_tensor(out=OB[:], in0=OB[:], in1=OA[:], op=ALU.subtract),
                 reads=[bOB, bOA], writes=[bOB])
            S.op("dve", lambda e: e.scalar_tensor_tensor(out=Ost[:, 124:132, :], in0=OB[:], scalar=fl[0:64, 0:1], in1=OA[:],
                                                         op0=ALU.mult, op1=ALU.add),
                 reads=[bOB, bOA, bfl], writes=[bOst])
            dst = bass.AP(natO.tensor, natO.offset + 512 + h * 64, [[1024, 64], [64 * 1024, 256], [1, 64]])
            S.dma("sp", dst, Ost[:], reads=[bOst])
        S.emit()


def phase_proj(nc, S, C, h_in, h_out, w_out, g_out, tokM, ntok, feats, wmap=None, tok_cols=None):
    nfeat = 8 - ntok
    with contextlib.ExitStack() as st:
        sb, ps = mk(st, nc)
        Wo = sb("Wo", [128, 8, D], BF16)
        gout = sb("gout", [128, D], F32)
        X = sb("X", [128, 4, D], F32)
        fT = sb("fT", [128, 2, 8, 512], BF16)
        tM = sb("tM", [128, 2, 4, max(ntok, 1) * 128], BF16)
        tmp = sb("tmp", [128, 2, 512], F32)
        junk = sb("junk", [128, 512], BF16)
        ss = sb("ss", [128, 8], F32)
        TP = ps("TP", [128, 8, 128], BF16)
        Dp = [ps("D%d" % i, [128, 512], F32) for i in range(4)]
        bW, bg = Buf(), Buf()
        bX = [Buf() for _ in range(4)]
        bfT = [Buf() for _ in range(2)]
        bfT2 = [[Buf() for _ in range(4)] for _ in range(2)]
        btM = [Buf() for _ in range(2)]
        btmp = [Buf() for _ in range(2)]
        bjunk = Buf()
        bss = [Buf() for _ in range(4)]
        bTP = Buf()
        bD = [Buf() for _ in range(4)]
        S.dma("sp", gout[:], bcast_rows(g_out, 128), writes=[bg])
        load_weight_bf16(S, Wo, bW, w_out, 8)
        for t in range(T // 512):
            b = t % 2
            cols = slice(t * 512, (t + 1) * 512)
            c_at = ntok
            for (fap, nch) in (feats or []):
                S.dma("sp", fT[:, b, c_at:c_at + nch, :], fap[:, cols].rearrange("(c p) t -> p c t", p=128), writes=[bfT[b]])
                c_at += nch
            if ntok:
                tsrc = tokM[t * 512:(t + 1) * 512, :] if tok_cols is None else tokM[t * 512:(t + 1) * 512, tok_cols[0]:tok_cols[1]]
                S.dma("sp", tM[:, b, :, :], tsrc.rearrange("(s p) f -> p s f", p=128), writes=[btM[b]])
            for s in range(4):
                r0 = t * 512 + s * 128
                S.dma("sp", X[:, s, :], h_in[r0:r0 + 128, :], writes=[bX[s]])
                for c in range(ntok):
                    S.op("pe", lambda e, s=s, c=c, b=b: e.transpose(out=TP[:, c, :], in_=tM[:, b, s, c * 128:(c + 1) * 128],
                                                                    identity=C.ident[:]),
                         reads=[btM[b], C.bident], writes=[bTP])
                if ntok:
                    S.op("act", lambda e, s=s, b=b: e.copy(out=fT[:, b, 0:ntok, s * 128:(s + 1) * 128], in_=TP[:, 0:ntok, :]),
                         reads=[bTP], writes=[bfT2[b][s]])
            for s in range(4):
                dis = []
                for half in range(2):
                    di = (2 * s + half) % 4
                    dis.append(di)
                    for c in range(8):
                        rd = [bW, bfT2[b][s]] if c < ntok else [bW, bfT[b]]
                        wc = c if wmap is None else wmap[c]
                        S.op("pe", lambda e, di=di, c=c, s=s, half=half, b=b, wc=wc: e.matmul(
                            Dp[di][:], lhsT=fT[:, b, c, s * 128:(s + 1) * 128], rhs=Wo[:, wc, half * 512:(half + 1) * 512],
                            start=(c == 0), stop=(c == 7)), reads=rd, writes=[bD[di]])
                    S.op("act", lambda e, di=di, s=s, half=half: e.activation(
                        out=junk[:], in_=Dp[di][:], func=AF.Square, accum_out=ss[:, 2 * s + half:2 * s + half + 1]),
                        reads=[bD[di]], writes=[bjunk, bss[s]])
                c0 = 2 * s
                S.op("dve", lambda e, c0=c0: e.tensor_tensor(out=ss[:, c0:c0 + 1], in0=ss[:, c0:c0 + 1],
                                                             in1=ss[:, c0 + 1:c0 + 2], op=ALU.add),
                     reads=[bss[s]], writes=[bss[s]])
                rstd_inplace(S, ss[:, c0:c0 + 1], bss[s], D)
                for half in range(2):
                    di = dis[half]
                    S.op("dve", lambda e, di=di, c0=c0, half=half: e.scalar_tensor_tensor(
                        out=tmp[:, half, :], in0=Dp[di][:], scalar=ss[:, c0:c0 + 1],
                        in1=gout[:, half * 512:(half + 1) * 512], op0=ALU.mult, op1=ALU.mult),
                        reads=[bD[di], bss[s], bg], writes=[btmp[half]])
                    S.op("pool", lambda e, s=s, half=half: e.tensor_tensor(
                        out=X[:, s, half * 512:(half + 1) * 512], in0=X[:, s, half * 512:(half + 1) * 512],
                        in1=tmp[:, half, :], op=ALU.add),
                        reads=[bX[s], btmp[half]], writes=[bX[s]])
                r0 = t * 512 + s * 128
                S.dma("pool", h_out[r0:r0 + 128, :], X[:, s, :], reads=[bX[s]])
        S.emit()


def norm_tile(nc, S, C, t, h_in, X, bX, xn, bxn, mT, bmT, TPm, bTPm, junk, bjunk, ss, bss, gin, bg):
    tb = t % 2
    for s in range(4):
        r0 = t * 512 + s * 128
        S.dma("sp", X[:, tb, s, :], h_in[r0:r0 + 128, :], writes=[bX[tb][s]])
        S.op("act", lambda e, s=s: e.activation(out=junk[:], in_=X[:, tb, s, :], func=AF.Square, accum_out=ss[:, tb, s:s + 1]),
             reads=[bX[tb][s]], writes=[bjunk, bss[tb]])
    rstd_inplace(S, ss[:, tb, 0:4], bss[tb], D)
    for s in range(4):
        S.op("dve", lambda e, s=s: e.scalar_tensor_tensor(out=xn[:, s % 2, :], in0=X[:, tb, s, :], scalar=ss[:, tb, s:s + 1],
                                                          in1=gin[:], op0=ALU.mult, op1=ALU.mult),
             reads=[bX[tb][s], bss[tb], bg], writes=[bxn[s % 2]])
        for kc in range(8):
            S.op("pe", lambda e, s=s, kc=kc: e.transpose(out=TPm[:, kc, :], in_=xn[:, s % 2, kc * 128:(kc + 1) * 128],
                                                         identity=C.ident[:]),
                 reads=[bxn[s % 2], C.bident], writes=[bTPm])
        S.op("act", lambda e, s=s: e.copy(out=mT[:, tb, :, s * 128:(s + 1) * 128], in_=TPm[:]), reads=[bTPm], writes=[bmT[tb][s]])


def phase_od_in(nc, S, C, h_in, g2, w_in, sel_d, uT, Ug):
    with contextlib.ExitStack() as st:
        sb, ps = mk(st, nc)
        Wod = sb("Wod", [128, 8, 1536], BF16)
        gin = sb("gin", [128, D], F32)
        X = sb("X", [128, 2, 4, D], F32)
        xn = sb("xn", [128, 2, D], BF16)
        mT = sb("mT", [128, 2, 8, 512], BF16)
        junk = sb("junk", [128, D], BF16)
        ss = sb("ss", [128, 2, 8], F32)
        sel = sb("sel", [128, 64, 128], BF16)
        suT = sb("suT", [128, 2, 512], BF16)
        Ust = sb("Ust", [128, 32, 256], BF16)
        ust = sb("ust", [128, 2, 512], BF16)
        sgm = sb("sgm", [128, 2, 512], F32)
        TPm = ps("TPm", [128, 8, 128], BF16)
        CA = [ps("CA%d" % i, [128, 512], F32) for i in range(2)]
        CG = [ps("CG%d" % i, [128, 512], F32) for i in range(2)]
        SU = ps("SU", [128, 512], F32)
        RG = [ps("RG%d" % i, [128, 8, 64], F32) for i in range(2)]
        bW, bg, bsel = Buf(), Buf(), Buf()
        bX = [[Buf() for _ in range(4)] for _ in range(2)]
        bxn = [Buf() for _ in range(2)]
        bmT = [[Buf() for _ in range(4)] for _ in range(2)]
        bjunk, bTPm, bSU, bUst = Buf(), Buf(), Buf(), Buf()
        bss = [Buf() for _ in range(2)]
        bCA = [Buf() for _ in range(2)]
        bCG = [Buf() for _ in range(2)]
        bRG = [Buf() for _ in range(2)]
        bsuT = [Buf() for _ in range(2)]
        bust = [Buf() for _ in range(2)]
        bsgm = [Buf() for _ in range(2)]
        S.dma("sp", gin[:], bcast_rows(g2, 128), writes=[bg])
        S.dma("sp", sel[:], sel_d, writes=[bsel])
        load_weight_bf16(S, Wod, bW, w_in, 8)
        nrg = 0
        for t in range(T // 512):
            norm_tile(nc, S, C, t, h_in, X, bX, xn, bxn, mT, bmT, TPm, bTPm, junk, bjunk, ss, bss, gin, bg)
            cols = slice(t * 512, (t + 1) * 512)
            tb = t % 2
            for cc in range(4):
                b = cc % 2
                for kc in range(8):
                    S.op("pe", lambda e, kc=kc, cc=cc, b=b, tb=tb: e.matmul(CA[b][:], lhsT=Wod[:, kc, cc * 128:(cc + 1) * 128],
                                                                     rhs=mT[:, tb, kc, :], start=(kc == 0), stop=(kc == 7)),
                         reads=bmT[tb] + [bW], writes=[bCA[b]])
                for kc in range(8):
                    S.op("pe", lambda e, kc=kc, cc=cc, b=b, tb=tb: e.matmul(CG[b][:], lhsT=Wod[:, kc, 512 + cc * 128:640 + cc * 128],
                                                                     rhs=mT[:, tb, kc, :], start=(kc == 0), stop=(kc == 7)),
                         reads=bmT[tb] + [bW], writes=[bCG[b]])
                S.op("act", lambda e, b=b: e.activation(out=sgm[:, b, :], in_=CG[b][:], func=AF.Sigmoid),
                     reads=[bCG[b]], writes=[bsgm[b]])
                S.op("dve", lambda e, b=b: e.tensor_tensor(out=ust[:, b, :], in0=CA[b][:], in1=sgm[:, b, :], op=ALU.mult),
                     reads=[bCA[b], bsgm[b]], writes=[bust[b]])
                S.dma("pool", uT[cc * 128:(cc + 1) * 128, cols], ust[:, b, :], reads=[bust[b]])
            for cc in range(4):
                b = cc % 2
                for kc in range(8):
                    S.op("pe", lambda e, kc=kc, cc=cc, tb=tb: e.matmul(SU[:], lhsT=Wod[:, kc, 1024 + cc * 128:1152 + cc * 128],
                                                                rhs=mT[:, tb, kc, :], start=(kc == 0), stop=(kc == 7)),
                         reads=bmT[tb] + [bW], writes=[bSU])
                S.op("act", lambda e, b=b: e.copy(out=bc_ap(suT[:, b, :], [[64, 8], [1, 64]]), in_=bc_ap(SU[:], [[1, 8], [8, 64]])),
                     reads=[bSU], writes=[bsuT[b]])
                rb = nrg % 2
                nrg += 1
                base = suT[:, b, :]
                for g1 in range(8):
                    for j in range(8):
                        rhs = bass.AP(base.tensor, base.offset + j * 64, [list(base.ap[0]), [1, 64]])
                        S.op("pe", lambda e, g1=g1, j=j, rhs=rhs, rb=rb: e.matmul(RG[rb][:, g1, :], lhsT=sel[:, g1 * 8 + j, :],
                                                                                  rhs=rhs, start=(j == 0), stop=(j == 7)),
                             reads=[bsuT[b], bsel], writes=[bRG[rb]])
                k0 = (t % 4) * 64
                S.op("dve", lambda e, cc=cc, rb=rb, k0=k0: e.tensor_copy(out=Ust[:, cc * 8:(cc + 1) * 8, k0:k0 + 64], in_=RG[rb][:]),
                     reads=[bRG[rb]], writes=[bUst])
            if t % 4 == 3:
                kk = (t // 4) * 256
                S.dma("pool", Ug[:, :, kk:kk + 256].rearrange("g p k -> p g k"), Ust[:], reads=[bUst])
        S.emit()


def conv_gen(nc, S, C, sb, ps, uT, dw_w, dw_b, ln_g, ln_b, keep_d, ident32_d, convT, acc_bufs=2):
    if True:
        Uw = sb("Uw", [128, 3, 542], BF16)
        Wdg = sb("Wdg", [128, 4, 31, 128], BF16)
        acc = sb("acc", [128, acc_bufs, 4, 512], F32)
        sq = sb("sq", [128, 4, 512], F32)
        wt = sb("wt", [128, 4, 31], F32)
        bi = sb("bi", [128, 4], F32)
        lg = sb("lg", [128, 4], F32)
        lb = sb("lb", [128, 4], F32)
        keep = sb("keep", [128, 1], F32)
        id32 = sb("id32", [128, 128], F32)
        onesc = sb("onesc", [128, 128], F32)
        m1s = sb("m1s", [128, 512], F32)
        rs = sb("rs", [128, 512], F32)
        tt_ = sb("tt", [128, 512], F32)
        yn = sb("yn", [128, 2, 512], F32)
        cst = sb("cst", [128, 2, 512], BF16)
        CV = [ps("CV%d" % i, [128, 512], F32) for i in range(2)]
        M1 = ps("M1", [128, 512], F32)
        M2 = ps("M2", [128, 512], F32)
        bCV = [Buf() for _ in range(2)]
        bUw = [Buf() for _ in range(3)]
        bacc = [[Buf() for _ in range(4)] for _ in range(acc_bufs)]
        bsq = [Buf() for _ in range(4)]
        bc = Buf()
        bM1, bM2, bm1s, brs, btt = Buf(), Buf(), Buf(), Buf(), Buf()
        byn = [Buf() for _ in range(2)]
        bcst = [Buf() for _ in range(2)]
        for cc in range(4):
            S.dma("sp", wt[:, cc, :], dw_w[:, cc * 128:(cc + 1) * 128].rearrange("k p -> p k"), writes=[bc],
                  allow_slow_non_contiguous=True)
        S.dma("sp", bi[:], dw_b.rearrange("(c p) -> p c", p=128), writes=[bc], allow_slow_non_contiguous=True)
        S.dma("sp", lg[:], ln_g.rearrange("(c p) -> p c", p=128), writes=[bc], allow_slow_non_contiguous=True)
        S.dma("sp", lb[:], ln_b.rearrange("(c p) -> p c", p=128), writes=[bc], allow_slow_non_contiguous=True)
        S.dma("sp", keep[:], keep_d, writes=[bc])
        S.dma("sp", id32[:], ident32_d, writes=[bc])
        S.op("pool", lambda e: e.memset(onesc[:], 1.0 / 512), writes=[bc])
        for cc in range(4):
            for k in range(31):
                S.op("dve", lambda e, cc=cc, k=k: e.tensor_scalar(out=Wdg[:, cc, k, :], in0=id32[:], scalar1=wt[:, cc, k:k + 1],
                                                                  scalar2=None, op0=ALU.mult), reads=[bc], writes=[bc])
        n = 0
        ny = 0
        NT = T // 512
        for t in range(NT):
            ab = t % acc_bufs
            for cc in range(4):
                b = n % 3
                cb = n % 2
                n += 1
                lo = t * 512 - 15
                hi = t * 512 + 527
                o0 = 0
                if t == 0:
                    S.op("pool", lambda e, b=b: e.memset(Uw[:, b, 0:15], 0.0), writes=[bUw[b]])
                    o0, lo = 15, 0
                o1 = 542
                if t == NT - 1:
                    S.op("pool", lambda e, b=b: e.memset(Uw[:, b, 527:542], 0.0), writes=[bUw[b]])
                    o1, hi = 527, T
                S.dma("sp", Uw[:, b, o0:o1], uT[cc * 128:(cc + 1) * 128, lo:hi], writes=[bUw[b]])
                if t == NT // 2 - 1:
                    S.op("pool", lambda e, b=b: e.tensor_scalar(out=Uw[:, b, 527:542], in0=Uw[:, b, 527:542], scalar1=keep[:, 0:1],
                                                                scalar2=None, op0=ALU.mult), reads=[bUw[b], bc], writes=[bUw[b]])
                if t == NT // 2:
                    S.op("pool", lambda e, b=b: e.tensor_scalar(out=Uw[:, b, 0:15], in0=Uw[:, b, 0:15], scalar1=keep[:, 0:1],
                                                                scalar2=None, op0=ALU.mult), reads=[bUw[b], bc], writes=[bUw[b]])
                for k in range(31):
                    S.op("pe", lambda e, b=b, cc=cc, k=k, cb=cb: e.matmul(CV[cb][:], lhsT=Wdg[:, cc, k, :], rhs=Uw[:, b, k:k + 512],
                                                                          start=(k == 0), stop=(k == 30)),
                         reads=[bUw[b], bc], writes=[bCV[cb]])
                S.op("act", lambda e, cc=cc, cb=cb, ab=ab: e.activation(out=acc[:, ab, cc, :], in_=CV[cb][:], func=AF.Identity,
                                                                        bias=bi[:, cc:cc + 1]),
                     reads=[bCV[cb], bc], writes=[bacc[ab][cc]])
                S.op("act", lambda e, cc=cc, ab=ab: e.activation(out=sq[:, cc, :], in_=acc[:, ab, cc, :], func=AF.Square),
                     reads=[bacc[ab][cc]], writes=[bsq[cc]])
            for cc in range(4):
                S.op("pe", lambda e, cc=cc, ab=ab: e.matmul(M1[:], lhsT=onesc[:], rhs=acc[:, ab, cc, :], start=(cc == 0), stop=(cc == 3)),
                     reads=[bacc[ab][cc], bc], writes=[bM1])
            for cc in range(4):
                S.op("pe", lambda e, cc=cc: e.matmul(M2[:], lhsT=onesc[:], rhs=sq[:, cc, :], start=(cc == 0), stop=(cc == 3)),
                     reads=[bsq[cc], bc], writes=[bM2])
            S.op("act", lambda e: e.copy(out=m1s[:], in_=M1[:]), reads=[bM1], writes=[bm1s])
            S.op("dve", lambda e: e.tensor_tensor(out=tt_[:], in0=m1s[:], in1=m1s[:], op=ALU.mult), reads=[bm1s], writes=[btt])
            S.op("dve", lambda e: e.tensor_tensor(out=rs[:], in0=M2[:], in1=tt_[:], op=ALU.subtract), reads=[bM2, btt], writes=[brs])
            S.op("dve", lambda e: e.tensor_scalar(out=rs[:], in0=rs[:], scalar1=EPS, scalar2=None, op0=ALU.add), reads=[brs], writes=[brs])
            S.op("act", lambda e: e.activation(out=rs[:], in_=rs[:], func=AF.Sqrt), reads=[brs], writes=[brs])
            S.op("dve", lambda e: e.reciprocal(out=rs[:], in_=rs[:]), reads=[brs], writes=[brs])
            for cc in range(4):
                yb = ny % 2
                ny += 1
                S.op("dve", lambda e, cc=cc, yb=yb, ab=ab: e.tensor_tensor(out=yn[:, yb, :], in0=acc[:, ab, cc, :], in1=m1s[:], op=ALU.subtract),
                     reads=[bacc[ab][cc], bm1s], writes=[byn[yb]])
                S.op("dve", lambda e, yb=yb: e.tensor_tensor(out=yn[:, yb, :], in0=yn[:, yb, :], in1=rs[:], op=ALU.mult),
                     reads=[byn[yb], brs], writes=[byn[yb]])
                S.op("act", lambda e, cc=cc, yb=yb: e.activation(out=cst[:, yb, :], in_=yn[:, yb, :], func=AF.Silu,
                                                                 bias=lb[:, cc:cc + 1], scale=lg[:, cc:cc + 1]),
                     reads=[byn[yb], bc], writes=[bcst[yb]])
                S.dma("pool", convT[cc * 128:(cc + 1) * 128, t * 512:(t + 1) * 512], cst[:, yb, :], reads=[bcst[yb]])
            yield t


def phase_conv(nc, S, C, *args):
    with contextlib.ExitStack() as st:
        sb, ps = mk(st, nc)
        for _ in conv_gen(nc, S, C, sb, ps, *args):
            pass
        S.emit()


class El:
    def __init__(self, S, buf, eng="dve"):
        self.S, self.b, self.eng = S, buf, eng

    def tt(self, out, a, b, op, extra=()):
        self.S.op(self.eng, lambda e: e.tensor_tensor(out=out, in0=a, in1=b, op=op), reads=[self.b] + list(extra), writes=[self.b])

    def ts(self, out, a, s1, op0, s2=None, op1=None, extra=()):
        if op1 is None:
            self.S.op(self.eng, lambda e: e.tensor_scalar(out=out, in0=a, scalar1=s1, scalar2=None, op0=op0),
                      reads=[self.b] + list(extra), writes=[self.b])
        else:
            self.S.op(self.eng, lambda e: e.tensor_scalar(out=out, in0=a, scalar1=s1, scalar2=s2, op0=op0, op1=op1),
                      reads=[self.b] + list(extra), writes=[self.b])

    def stt(self, out, a, sc, b, op0, op1, extra=()):
        self.S.op(self.eng, lambda e: e.scalar_tensor_tensor(out=out, in0=a, scalar=sc, in1=b, op0=op0, op1=op1),
                  reads=[self.b] + list(extra), writes=[self.b])

    def cp(self, out, a, extra=()):
        self.S.op(self.eng, lambda e: e.tensor_copy(out=out, in_=a), reads=[self.b] + list(extra), writes=[self.b])

    def ms(self, out, val):
        self.S.op(self.eng, lambda e: e.memset(out, val), writes=[self.b])

    def cmul(self, o_r, o_i, ar, ai, br, bi, t1, t2):
        self.tt(t1, ar, br, ALU.mult)
        self.tt(t2, ai, bi, ALU.mult)
        self.tt(o_r, t1, t2, ALU.subtract)
        self.tt(t1, ar, bi, ALU.mult)
        self.tt(t2, ai, br, ALU.mult)
        self.tt(o_i, t1, t2, ALU.add)

    def exp_taylor(self, out, x, deg):
        self.ts(out, x, 1.0 / deg, ALU.mult, 1.0, ALU.add)
        for n in range(deg - 1, 0, -1):
            self.tt(out, out, x, ALU.mult)
            self.ts(out, out, 1.0 / n, ALU.mult, 1.0, ALU.add)


def phase_s5_prep(nc, S, C, I, Pp):
    lam_re, lam_im, log_step = I["s5_lambda_re"][0], I["s5_lambda_im"][0], I["s5_log_step"][0]
    b_re, b_im, c_re, c_im, d_skip = I["s5_b_re"][0], I["s5_b_im"][0], I["s5_c_re"][0], I["s5_c_im"][0], I["s5_d"][0]
    with contextlib.ExitStack() as st:
        sb, ps = mk(st, nc)
        A = sb("A", [128, 48, 32], F32)
        LPr = sb("LPr", [128, 16, 32], F32)
        LPi = sb("LPi", [128, 16, 32], F32)
        Bre = sb("Bre", [128, 32, 16], F32)
        Bim = sb("Bim", [128, 32, 16], F32)
        Bbr = sb("Bbr", [128, 32, 16], F32)
        Bbi = sb("Bbi", [128, 32, 16], F32)
        Cre = sb("Cre", [128, 32, 16], F32)
        Cim = sb("Cim", [128, 32, 16], F32)
        MBr = sb("MBr", [128, 32, 8, 16], F32)
        MBi = sb("MBi", [128, 32, 8, 16], F32)
        MKr = sb("MKr", [128, 32, 8, 16], F32)
        MKi = sb("MKi", [128, 32, 8, 16], F32)
        t1 = sb("t1", [128, 16, 16], F32)
        t2 = sb("t2", [128, 16, 16], F32)
        mf = sb("mf", [128, 128], F32)
        mbk = sb("mbk", [128, 128], F32)
        id32 = sb("id32", [128, 128], F32)
        dall = sb("dall", [128, 32], F32)
        Kt = sb("Kt", [128, 128], F32)
        Kt2 = sb("Kt2", [128, 128], F32)
        KP = [ps("KP%d" % i, [128, 128], F32) for i in range(2)]
        TPp = [ps("TPp%d" % i, [128, 4, 128], F32) for i in range(2)]
        bp = Buf()
        bKP = [Buf() for _ in range(2)]
        bTPp = [Buf() for _ in range(2)]
        bKt = Buf()
        E = El(S, bp)
        nslot = [0]
        names = {}

        def v(name):
            if name not in names:
                names[name] = nslot[0]
                nslot[0] += 1
                assert nslot[0] <= 48
            return A[:, names[name], :]
        for a in range(2):
            ps_ = slice(a * 64, (a + 1) * 64)
            for dd in range(2):
                for nm, src in (("lr", lam_re), ("li", lam_im)):
                    dst = v(nm)[ps_, dd * 16:(dd + 1) * 16]
                    sap = bass.AP(src.tensor, src.offset + dd * 32 * 64 + a * 64, [[1, 64], [2 * 64, 16]])
                    S.dma("sp", dst, sap, writes=[bp], allow_slow_non_contiguous=True)
                dst = v("ls")[ps_, dd * 16:(dd + 1) * 16]
                sap = bass.AP(log_step.tensor, log_step.offset + dd * 32 + a, [[0, 64], [2, 16]])
                S.dma("sp", dst, sap, writes=[bp], allow_slow_non_contiguous=True)
            for dstT, src in ((Bre, b_re), (Bim, b_im)):
                for dd in range(2):
                    dst4 = dstT[ps_, dd * 16:(dd + 1) * 16, :]
                    sap = bass.AP(src.tensor, src.offset + dd * 32 * 64 * 16 + a * 64 * 16, [[16, 64], [2 * 64 * 16, 16], [1, 16]])
                    S.dma("sp", dst4, sap, writes=[bp], allow_slow_non_contiguous=True)
            for dstT, src in ((Cre, c_re), (Cim, c_im)):
                for dd in range(2):
                    for gp in range(16):
                        dst4 = dstT[ps_, dd * 16 + gp, :]
                        sap = bass.AP(src.tensor, src.offset + dd * 32 * 16 * 64 + gp * 2 * 16 * 64 + a * 16 * 64, [[1, 64], [64, 16]])
                        S.dma("sp", dst4, sap, writes=[bp], allow_slow_non_contiguous=True)
        S.dma("sp", mf[:], I["maskf"], writes=[bp])
        S.dma("sp", mbk[:], I["maskb"], writes=[bp])
        S.dma("sp", id32[:], I["ident32"], writes=[bp])
        S.dma("sp", Pp.keep[:], I["segkeep"], writes=[Pp.bk])
        dsrc = d_skip.rearrange("(g c) -> c g", c=16)
        for j in range(8):
            S.dma("sp", dall[j * 16:(j + 1) * 16, :], dsrc, writes=[bp], allow_slow_non_contiguous=True)
        if int(os.environ.get('S5_STOP', '9')) == 1:
            S.emit()
            return
        lr, li, ls = v("lr"), v("li"), v("ls")
        x, dt = v("x"), v("dt")
        E.ts(x, ls, 0.125, ALU.mult)
        E.exp_taylor(dt, x, 11)
        for _ in range(3):
            E.tt(dt, dt, dt, ALU.mult)
        a_, th = v("a"), v("th")
        E.tt(a_, lr, dt, ALU.mult)
        E.tt(th, li, dt, ALU.mult)
        ea, eam, na = v("ea"), v("eam"), v("na")
        E.exp_taylor(ea, a_, 8)
        E.ts(na, a_, -1.0, ALU.mult)
        E.exp_taylor(eam, na, 8)
        x2, p, sn, cs = v("x2"), v("p"), v("sn"), v("cs")
        E.ts(x, th, 1.0 / 32, ALU.mult)
        E.tt(x2, x, x, ALU.mult)
        E.ts(p, x2, -1.0 / 110, ALU.mult, 1.0, ALU.add)
        for m in (72.0, 42.0, 20.0, 6.0):
            E.tt(p, p, x2, ALU.mult)
            E.ts(p, p, -1.0 / m, ALU.mult, 1.0, ALU.add)
        E.tt(sn, p, x, ALU.mult)
        E.ts(p, x2, -1.0 / 132, ALU.mult, 1.0, ALU.add)
        for m in (90.0, 56.0, 30.0, 12.0, 2.0):
            E.tt(p, p, x2, ALU.mult)
            E.ts(p, p, -1.0 / m, ALU.mult, 1.0, ALU.add)
        E.cp(cs, p)
        cc_, ss_, q = v("cc"), v("ss2"), v("q")
        for _ in range(5):
            E.tt(cc_, cs, cs, ALU.mult)
            E.tt(ss_, sn, sn, ALU.mult)
            E.tt(q, cs, sn, ALU.mult)
            E.tt(cs, cc_, ss_, ALU.subtract)
            E.ts(sn, q, 2.0, ALU.mult)
        E.tt(cc_, cs, cs, ALU.mult)
        E.tt(ss_, sn, sn, ALU.mult)
        E.tt(q, cc_, ss_, ALU.add)
        E.ts(q, q, -1.0, ALU.add)
        E.ts(p, q, 0.375, ALU.mult, -0.5, ALU.add)
        E.tt(p, p, q, ALU.mult)
        E.ts(p, p, 1.0, ALU.add)
        E.tt(cs, cs, p, ALU.mult)
        E.tt(sn, sn, p, ALU.mult)
        lbr, lbi, lir, lii = v("lbr"), v("lbi"), v("lir"), v("lii")
        E.tt(lbr, ea, cs, ALU.mult)
        E.tt(lbi, ea, sn, ALU.mult)
        E.tt(lir, eam, cs, ALU.mult)
        E.tt(lii, eam, sn, ALU.mult)
        E.ts(lii, lii, -1.0, ALU.mult)
        den, m1, nr, ni, cr, ci, u1 = v("den"), v("m1"), v("nr"), v("ni"), v("cr"), v("ci"), v("u1")
        E.tt(den, lr, lr, ALU.mult)
        E.tt(u1, li, li, ALU.mult)
        E.tt(den, den, u1, ALU.add)
        S.op("dve", lambda e: e.reciprocal(out=den, in_=den), reads=[bp], writes=[bp])
        E.ts(m1, lbr, -1.0, ALU.add)
        E.tt(nr, m1, lr, ALU.mult)
        E.tt(u1, lbi, li, ALU.mult)
        E.tt(nr, nr, u1, ALU.add)
        E.tt(ni, lbi, lr, ALU.mult)
        E.tt(u1, m1, li, ALU.mult)
        E.tt(ni, ni, u1, ALU.subtract)
        E.tt(cr, nr, den, ALU.mult)
        E.tt(ci, ni, den, ALU.mult)
        E.ms(LPr[:, 7, :], 1.0)
        E.ms(LPi[:, 7, :], 0.0)
        w1, w2 = v("w1"), v("w2")
        for n in range(1, 9):
            E.cmul(LPr[:, 7 + n, :], LPi[:, 7 + n, :], LPr[:, 6 + n, :], LPi[:, 6 + n, :], lbr, lbi, w1, w2)
        for n in range(1, 8):
            E.cmul(LPr[:, 7 - n, :], LPi[:, 7 - n, :], LPr[:, 8 - n, :], LPi[:, 8 - n, :], lir, lii, w1, w2)
        E.cp(Pp.LamR[:, 0, :], LPr[:, 15, :], extra=[Pp.bL])
        E.cp(Pp.LamI[:, 0, :], LPi[:, 15, :], extra=[Pp.bL])
        for l in range(1, 10):
            E.cmul(Pp.LamR[:, l, :], Pp.LamI[:, l, :], Pp.LamR[:, l - 1, :], Pp.LamI[:, l - 1, :],
                   Pp.LamR[:, l - 1, :], Pp.LamI[:, l - 1, :], w1, w2)
        S.op("dve", lambda e: e.tensor_scalar(out=Pp.LamN[:], in0=Pp.LamI[:], scalar1=-1.0, scalar2=None, op0=ALU.mult),
             reads=[bp], writes=[bp, Pp.bL])
        if int(os.environ.get('S5_STOP', '9')) == 2:
            S.emit()
            return
        crb = bc_ap(cr, [[1, 32], [0, 16]])
        cib = bc_ap(ci, [[1, 32], [0, 16]])
        T1 = sb("T1", [128, 32, 16], F32)
        T2 = sb("T2", [128, 32, 16], F32)
        E.tt(T1[:], crb, Bre[:], ALU.mult)
        E.tt(T2[:], cib, Bim[:], ALU.mult)
        E.tt(Bbr[:], T1[:], T2[:], ALU.subtract)
        E.tt(T1[:], crb, Bim[:], ALU.mult)
        E.tt(T2[:], cib, Bre[:], ALU.mult)
        E.tt(Bbi[:], T1[:], T2[:], ALU.add)
        for d in range(2):
            dsl = slice(d * 16, (d + 1) * 16)
            for j in range(8):
                def lp(n):
                    return (bc_ap(LPr[:, n + 7, dsl], [[1, 16], [0, 16]]), bc_ap(LPi[:, n + 7, dsl], [[1, 16], [0, 16]]))
                pr_, pi_ = lp(7 - j if d == 0 else j)
                E.cmul(MBr[:, dsl, j, :], MBi[:, dsl, j, :], pr_, pi_, Bbr[:, dsl, :], Bbi[:, dsl, :], t1[:], t2[:])
                pr_, pi_ = lp(j + 1 if d == 0 else 8 - j)
                E.tt(t1[:], pr_, Cre[:, dsl, :], ALU.mult)
                E.tt(t2[:], pi_, Cim[:, dsl, :], ALU.mult)
                E.tt(Pp.MCr[:, dsl, j, :], t1[:], t2[:], ALU.subtract, extra=[Pp.bM])
                E.tt(t1[:], pr_, Cim[:, dsl, :], ALU.mult)
                E.tt(t2[:], pi_, Cre[:, dsl, :], ALU.mult)
                E.stt(Pp.MCn[:, dsl, j, :], t1[:], -1.0, t2[:], ALU.mult, ALU.subtract, extra=[Pp.bM])
                pr_, pi_ = lp(j - 7 if d == 0 else -j)
                E.tt(t1[:], pr_, Cre[:, dsl, :], ALU.mult)
                E.tt(t2[:], pi_, Cim[:, dsl, :], ALU.mult)
                E.tt(MKr[:, dsl, j, :], t1[:], t2[:], ALU.subtract)
                E.tt(t1[:], pr_, Cim[:, dsl, :], ALU.mult)
                E.tt(t2[:], pi_, Cre[:, dsl, :], ALU.mult)
                E.stt(MKi[:, dsl, j, :], t1[:], -1.0, t2[:], ALU.mult, ALU.subtract)
        if int(os.environ.get('S5_STOP', '9')) == 3:
            S.emit()
            return
        for g in range(32):
            gp, a = g // 2, g % 2
            ps_ = slice(a * 64, (a + 1) * 64)
            for d in range(2):
                q_ = d * 16 + gp
                S.op("pe", lambda e, d=d, q_=q_, ps_=ps_: e.matmul(
                    KP[d][:], lhsT=MBr[ps_, q_, :, :].rearrange("p j c -> p (j c)"),
                    rhs=MKr[ps_, q_, :, :].rearrange("p j c -> p (j c)"), start=True, stop=False),
                    reads=[bp], writes=[bKP[d]])
                S.op("pe", lambda e, d=d, q_=q_, ps_=ps_: e.matmul(
                    KP[d][:], lhsT=MBi[ps_, q_, :, :].rearrange("p j c -> p (j c)"),
                    rhs=MKi[ps_, q_, :, :].rearrange("p j c -> p (j c)"), start=False, stop=True),
                    reads=[bp], writes=[bKP[d]])
            S.op("dve", lambda e: e.tensor_tensor(out=Kt[:], in0=KP[0][:], in1=mf[:], op=ALU.mult), reads=[bKP[0], bp], writes=[bKt])
            S.op("dve", lambda e: e.tensor_tensor(out=Kt2[:], in0=KP[1][:], in1=mbk[:], op=ALU.mult), reads=[bKP[1], bp], writes=[bKt])
            S.op("dve", lambda e: e.tensor_tensor(out=Kt[:], in0=Kt[:], in1=Kt2[:], op=ALU.add), reads=[bKt], writes=[bKt])
            S.op("dve", lambda e, g=g: e.scalar_tensor_tensor(out=Pp.Kin[:, g, :], in0=id32[:], scalar=dall[:, g:g + 1], in1=Kt[:],
                                                              op0=ALU.mult, op1=ALU.add), reads=[bKt, bp], writes=[Pp.bM])
        if int(os.environ.get('S5_STOP', '9')) == 4:
            S.emit()
            return
        nb = 0
        for q_ in range(32):
            tb = nb % 2
            for ri, M in enumerate((MBr, MBi)):
                k = (q_ % 2) * 2 + ri
                S.op("pe", lambda e, M=M, q_=q_, k=k, tb=tb: e.transpose(
                    out=TPp[tb][:, k, :], in_=M[:, q_, :, :].rearrange("p j c -> p (j c)"), identity=id32[:]),
                    reads=[bp], writes=[bTPp[tb]])
            if q_ % 2 == 1:
                s0 = (q_ - 1) * 2
                S.op("act", lambda e, s0=s0, tb=tb: e.copy(out=Pp.MBT[:, s0:s0 + 4, :], in_=TPp[tb][:]),
                     reads=[bTPp[tb]], writes=[Pp.bM])
                nb += 1
        S.emit()


def phase_s5_main(nc, S, C, Pp, Ug, Yg, bg_factory=None):
    with contextlib.ExitStack() as st:
        sb, ps = mk(st, nc)
        bg = bg_factory(sb, ps) if bg_factory is not None else None
        nbuf = 1 if bg is not None else 2
        U = sb("U", [128, nbuf, 2, 2048], BF16)
        AB = sb("AB", [128, 2, 2, 2048], F32)
        Xb = sb("Xb", [128, 2, 2, 2049], BF16)
        Yst = sb("Yst", [128, 2, 512], BF16)
        tv = sb("tv", [128, 4], F32)
        PSs = [ps("PSs%d" % i, [128, 512], F32) for i in range(2 * nbuf)]
        PY = [ps("PY%d" % i, [128, 512], F32) for i in range(2)]
        bU = [Buf() for _ in range(2)]
        bA = [[Buf() for _ in range(2)] for _ in range(2)]
        bXb = [[Buf() for _ in range(2)] for _ in range(2)]
        bYst = [Buf() for _ in range(2)]
        btv = Buf()
        bPS = [Buf() for _ in range(2 * nbuf)]
        bPY = [Buf() for _ in range(2)]
        S.op("pool", lambda e: e.memset(Xb[:, 0, :, 0:1], 0.0), writes=[bXb[0][0], bXb[0][1]])
        S.op("pool", lambda e: e.memset(Xb[:, 1, :, 2048:2049], 0.0), writes=[bXb[1][0], bXb[1][1]])
        nps = 0
        npy = 0
        H = 1024

        def hs_scan(d, col, lo):
            base = [AB[:, 0, ri, lo:lo + H] for ri in range(2)]

            def view(ri, off, step, cnt):
                a = base[ri]
                return bass.AP(a.tensor, a.offset + off, [list(a.ap[0]), [step, cnt]])

            def update(l, doff, soff, step, cnt):
                ar = Pp.LamR[:, l, col:col + 1]
                ai = Pp.LamI[:, l, col:col + 1]
                an = Pp.LamN[:, l, col:col + 1]
                dr_, di_ = view(0, doff, step, cnt), view(1, doff, step, cnt)
                sr_, si_ = view(0, soff, step, cnt), view(1, soff, step, cnt)
                a0 = base[0]
                d2 = bass.AP(a0.tensor, a0.offset + doff, [list(a0.ap[0]), [2048, 2], [step, cnt]])
                s2 = bass.AP(a0.tensor, a0.offset + soff, [list(a0.ap[0]), [2048, 2], [step, cnt]])
                rd = [bA[0][0], bA[0][1], Pp.bL]
                S.op("dve", lambda e: e.scalar_tensor_tensor(out=d2, in0=s2, scalar=ar, in1=d2, op0=ALU.mult, op1=ALU.add),
                     reads=rd, writes=[bA[0][0], bA[0][1]])
                S.op("dve", lambda e: e.scalar_tensor_tensor(out=dr_, in0=si_, scalar=an, in1=dr_, op0=ALU.mult, op1=ALU.add),
                     reads=rd, writes=[bA[0][0]])
                S.op("dve", lambda e: e.scalar_tensor_tensor(out=di_, in0=sr_, scalar=ai, in1=di_, op0=ALU.mult, op1=ALU.add),
                     reads=[bA[0][0], bA[0][1], Pp.bL], writes=[bA[0][1]])
            for l in range(10):
                st2 = 1 << (l + 1)
                s_ = 1 << l
                cnt = H // st2
                if d == 0:
                    update(l, st2 - 1, s_ - 1, st2, cnt)
                else:
                    update(l, 0, s_, st2, cnt)
            for l in range(8, -1, -1):
                st2 = 1 << (l + 1)
                s_ = 1 << l
                cnt = H // st2 - 1
                if d == 0:
                    update(l, st2 - 1 + s_, st2 - 1, st2, cnt)
                else:
                    update(l, s_, st2, st2, cnt)

        def inject(d, col):
            fr, to = (H - 1, H) if d == 0 else (H, H - 1)
            ar = Pp.LamR[:, 0, col:col + 1]
            ai = Pp.LamI[:, 0, col:col + 1]
            an = Pp.LamN[:, 0, col:col + 1]
            er, ei = AB[:, 0, 0, fr:fr + 1], AB[:, 0, 1, fr:fr + 1]
            rd = [bA[0][0], bA[0][1], Pp.bL, btv, Pp.bk]
            wr = [btv]
            S.op("dve", lambda e: e.tensor_scalar(out=tv[:, 0:1], in0=er, scalar1=ar, scalar2=None, op0=ALU.mult), reads=rd, writes=wr)
            S.op("dve", lambda e: e.scalar_tensor_tensor(out=tv[:, 0:1], in0=ei, scalar=an, in1=tv[:, 0:1], op0=ALU.mult, op1=ALU.add),
                 reads=rd, writes=wr)
            S.op("dve", lambda e: e.tensor_scalar(out=tv[:, 1:2], in0=ei, scalar1=ar, scalar2=None, op0=ALU.mult), reads=rd, writes=wr)
            S.op("dve", lambda e: e.scalar_tensor_tensor(out=tv[:, 1:2], in0=er, scalar=ai, in1=tv[:, 1:2], op0=ALU.mult, op1=ALU.add),
                 reads=rd, writes=wr)
            for ri in range(2):
                S.op("dve", lambda e, ri=ri: e.scalar_tensor_tensor(out=AB[:, 0, ri, to:to + 1], in0=tv[:, ri:ri + 1], scalar=Pp.keep[:, 0:1],
                                                                     in1=AB[:, 0, ri, to:to + 1], op0=ALU.mult, op1=ALU.add),
                     reads=rd, writes=[bA[0][ri]])

        for gp in range(16):
            ub = gp % nbuf
            for a in range(2):
                S.dma("sp", U[:, ub, a, :], Ug[2 * gp + a], writes=[bU[ub]])
            for d in range(2):
                col = d * 16 + gp
                for kb in range(4):
                    ks = slice(kb * 512, (kb + 1) * 512)
                    for ri in range(2):
                        pi = nps % (2 * nbuf)
                        nps += 1
                        for a in range(2):
                            slot = col * 2 + ri
                            S.op("pe", lambda e, pi=pi, a=a, slot=slot, ks=ks, ub=ub: e.matmul(
                                PSs[pi][a * 64:(a + 1) * 64, :], lhsT=Pp.MBT[:, slot, a * 64:(a + 1) * 64], rhs=U[:, ub, a, ks],
                                start=True, stop=True),
                                reads=[Pp.bM, bU[ub]], writes=[bPS[pi]])
                        if ri == 0:
                            S.op("act", lambda e, pi=pi, ri=ri, ks=ks: e.copy(out=AB[:, 0, ri, ks], in_=PSs[pi][:]),
                                 reads=[bPS[pi]], writes=[bA[0][ri]])
                        else:
                            S.op("dve", lambda e, pi=pi, ri=ri, ks=ks: e.tensor_copy(out=AB[:, 0, ri, ks], in_=PSs[pi][:]),
                                 reads=[bPS[pi]], writes=[bA[0][ri]])
                if d == 0:
                    hs_scan(0, col, 0)
                    inject(0, col)
                    hs_scan(0, col, H)
                else:
                    hs_scan(1, col, H)
                    inject(1, col)
                    hs_scan(1, col, 0)
                if bg is not None:
                    next(bg, None)
                off = 1 if d == 0 else 0
                for ri in range(2):
                    S.op("act", lambda e, d=d, ri=ri, off=off: e.copy(out=Xb[:, d, ri, off:off + 2048], in_=AB[:, 0, ri, :]),
                         reads=[bA[0][ri]], writes=[bXb[d][ri]])
                    S.op("pool", lambda e, d=d, ri=ri: e.tensor_scalar(out=Xb[:, d, ri, H:H + 1], in0=Xb[:, d, ri, H:H + 1],
                                                                       scalar1=Pp.keep[:, 0:1], scalar2=None, op0=ALU.mult),
                         reads=[bXb[d][ri], Pp.bk], writes=[bXb[d][ri]])
            for a in range(2):
                g = 2 * gp + a
                ps_ = slice(a * 64, (a + 1) * 64)
                for kb in range(4):
                    ks = slice(kb * 512, (kb + 1) * 512)
                    yi = npy % 2
                    npy += 1
                    S.op("pe", lambda e, yi=yi, g=g, a=a, ks=ks, ub=ub: e.matmul(PY[yi][:], lhsT=Pp.Kin[:, g, :], rhs=U[:, ub, a, ks],
                                                                                 start=True, stop=False),
                         reads=[Pp.bM, bU[ub]], writes=[bPY[yi]])
                    for d in range(2):
                        q_ = d * 16 + gp
                        xs = slice(kb * 512 + d, kb * 512 + d + 512)
                        S.op("pe", lambda e, yi=yi, q_=q_, ps_=ps_, d=d, xs=xs: e.matmul(
                            PY[yi][:], lhsT=Pp.MCr[ps_, q_, :, :].rearrange("p j c -> p (j c)"), rhs=Xb[ps_, d, 0, xs],
                            start=False, stop=False), reads=[Pp.bM, bXb[d][0]], writes=[bPY[yi]])
                        S.op("pe", lambda e, yi=yi, q_=q_, ps_=ps_, d=d, xs=xs: e.matmul(
                            PY[yi][:], lhsT=Pp.MCn[ps_, q_, :, :].rearrange("p j c -> p (j c)"), rhs=Xb[ps_, d, 1, xs],
                            start=False, stop=(d == 1)), reads=[Pp.bM, bXb[d][1]], writes=[bPY[yi]])
                    S.op("act", lambda e, yi=yi: e.copy(out=Yst[:, yi, :], in_=PY[yi][:]), reads=[bPY[yi]], writes=[bYst[yi]])
                    S.dma("sp", Yg[g, :, ks], Yst[:, yi, :], reads=[bYst[yi]])
        if bg is not None:
            for _ in bg:
                pass
        S.emit()


GELU_C = 1.5957691216057308


def phase_s5_post(nc, S, C, Yg, selT_d, w_glu, sT):
    with contextlib.ExitStack() as st:
        sb, ps = mk(st, nc)
        Wgl = sb("Wgl", [128, 4, 512], BF16)
        selT = sb("selT", [128, 64, 128], BF16)
        Ysb = sb("Ysb", [128, 2, 32, 256], BF16)
        y = sb("y", [128, 2, 512], F32)
        w = sb("w", [128, 2, 512], F32)
        zf = sb("zf", [128, 4, 512], F32)
        zb = sb("zb", [128, 4, 512], BF16)
        sg = sb("sg", [128, 2, 512], F32)
        so = sb("so", [128, 2, 512], BF16)
        DG = [ps("DG%d" % i, [128, 8, 64], F32) for i in range(2)]
        GP = [ps("GP%d" % i, [128, 512], F32) for i in range(2)]
        bW, bsel = Buf(), Buf()
        bY = [Buf() for _ in range(2)]
        by = [Buf() for _ in range(2)]
        bw = [Buf() for _ in range(2)]
        bz = [Buf() for _ in range(4)]
        bsg = [Buf() for _ in range(2)]
        bso = [Buf() for _ in range(2)]
        bDG = [Buf() for _ in range(2)]
        bGP = [Buf() for _ in range(2)]
        load_weight_bf16(S, Wgl, bW, w_glu, 4)
        S.dma("sp", selT[:], selT_d, writes=[bsel])
        n = 0
        for t4 in range(T // 2048):
            yb = t4 % 2
            S.dma("sp", Ysb[:, yb, :, :], Yg[:, :, t4 * 256:(t4 + 1) * 256].rearrange("g p k -> p g k"), writes=[bY[yb]])
            for tt in range(4):
                t = t4 * 4 + tt
                for cc in range(4):
                    b = n % 2
                    n += 1
                    for i in range(8):
                        for g1 in range(8):
                            S.op("pe", lambda e, b=b, i=i, g1=g1, cc=cc, yb=yb, tt=tt: e.matmul(
                                DG[b][:, i, :], lhsT=selT[:, g1 * 8 + i, :], rhs=Ysb[:, yb, cc * 8 + g1, tt * 64:(tt + 1) * 64],
                                start=(g1 == 0), stop=(g1 == 7)), reads=[bsel, bY[yb]], writes=[bDG[b]])
                    src = bc_ap(DG[b][:], [[1, 64], [64, 8]])
                    dst = bc_ap(y[:, b, :], [[8, 64], [1, 8]])
                    S.op("act", lambda e, src=src, dst=dst: e.copy(out=dst, in_=src), reads=[bDG[b]], writes=[by[b]])
                    S.op("dve", lambda e, b=b: e.tensor_tensor(out=w[:, b, :], in0=y[:, b, :], in1=y[:, b, :], op=ALU.mult),
                         reads=[by[b]], writes=[bw[b]])
                    S.op("dve", lambda e, b=b: e.tensor_scalar(out=w[:, b, :], in0=w[:, b, :], scalar1=0.044715, scalar2=1.0,
                                                               op0=ALU.mult, op1=ALU.add), reads=[bw[b]], writes=[bw[b]])
                    S.op("dve", lambda e, b=b: e.tensor_tensor(out=w[:, b, :], in0=w[:, b, :], in1=y[:, b, :], op=ALU.mult),
                         reads=[bw[b], by[b]], writes=[bw[b]])
                    S.op("act", lambda e, b=b: e.activation(out=w[:, b, :], in_=w[:, b, :], func=AF.Sigmoid, scale=GELU_C),
                         reads=[bw[b]], writes=[bw[b]])
                    S.op("pool", lambda e, b=b, cc=cc: e.tensor_tensor(out=zf[:, cc, :], in0=y[:, b, :], in1=w[:, b, :], op=ALU.mult),
                         reads=[by[b], bw[b]], writes=[bz[cc]])
                    S.op("pool", lambda e, cc=cc: e.tensor_copy(out=zb[:, cc, :], in_=zf[:, cc, :]), reads=[bz[cc]], writes=[bz[cc]])
                for co in range(4):
                    b = co % 2
                    for cc in range(4):
                        S.op("pe", lambda e, b=b, cc=cc, co=co: e.matmul(GP[b][:], lhsT=Wgl[:, cc, co * 128:(co + 1) * 128],
                                                                         rhs=zb[:, cc, :], start=(cc == 0), stop=(cc == 3)),
                             reads=[bW] + bz, writes=[bGP[b]])
                    S.op("act", lambda e, b=b: e.activation(out=sg[:, b, :], in_=GP[b][:], func=AF.Sigmoid),
                         reads=[bGP[b]], writes=[bsg[b]])
                    S.op("dve", lambda e, b=b, co=co: e.tensor_tensor(out=so[:, b, :], in0=zf[:, co, :], in1=sg[:, b, :], op=ALU.mult),
                         reads=[bz[co], bsg[b]], writes=[bso[b]])
                    S.dma("sp", sT[co * 128:(co + 1) * 128, t * 512:(t + 1) * 512], so[:, b, :], reads=[bso[b]])
        S.emit()


INPUT_SPECS = [
    ("x", [T, D], F32), ("norm_g", [2, 6, D], F32),
    ("ffn_w_gate", [2, 2, D, DFF], F32), ("ffn_w_up", [2, 2, D, DFF], F32), ("ffn_w_down", [2, 2, DFF, D], F32),
    ("ev_w_in", [1, D, 1952], F32), ("mla_q_norm", [1, 256], F32), ("mla_kv_norm", [1, 128], F32),
    ("mla_w_uq", [1, 256, 768], F32), ("mla_w_ukv", [1, 128, 1024], F32), ("nat_rpb", [1, 8, 15, 31], F32),
    ("ev_w_out", [1, D, D], F32),
    ("od_w_in", [1, D, 1536], F32), ("conv_dw_w", [1, 31, 512], F32), ("conv_dw_b", [1, 512], F32),
    ("conv_ln_g", [1, 512], F32), ("conv_ln_b", [1, 512], F32),
    ("s5_lambda_re", [1, 2, 32, 64], F32), ("s5_lambda_im", [1, 2, 32, 64], F32), ("s5_log_step", [1, 2, 32], F32),
    ("s5_b_re", [1, 2, 32, 64, 16], F32), ("s5_b_im", [1, 2, 32, 64, 16], F32),
    ("s5_c_re", [1, 2, 32, 16, 64], F32), ("s5_c_im", [1, 2, 32, 16, 64], F32),
    ("s5_d", [1, 512], F32), ("s5_w_glu", [1, 512, 512], F32), ("od_w_out", [1, D, D], F32),
    ("ident", [128, 128], BF16), ("ident32", [128, 128], F32), ("rope_cos", [T, 16], F32), ("rope_sin", [T, 16], F32),
    ("kaug", [2, T], BF16), ("qaug", [2, T], BF16), ("segflag", [128, 1], F32), ("segkeep", [128, 1], F32),
    ("colmask", [64, 64], F32), ("jpad", [31, 127], F32),
    ("sel", [128, 64, 128], BF16), ("selT", [128, 64, 128], BF16), ("maskf", [128, 128], F32), ("maskb", [128, 128], F32),
]


def build(phases=("all",), dbg=()):
    nc = bass.Bass("TRN2", target_bir_lowering=False)
    I = {}
    for name, shape, dt in INPUT_SPECS:
        I[name] = nc.dram_tensor(name, shape, dt, kind="ExternalInput").ap()
    y = nc.dram_tensor("y", [T, D], F32, kind="ExternalOutput").ap()

    def scr(name, shape, dt):
        return nc.dram_tensor(name, shape, dt, kind=("ExternalOutput" if name in dbg else "Internal")).ap()
    hA = scr("hA", [T, D], F32)
    hB = scr("hB", [T, D], F32)
    qT = scr("qT", [8, 98, T], BF16)
    kT = scr("kT", [8, 98, T], BF16)
    vA = scr("vA", [T, 8, 65], BF16)
    nqT = scr("nqT", [8, 64, T], BF16)
    nkT = scr("nkT", [8, 64, T], BF16)
    nvA = scr("nvA", [T, 8, 65], BF16)
    mixO = scr("mixO", [T, 1024], BF16)
    mlaT = scr("mlaT", [512, T], BF16)
    rcd = scr("rcd", [2, 512], F32)
    rep = scr("rep", [120 * 64, 127], F32)
    uT = scr("uT", [512, T], BF16)
    Ug = scr("Ug", [32, 128, 2048], BF16)
    Yg = scr("Yg", [32, 128, 2048], BF16)
    convO = scr("convO", [512, T], BF16)
    sT = scr("sT", [512, T], BF16)
    allp = "all" in phases

    def on(p):
        return allp or p in phases

    with contextlib.ExitStack() as stack:
        S = Sched(nc, stack)
        C = Ctx()
        C.ident = stack.enter_context(nc.sbuf_tensor("ident_sb", [128, 128], BF16))
        C.bident = Buf()
        S.dma("sp", C.ident[:], I["ident"][:], writes=[C.bident])
        g = I["norm_g"]
        x = I["x"]
        cur = x
        if on("ffn1_0"):
            phase_ffn(nc, S, C, cur, hA, I["ffn_w_gate"][0, 0], I["ffn_w_up"][0, 0], I["ffn_w_down"][0, 0], g[0, 0], g[0, 1])
            cur = hA
        if on("ev_in"):
            phase_ev_in(nc, S, C, cur, g[0, 2], I["ev_w_in"][0], I["mla_q_norm"][0], I["mla_kv_norm"][0],
                        I["mla_w_uq"][0], I["mla_w_ukv"][0], I["rope_cos"], I["rope_sin"], I["kaug"], I["qaug"], qT, kT, vA, nqT, nkT, nvA)
        if on("mla"):
            phase_mla(nc, S, C, qT, kT, vA, mlaT, rcd)
        if on("nat"):
            phase_nat(nc, S, C, nqT, nkT, nvA, I["nat_rpb"][0], I["jpad"], I["colmask"], I["segflag"], rep, mixO)
        if on("ev_out"):
            dst = hB if allp else y
            phase_proj(nc, S, C, cur, dst, I["ev_w_out"][0], g[0, 3], mixO, 4, [(mlaT, 4)], wmap=[4, 5, 6, 7, 0, 1, 2, 3], tok_cols=(512, 1024))
            cur = dst
        if on("ffn2_0"):
            phase_ffn(nc, S, C, cur, hA, I["ffn_w_gate"][0, 1], I["ffn_w_up"][0, 1], I["ffn_w_down"][0, 1], g[0, 4], g[0, 5])
            cur = hA
        if on("ffn1_1"):
            phase_ffn(nc, S, C, cur, hB, I["ffn_w_gate"][1, 0], I["ffn_w_up"][1, 0], I["ffn_w_down"][1, 0], g[1, 0], g[1, 1])
            cur = hB
        if on("od_in"):
            phase_od_in(nc, S, C, cur, g[1, 2], I["od_w_in"][0], I["sel"], uT, Ug)
        conv_args = (uT, I["conv_dw_w"][0], I["conv_dw_b"][0], I["conv_ln_g"][0], I["conv_ln_b"][0],
                     I["segkeep"], I["ident32"], convO)
        fuse_conv = False
        if on("conv") and not fuse_conv:
            phase_conv(nc, S, C, *conv_args)
        if on("s5") or on("s5prep"):
            with contextlib.ExitStack() as pst:
                Pp = Ctx()
                Pp.Kin = pst.enter_context(nc.sbuf_tensor("Kin", [128, 32, 128], BF16))
                Pp.MBT = pst.enter_context(nc.sbuf_tensor("MBT", [128, 64, 128], BF16))
                Pp.MCr = pst.enter_context(nc.sbuf_tensor("MCr", [128, 32, 8, 16], BF16))
                Pp.MCn = pst.enter_context(nc.sbuf_tensor("MCn", [128, 32, 8, 16], BF16))
                Pp.LamR = pst.enter_context(nc.sbuf_tensor("LamR", [128, 10, 32], F32))
                Pp.LamI = pst.enter_context(nc.sbuf_tensor("LamI", [128, 10, 32], F32))
                Pp.LamN = pst.enter_context(nc.sbuf_tensor("LamN", [128, 10, 32], F32))
                Pp.keep = pst.enter_context(nc.sbuf_tensor("keep_s5", [128, 1], F32))
                Pp.bM, Pp.bL, Pp.bk = Buf(), Buf(), Buf()
                phase_s5_prep(nc, S, C, I, Pp)
                if on("s5"):
                    bgf = (lambda sb_, ps_: conv_gen(nc, S, C, sb_, ps_, *conv_args, acc_bufs=1)) if fuse_conv else None
                    phase_s5_main(nc, S, C, Pp, Ug, Yg, bgf)
        if on("s5_post"):
            phase_s5_post(nc, S, C, Yg, I["selT"], I["s5_w_glu"][0], sT)
        if on("od_out"):
            dst = hA if allp else y
            phase_proj(nc, S, C, cur, dst, I["od_w_out"][0], g[1, 3], None, 0, [(convO, 4), (sT, 4)])
            cur = dst
        if on("ffn2_1"):
            phase_ffn(nc, S, C, cur, y, I["ffn_w_gate"][1, 1], I["ffn_w_up"][1, 1], I["ffn_w_down"][1, 1], g[1, 4], g[1, 5])
    return nc


def host_consts(kind):
    c = {}
    c["ident"] = np.eye(128, dtype=ml_dtypes.bfloat16)
    pos = np.arange(T, dtype=np.float32)
    if kind == 1:
        pos = np.concatenate([np.arange(T // 2, dtype=np.float32)] * 2)
    inv = (np.float32(10000.0) ** (-np.arange(16, dtype=np.float32) / np.float32(16))).astype(np.float32)
    ang = (pos[:, None] * inv[None, :]).astype(np.float32)
    c["rope_cos"] = np.cos(ang).astype(np.float32)
    c["rope_sin"] = np.sin(ang).astype(np.float32)
    seg = np.zeros(T, np.float32)
    if kind == 1:
        seg[T // 2:] = 1.0
    big = -30000.0 * float(kind)
    c["kaug"] = np.stack([big * seg, np.full(T, big, np.float32)]).astype(ml_dtypes.bfloat16)
    c["qaug"] = np.stack([1.0 - 2.0 * seg, seg]).astype(ml_dtypes.bfloat16)
    c["segflag"] = np.full((128, 1), float(kind), np.float32)
    cq = np.arange(64)
    cs = np.clip(cq - 8, 0, 48)
    ck = np.arange(64)
    ok = (ck[:, None] >= cs[None, :]) & (ck[:, None] < cs[None, :] + 16)
    c["colmask"] = np.where(ok, 0.0, -30000.0).astype(np.float32)
    jp = np.zeros((31, 127), np.float32)
    for m in range(31):
        jp[m, 78 - m] = 1.0
    c["jpad"] = jp
    c["ident32"] = np.eye(128, dtype=np.float32)
    c["segkeep"] = np.full((128, 1), 1.0 - float(kind), np.float32)
    sel = np.zeros((128, 64, 128), np.float32)
    selT = np.zeros((128, 64, 128), np.float32)
    for g1 in range(8):
        for j in range(8):
            for ch in range(16):
                sel[g1 * 16 + ch, g1 * 8 + j, j * 16 + ch] = 1.0
                selT[j * 16 + ch, g1 * 8 + j, g1 * 16 + ch] = 1.0
    c["sel"] = sel.astype(ml_dtypes.bfloat16)
    c["selT"] = selT.astype(ml_dtypes.bfloat16)
    jj = np.arange(128) // 16
    c["maskf"] = (jj[None, :] >= jj[:, None]).astype(np.float32)
    c["maskb"] = (jj[:, None] >= jj[None, :]).astype(np.float32)
    return c


def kernel(**inputs):
    inp = {k: np.asarray(v) for k, v in inputs.items()}
    xs = [inp["x_sample"][0], inp["x_sample"][1], np.concatenate([inp["x_prompt"][0], inp["x_prompt"][1]], axis=0)]
    kinds = [0, 0, 1]
    nc = build()
    in_maps = []
    for x, k in zip(xs, kinds):
        m = {"x": np.ascontiguousarray(x, dtype=np.float32)}
        for name, shape, dt in INPUT_SPECS:
            if name in inp:
                m[name] = np.ascontiguousarray(inp[name], dtype=np.float32)
        m.update(host_consts(k))
        in_maps.append(m)
    res = run_bass_kernel_spmd(nc, in_maps, core_ids=list(range(NCORES)))
    ys = [np.asarray(r["y"], dtype=np.float32) for r in res.results]
    y_sample = np.stack([ys[0], ys[1]], axis=0)
    y_prompt = ys[2].reshape(2, T // 2, D)
    return (y_prompt, y_sample)
```
